# Optimizing a Trainium2 kernel written in Bass

```python
import math
import jax, jax.numpy as jnp
from jax import lax
import numpy as np

D_MODEL = 4096
BATCH = 2
SEQ = 8192
DEPTH = 1

GRID_W = 64
CTX_LEN = 256

NA_HEAD_DIM = 128
NA_WIDTH = D_MODEL // 2
NA_HEADS = NA_WIDTH // NA_HEAD_DIM
NA_KH = 8
NA_KW = 16

HG_HEADS = 16
HG_KDIM = 128
HG_VDIM = (D_MODEL // 2) // HG_HEADS
HG_WIDTH = HG_HEADS * HG_KDIM
HG_CHUNK = 64

ALPHA = (2.0 * DEPTH) ** 0.25
BETA = (8.0 * DEPTH) ** -0.25
LN_EPS = 1e-6
RMS_EPS = 1e-6

kernel_name = 'hybrid_na_hgrn2_dit_block'


def _in_sizes():
    return [NA_WIDTH] * 4 + [HG_WIDTH] * 5 + [D_MODEL] * 2


def _split_points():
    return [int(p) for p in np.cumsum(_in_sizes())[:-1]]


def layer_norm(x):
    x32 = x.astype(jnp.float32)
    mu = jnp.mean(x32, axis=-1, keepdims=True)
    var = jnp.mean(jnp.square(x32 - mu), axis=-1, keepdims=True)
    return ((x32 - mu) * lax.rsqrt(var + LN_EPS)).astype(x.dtype)


def adaln_params(cond, w_ada, b_ada):
    mod = jax.nn.silu(cond) @ w_ada + b_ada
    return jnp.split(mod, 3, axis=-1)


def split_heads(t, n_heads):
    b, n, w = t.shape
    return t.reshape(b, n, n_heads, w // n_heads).transpose(0, 2, 1, 3)


def merge_heads(t):
    b, h, n, d = t.shape
    return t.transpose(0, 2, 1, 3).reshape(b, n, h * d)


def neighbourhood_attention(q, k, v, k_ctx, v_ctx, rpb):
    b, h, t, dh = q.shape
    rows = t // GRID_W
    kh = min(NA_KH, rows)
    qg = q.reshape(b, h, rows, GRID_W, dh) * (dh ** -0.5)
    kg = k.reshape(b, h, rows, GRID_W, dh)
    vg = v.reshape(b, h, rows, GRID_W, dh)
    r = jnp.arange(rows)
    row_start = jnp.clip(r - NA_KH // 2, 0, rows - kh)
    row_idx = row_start[:, None] + jnp.arange(kh)[None, :]
    k_strip = kg[:, :, row_idx].reshape(b, h, rows, kh * GRID_W, dh)
    v_strip = vg[:, :, row_idx].reshape(b, h, rows, kh * GRID_W, dh)
    col = jnp.arange(GRID_W)
    col_start = jnp.clip(col - NA_KW // 2, 0, GRID_W - NA_KW)
    col_in = (col[None, :] >= col_start[:, None]) & (col[None, :] < col_start[:, None] + NA_KW)
    d_row = row_idx - r[:, None]
    d_col = jnp.clip(col[None, :] - col[:, None], -(NA_KW - 1), NA_KW - 1)
    bias = rpb.astype(jnp.float32)[:, d_row[:, None, :, None] + NA_KH - 1,
                                   d_col[None, :, None, :] + NA_KW - 1]
    bias = jnp.where(col_in[:, None, :], bias, -jnp.inf).reshape(h, rows, GRID_W, kh * GRID_W)
    s_loc = jnp.einsum('bhrqd,bhrkd->bhrqk', qg, k_strip).astype(jnp.float32) + bias[None]
    s_ctx = jnp.einsum('bhrqd,bhld->bhrql', qg, k_ctx).astype(jnp.float32)
    p = jax.nn.softmax(jnp.concatenate([s_loc, s_ctx], axis=-1), axis=-1).astype(v.dtype)
    n_loc = kh * GRID_W
    o = (jnp.einsum('bhrqk,bhrkd->bhrqd', p[..., :n_loc], v_strip)
         + jnp.einsum('bhrql,bhld->bhrqd', p[..., n_loc:], v_ctx))
    return o.reshape(b, h, t, dh)


def context_attention(q, k, v):
    s = jnp.einsum('bhqd,bhkd->bhqk', q, k).astype(jnp.float32) * (q.shape[-1] ** -0.5)
    p = jax.nn.softmax(s, axis=-1).astype(v.dtype)
    return jnp.einsum('bhqk,bhkd->bhqd', p, v)


def hgrn2_lower_bound(lb_logits, layer):
    lb = jnp.cumsum(jax.nn.softmax(lb_logits.astype(jnp.float32), axis=0), axis=0)[layer]
    return lb.reshape(HG_HEADS, 1, HG_KDIM)


def hgrn2_forget(f_pre, lb):
    f_pre = f_pre.astype(jnp.float32)
    f = lb + (1.0 - lb) * jax.nn.sigmoid(f_pre)
    return (1.0 - lb) * jax.nn.sigmoid(-f_pre), jnp.log(f)


def gla_chunked(q, k, v, log_f, s0):
    b, h, t, dk = q.shape
    dv = v.shape[-1]
    n, c = t // HG_CHUNK, HG_CHUNK
    qc = q.reshape(b, h, n, c, dk)
    kc = k.reshape(b, h, n, c, dk)
    vc = v.reshape(b, h, n, c, dv)
    cum = jnp.cumsum(log_f.reshape(b, h, n, c, dk), axis=3)
    ref = cum[:, :, :, c // 2 - 1:c // 2]
    a = jnp.einsum('bhncd,bhnsd->bhncs', qc * jnp.exp(cum - ref), kc * jnp.exp(ref - cum))
    a = jnp.where(jnp.tril(jnp.ones((c, c), dtype=bool)), a, 0.0)
    o_intra = jnp.einsum('bhncs,bhnsv->bhncv', a, vc)
    last = cum[:, :, :, -1]
    u = jnp.einsum('bhncd,bhncv->bhndv', kc * jnp.exp(last[:, :, :, None] - cum), vc)

    def step(s, inp):
        decay, du = inp
        return decay[..., None] * s + du, s

    s_last, s_start = lax.scan(step, s0, (jnp.moveaxis(jnp.exp(last), 2, 0), jnp.moveaxis(u, 2, 0)))
    s_start = jnp.moveaxis(s_start, 0, 2)
    o_inter = jnp.einsum('bhncd,bhndv->bhncv', qc * jnp.exp(cum), s_start)
    return (o_intra + o_inter).reshape(b, h, t, dv), s_last


def hgrn2_readout(o, g, norm_w):
    o = o * lax.rsqrt(jnp.mean(jnp.square(o), axis=-1, keepdims=True) + RMS_EPS) * norm_w
    return merge_heads(o.astype(g.dtype)) * jax.nn.silu(g)


def merge_branches(y_a, y_b, gate_a, gate_b, w_pa, w_pb, w_out):
    m = jax.nn.sigmoid(gate_a) * (y_a @ w_pa) + jax.nn.sigmoid(gate_b) * (y_b @ w_pb)
    return m @ w_out


def trunk_layer(x, ctx, c, c_ctx, w_ada, b_ada, w_in, na_rpb, lb_fwd, lb_bwd, hg_norm_w,
                w_pa, w_pb, w_out, ln_g, ln_b, layer, update_ctx):
    bsz = x.shape[0]
    L = ctx.shape[1]
    shift, scale, gate = adaln_params(c, w_ada, b_ada)
    shift_c, scale_c, gate_c = adaln_params(c_ctx, w_ada, b_ada)
    h_lat = layer_norm(x) * (1.0 + scale[:, None]) + shift[:, None]
    h_ctx = layer_norm(ctx) * (1.0 + scale_c) + shift_c
    proj = jnp.concatenate([h_ctx, h_lat], axis=1) @ w_in
    (na_q, na_k, na_v, na_z, hg_q, hg_ff, hg_fb, hg_i, hg_g,
     gate_a, gate_b) = jnp.split(proj, _split_points(), axis=-1)

    q, k, v = (split_heads(t, NA_HEADS) for t in (na_q, na_k, na_v))
    o_a = neighbourhood_attention(q[:, :, L:], k[:, :, L:], v[:, :, L:], k[:, :, :L], v[:, :, :L], na_rpb)
    y_a = merge_heads(o_a) * jax.nn.silu(na_z[:, L:])

    hq = split_heads(jax.nn.silu(hg_q), HG_HEADS).astype(jnp.float32)
    hv = split_heads(hg_i, HG_HEADS).astype(jnp.float32)
    k_f, logf_f = hgrn2_forget(split_heads(hg_ff, HG_HEADS), hgrn2_lower_bound(lb_fwd, layer))
    k_b, logf_b = hgrn2_forget(split_heads(hg_fb, HG_HEADS), hgrn2_lower_bound(lb_bwd, layer))
    rev = lambda t: jnp.flip(t, axis=2)
    s0 = jnp.zeros((bsz, HG_HEADS, HG_KDIM, HG_VDIM), jnp.float32)
    o_cf, s_cf = gla_chunked(hq[:, :, :L], k_f[:, :, :L], hv[:, :, :L], logf_f[:, :, :L], s0)
    o_cb, s_cb = gla_chunked(rev(hq[:, :, :L]), rev(k_b[:, :, :L]), rev(hv[:, :, :L]),
                             rev(logf_b[:, :, :L]), s0)
    o_lf, _ = gla_chunked(hq[:, :, L:], k_f[:, :, L:], hv[:, :, L:], logf_f[:, :, L:], s_cf)
    o_lb, _ = gla_chunked(rev(hq[:, :, L:]), rev(k_b[:, :, L:]), rev(hv[:, :, L:]),
                          rev(logf_b[:, :, L:]), s_cb)
    y_b = hgrn2_readout(o_lf + rev(o_lb), hg_g[:, L:], hg_norm_w)

    out = merge_branches(y_a, y_b, gate_a[:, L:], gate_b[:, L:], w_pa, w_pb, w_out)
    x_new = layer_norm(ALPHA * x + gate[:, None] * out) * ln_g + ln_b
    if not update_ctx:
        return x_new, ctx

    y_ac = merge_heads(context_attention(q[:, :, :L], k[:, :, :L], v[:, :, :L])) * jax.nn.silu(na_z[:, :L])
    y_bc = hgrn2_readout(o_cf + rev(o_cb), hg_g[:, :L], hg_norm_w)
    out_c = merge_branches(y_ac, y_bc, gate_a[:, :L], gate_b[:, :L], w_pa, w_pb, w_out)
    ctx_new = layer_norm(ALPHA * ctx + gate_c * out_c) * ln_g + ln_b
    return x_new, ctx_new


def setup_inputs(seed: int = 0) -> dict:
    key = jax.random.key(seed)
    ks = jax.random.split(key, 16)
    d = D_MODEL
    nrm = jax.random.normal
    sizes = _in_sizes()
    n_in = sum(sizes)
    col_scale = jnp.concatenate([jnp.full((s,), BETA if j in (2, 7) else 1.0, jnp.float32)
                                 for j, s in enumerate(sizes)])
    return {
        'x': nrm(ks[0], (BATCH, SEQ, d), jnp.float32),
        'c': nrm(ks[1], (BATCH, d), jnp.float32),
        'ctx': nrm(ks[2], (BATCH, CTX_LEN, d), jnp.float32),
        'c_ctx': nrm(ks[3], (d,), jnp.float32),
        'w_ada': nrm(ks[4], (DEPTH, d, 3 * d), jnp.float32) * (0.5 * d ** -0.5),
        'b_ada': nrm(ks[5], (DEPTH, 3 * d), jnp.float32) * 0.01,
        'w_in': nrm(ks[6], (DEPTH, d, n_in), jnp.float32) * (d ** -0.5) * col_scale,
        'na_rpb': nrm(ks[7], (DEPTH, NA_HEADS, 2 * NA_KH - 1, 2 * NA_KW - 1), jnp.float32) * 0.02,
        'hg_lb_fwd': nrm(ks[8], (DEPTH + 1, HG_WIDTH), jnp.float32) * 0.1,
        'hg_lb_bwd': nrm(ks[9], (DEPTH + 1, HG_WIDTH), jnp.float32) * 0.1,
        'hg_norm_w': 1.0 + 0.01 * nrm(ks[10], (DEPTH, HG_VDIM), jnp.float32),
        'w_pa': nrm(ks[11], (DEPTH, NA_WIDTH, d), jnp.float32) * (NA_WIDTH ** -0.5) * BETA,
        'w_pb': nrm(ks[12], (DEPTH, HG_WIDTH, d), jnp.float32) * (HG_WIDTH ** -0.5) * BETA,
        'w_out': nrm(ks[13], (DEPTH, d, d), jnp.float32) * (d ** -0.5) * BETA,
        'ln_g': 1.0 + 0.01 * nrm(ks[14], (DEPTH, d), jnp.float32),
        'ln_b': 0.01 * nrm(ks[15], (DEPTH, d), jnp.float32),
    }


def reference(x, c, ctx, c_ctx, w_ada, b_ada, w_in, na_rpb, hg_lb_fwd, hg_lb_bwd, hg_norm_w,
              w_pa, w_pb, w_out, ln_g, ln_b):
    for l in range(DEPTH):
        x, ctx = trunk_layer(x, ctx, c, c_ctx, w_ada[l], b_ada[l], w_in[l], na_rpb[l],
                             hg_lb_fwd, hg_lb_bwd, hg_norm_w[l], w_pa[l], w_pb[l], w_out[l],
                             ln_g[l], ln_b[l], l, l < DEPTH - 1)
    return x
```

```python
import contextlib
import numpy as np
import concourse.bass as bass
import concourse.mybir as mybir
from concourse.bass_utils import run_bass_kernel_spmd

F32 = mybir.dt.float32
BF16 = mybir.dt.bfloat16
AF = mybir.ActivationFunctionType
ALU = mybir.AluOpType
AX = mybir.AxisListType

PE, ACT, DVE, POOL, SP = "pe", "act", "dve", "pool", "sp"
ENGS = (PE, ACT, DVE, POOL, SP)
NDMA_SEM = {SP: 8, POOL: 6, ACT: 2}

D = 4096
NH = 16
NT = 22
NBLK = 11
TLOC = 2048
ALPHA = 2.0 ** 0.25
NEG = -30000.0


class SemState:
    def __init__(self, nc, stack):
        self.esem = {e: stack.enter_context(nc.semaphore("s_" + e)) for e in ENGS}
        self.dsem = {}
        for e, K in NDMA_SEM.items():
            for i in range(K):
                self.dsem[(e, i)] = stack.enter_context(nc.semaphore("d_%s%d" % (e, i)))
        self.cnt = {e: 0 for e in ENGS}
        self.ndma = {e: 0 for e in ENGS}


class Prog:
    def __init__(self, ss, same_engine_sync=True):
        self.ss = ss
        self.ops = []
        self.last_w = {}
        self.readers = {}
        self.same_engine_sync = same_engine_sync
        self.ndma = ss.ndma
        self.dma_hist = {e: [] for e in ENGS}
        self.last_of = {}

    def op(self, eng, fn, reads=(), writes=(), dma=False, extra_deps=()):
        idx = len(self.ops)
        deps = set(extra_deps)
        for k in list(reads) + list(writes):
            w = self.last_w.get(k)
            if w is not None:
                deps.add(w)
        for k in writes:
            for r in self.readers.get(k, ()):
                deps.add(r)
        deps.discard(idx)
        rec = dict(idx=idx, eng=eng, fn=fn, deps=deps, dma=dma, signal=dma, dslot=None)
        if dma:
            n = self.ndma[eng]
            self.ndma[eng] += 1
            K = NDMA_SEM[eng]
            rec["dslot"] = (eng, n % K, 16 * (n // K + 1))
            rec["dprev"] = 16 * (n // K)
            self.dma_hist[eng].append(idx)
        elif fn is not None:
            self.last_of[eng] = idx
        self.ops.append(rec)
        for k in writes:
            self.last_w[k] = idx
            self.readers[k] = []
        for k in reads:
            if k not in writes:
                self.readers.setdefault(k, []).append(idx)
        return idx

    def barrier(self):
        deps = set(self.last_of.values())
        for e, K in NDMA_SEM.items():
            deps.update(self.dma_hist[e][-K:])
        for e in ENGS:
            self.op(e, None, extra_deps=deps)

    def emit(self, nc):
        ops = self.ops
        esem, dsem, cnt = self.ss.esem, self.ss.dsem, self.ss.cnt
        for o in ops:
            for d in o["deps"]:
                p = ops[d]
                if p["dma"]:
                    continue
                if p["eng"] == o["eng"] and not o["dma"]:
                    if p["eng"] == PE or not self.same_engine_sync:
                        continue
                p["signal"] = True
        for o in ops:
            if o["dma"]:
                e, s, v = o["dslot"]
                o["sig"] = (dsem[(e, s)], v, 16, ("d", e, s))
            elif o["signal"]:
                cnt[o["eng"]] += 1
                o["sig"] = (esem[o["eng"]], cnt[o["eng"]], 1, ("e", o["eng"]))
        waited = {e: {} for e in ENGS}
        for o in ops:
            e = o["eng"]
            waits = {}
            for d in o["deps"]:
                p = ops[d]
                if not p["signal"] or p["fn"] is None:
                    continue
                if (not p["dma"]) and p["eng"] == e and not o["dma"]:
                    if e == PE or not self.same_engine_sync:
                        continue
                sem, val, _, key = p["sig"]
                if waits.get(key, (None, 0))[1] < val:
                    waits[key] = (sem, val)
            if o["dma"] and o["dprev"] > 0:
                sem, val, _, key = o["sig"]
                if waits.get(key, (None, 0))[1] < o["dprev"]:
                    waits[key] = (sem, o["dprev"])
            wl = []
            for key, (sem, val) in waits.items():
                if waited[e].get(key, 0) >= val:
                    continue
                waited[e][key] = val
                wl.append((sem, val))
            o["waits"] = wl
        per = {e: [o for o in ops if o["eng"] == e] for e in ENGS}
        self.stats = {e: len(per[e]) for e in ENGS}
        def run(eng_ops):
            def body(engine):
                for o in eng_ops:
                    for sem, val in o["waits"]:
                        engine.wait_ge(sem, val)
                    if o["fn"] is None:
                        continue
                    inst = o["fn"](engine)
                    if o["signal"]:
                        sem, val, inc, _ = o["sig"]
                        inst.then_inc(sem, inc)
            return body

        with nc.Block() as block:
            block.sync(run(per[SP]))
            block.tensor(run(per[PE]))
            block.scalar(run(per[ACT]))
            block.vector(run(per[DVE]))
            block.gpsimd(run(per[POOL]))


class Rot:
    def __init__(self, items):
        self.items = items
        self.i = 0

    def next(self):
        it = self.items[self.i % len(self.items)]
        self.i += 1
        return it


def build_program(stage=99, phases=(2, 3)):
    nc = bass.Bass("TRN2", target_bir_lowering=False)
    dtn = nc.dram_tensor
    xe = dtn("xe", [NT * 128, D], F32, kind="ExternalInput").ap()
    condT = dtn("condT", [128, 64], F32, kind="ExternalInput").ap()
    w_ada = dtn("w_ada", [D, 3 * D], F32, kind="ExternalInput").ap()
    b_adaT = dtn("b_adaT", [128, 96], F32, kind="ExternalInput").ap()
    w_na = dtn("w_na", [NH, 128, 32 * 512], F32, kind="ExternalInput").ap()
    w_hg = dtn("w_hg", [NH, 128, 32 * 640], F32, kind="ExternalInput").ap()
    w_4a = dtn("w_4a", [32, 128, 96 * 128], F32, kind="ExternalInput").ap()
    w_4b = dtn("w_4b", [16, 128, 32 * 256], F32, kind="ExternalInput").ap()
    biasT = dtn("biasT", [NH, 128, 5 * 576], F32, kind="ExternalInput").ap()
    lbl = dtn("lbl", [128, NH * 4], F32, kind="ExternalInput").ap()
    nw_in = dtn("nw", [128, 1], F32, kind="ExternalInput").ap()
    lng = dtn("lng", [1, D], F32, kind="ExternalInput").ap()
    lnb = dtn("lnb", [1, D], F32, kind="ExternalInput").ap()
    cmask = dtn("cmask", [128, 4], F32, kind="ExternalInput").ap()
    out = dtn("out", [TLOC, D], F32, kind="ExternalOutput").ap()
    hT_d = dtn("hT_d", [NT, 128, D], BF16, kind="Internal").ap()
    ya_d = dtn("ya_d", [NH, 128, TLOC], BF16, kind="Internal").ap()
    yb_d = dtn("yb_d", [NH, 128, TLOC], BF16, kind="Internal").ap()
    gate_d = dtn("gate_d", [32, 128], F32, kind="Internal").ap()
    dbg = None
    if stage < 99:
        dbg = dtn("dbg", [128, 8192], F32, kind="ExternalOutput").ap()

    uid = [0]
    nops = [0]

    def nm(s):
        uid[0] += 1
        return "%s_%d" % (s, uid[0])

    with contextlib.ExitStack() as st:
        def sbuf(stack, name, shape, dt):
            return stack.enter_context(nc.sbuf_tensor(nm(name), shape, dt))

        ps0 = st.enter_context(nc.psum_tensor("ps0", [128, 512], F32))
        ps1 = st.enter_context(nc.psum_tensor("ps1", [128, 512], F32))
        ps23 = st.enter_context(nc.psum_tensor("ps23", [128, 1024], F32))
        ps4 = st.enter_context(nc.psum_tensor("ps4", [128, 512], F32))
        ps5 = st.enter_context(nc.psum_tensor("ps5", [128, 512], F32))
        ps6 = st.enter_context(nc.psum_tensor("ps6", [128, 1024], BF16))
        ps7 = st.enter_context(nc.psum_tensor("ps7", [128, 512], F32))
        ps7b = ps7[:].bitcast(BF16)
        ss = SemState(nc, st)

        def finish(P):
            P.barrier()
            P.emit(nc)
            nops[0] += len(P.ops)

        P = Prog(ss)

        ident = sbuf(st, "ident", [128, 128], F32)
        identb = sbuf(st, "identb", [128, 128], BF16)
        onesb = sbuf(st, "onesb", [128, 128], BF16)
        maskF = sbuf(st, "maskF", [64, 64], F32)
        maskB = sbuf(st, "maskB", [64, 64], F32)
        rmask = sbuf(st, "rmask", [128, 256], F32)
        modT = sbuf(st, "modT", [128, 192], F32)
        s1T = sbuf(st, "s1T", [128, 64], F32)
        shT = sbuf(st, "shT", [128, 64], F32)
        lbs = sbuf(st, "lbs", [128, NH * 2], F32)
        oml = sbuf(st, "oml", [128, NH * 2], F32)
        noml = sbuf(st, "noml", [128, NH * 2], F32)
        lbl_sb = sbuf(st, "lbl_sb", [128, NH * 4], F32)
        nw = sbuf(st, "nw_sb", [128, 1], F32)
        cm = sbuf(st, "cm", [128, 4], F32)

        P.op(POOL, lambda e: e.memset(ident[:], 0.0), writes=["ident"])
        P.op(POOL, lambda e: e.affine_select(out=ident[:], in_=ident[:], pattern=[[-1, 128]], compare_op=ALU.not_equal,
                                             fill=1.0, base=0, channel_multiplier=1), writes=["ident"])
        P.op(POOL, lambda e: e.tensor_copy(out=identb[:], in_=ident[:]), reads=["ident"], writes=["identb"])
        P.op(POOL, lambda e: e.memset(onesb[:], 1.0), writes=["onesb"])
        P.op(POOL, lambda e: e.memset(maskF[:], 1.0), writes=["maskF"])
        P.op(POOL, lambda e: e.affine_select(out=maskF[:], in_=maskF[:], pattern=[[1, 64]], compare_op=ALU.is_ge,
                                             fill=0.0, base=0, channel_multiplier=-1), writes=["maskF"])
        P.op(POOL, lambda e: e.memset(maskB[:], 1.0), writes=["maskB"])
        P.op(POOL, lambda e: e.affine_select(out=maskB[:], in_=maskB[:], pattern=[[-1, 64]], compare_op=ALU.is_ge,
                                             fill=0.0, base=0, channel_multiplier=1), writes=["maskB"])
        P.op(POOL, lambda e: e.memset(rmask[:], 1.0), writes=["rmask"])
        P.op(POOL, lambda e: e.memset(rmask[:].rearrange("p (n c) -> p n c", c=64)[:, :, 0:1], 0.0), writes=["rmask"])
        P.op(SP, lambda e: e.dma_start(out=lbl_sb[:], in_=lbl), writes=["lbl"], dma=True)
        P.op(SP, lambda e: e.dma_start(out=nw[:], in_=nw_in), writes=["nw"], dma=True)
        P.op(SP, lambda e: e.dma_start(out=cm[:], in_=cmask), writes=["cm"], dma=True)
        lv = lbl_sb[:].rearrange("p (a s) -> p a s", s=2)
        P.op(DVE, lambda e: e.tensor_tensor(out=oml[:], in0=lv[:, :, 0], in1=lv[:, :, 1], op=ALU.subtract), reads=["lbl"], writes=["oml"])
        P.op(ACT, lambda e: e.activation(out=lbs[:], in_=oml[:], func=AF.Sigmoid), reads=["oml"], writes=["lbs"])
        P.op(DVE, lambda e: e.tensor_scalar(out=oml[:], in0=lbs[:], scalar1=-1.0, scalar2=1.0, op0=ALU.mult, op1=ALU.add),
             reads=["lbs"], writes=["oml"])
        P.op(DVE, lambda e: e.tensor_scalar(out=noml[:], in0=lbs[:], scalar1=1.0, scalar2=-1.0, op0=ALU.mult, op1=ALU.add),
             reads=["lbs"], writes=["noml"])

        finish(P)

        with contextlib.ExitStack() as ph:
            P = Prog(ss)
            cT = sbuf(ph, "cT", [128, 64], F32)
            sc = sbuf(ph, "sc", [128, 64], F32)
            bT = sbuf(ph, "bT", [128, 96], F32)
            wa = [sbuf(ph, "wa%d" % i, [128, 32, 512], F32) for i in range(2)]
            mod2 = sbuf(ph, "mod2", [2, 3 * D], F32)
            P.op(SP, lambda e: e.dma_start(out=cT[:], in_=condT), writes=["cT"], dma=True)
            P.op(SP, lambda e: e.dma_start(out=bT[:], in_=b_adaT), writes=["bT"], dma=True)
            P.op(ACT, lambda e: e.activation(out=sc[:], in_=cT[:], func=AF.Silu), reads=["cT"], writes=["sc"])
            scv = sc[:].rearrange("p (v j) -> p j v", v=2)
            psm = Rot([(ps1, "ps1"), (ps4, "ps4"), (ps5, "ps5")])
            for n in range(24):
                wb, key = wa[n % 2], "wa%d" % (n % 2)
                P.op(SP, lambda e, wb=wb, n=n: e.dma_start(out=wb[:], in_=w_ada[:, n * 512:(n + 1) * 512].rearrange("(c p) n -> p c n", p=128)),
                     writes=[key], dma=True)
                pt, pk = psm.next()
                for c in range(32):
                    P.op(PE, lambda e, pt=pt, wb=wb, c=c: e.matmul(pt[0:2, 0:512], lhsT=scv[:, c, :], rhs=wb[:, c, :], start=(c == 0), stop=(c == 31)),
                         reads=[key, "sc"], writes=[pk])
                P.op(DVE, lambda e, pt=pt, n=n: e.tensor_copy(out=mod2[:, n * 512:(n + 1) * 512], in_=pt[0:2, 0:512]), reads=[pk], writes=["mod2"])
            for jb in range(96):
                P.op(PE, lambda e, jb=jb: e.transpose(out=ps0[:, jb * 2:jb * 2 + 2], in_=mod2[0:2, jb * 128:(jb + 1) * 128], identity=ident[0:2, 0:2]),
                     reads=["mod2", "ident"], writes=["ps0"])
            P.op(DVE, lambda e: e.tensor_tensor(out=modT[:].rearrange("p (j v) -> p j v", v=2), in0=ps0[:, 0:192].rearrange("p (j v) -> p j v", v=2),
                                                in1=bT[:].unsqueeze(2).to_broadcast([128, 96, 2]), op=ALU.add),
                 reads=["ps0", "bT"], writes=["modT"])
            P.op(DVE, lambda e: e.tensor_copy(out=shT[:], in_=modT[:, 0:64]), reads=["modT"], writes=["shT"])
            P.op(DVE, lambda e: e.tensor_scalar(out=s1T[:], in0=modT[:, 64:128], scalar1=1.0, scalar2=None, op0=ALU.add),
                 reads=["modT"], writes=["s1T"])
            gsb = sbuf(ph, "gsb", [128, 32], F32)
            P.op(DVE, lambda e: e.tensor_copy(out=gsb[:], in_=modT[:].rearrange("p (j v) -> p j v", v=2)[:, 64:96, 0]),
                 reads=["modT"], writes=["gsb"])
            P.op(SP, lambda e: e.dma_start(out=gate_d.rearrange("j p -> p j"), in_=gsb[:], allow_slow_non_contiguous=True),
                 reads=["gsb"], writes=["gate_d"], dma=True)
            if stage == 0:
                P.op(SP, lambda e: e.dma_start(out=dbg[:, 0:192], in_=modT[:]), reads=["modT"], writes=["dbg"], dma=True)
            finish(P)
        s1v = s1T[:].rearrange("p (c v) -> p c v", v=2)
        shv = shT[:].rearrange("p (c v) -> p c v", v=2)

        if stage >= 1:
            with contextlib.ExitStack() as ph:
                P = Prog(ss)
                xt = [sbuf(ph, "xt%d" % i, [128, D], F32) for i in range(3)]
                ht = [sbuf(ph, "ht%d" % i, [128, D], BF16) for i in range(3)]
                stt = [sbuf(ph, "stt%d" % i, [128, 8, 6], F32) for i in range(3)]
                mv = [sbuf(ph, "mv%d" % i, [128, 4], F32) for i in range(3)]
                psr = Rot([(ps0, "ps0"), (ps1, "ps1"), (ps4, "ps4"), (ps5, "ps5"), (ps7, "ps7")])
                for t in range(NT):
                    b = t % 3
                    X, H, S_, M = xt[b], ht[b], stt[b], mv[b]
                    kx, kh, ks, km = "xt%d" % b, "ht%d" % b, "stt%d" % b, "mv%d" % b
                    v = 1 if t >= 20 else 0
                    P.op(SP, lambda e, X=X, t=t: e.dma_start(out=X[:], in_=xe[t * 128:(t + 1) * 128, :]), writes=[kx], dma=True)
                    for g in range(8):
                        P.op(DVE, lambda e, X=X, S_=S_, g=g: e.bn_stats(out=S_[:, g, :], in_=X[:, g * 512:(g + 1) * 512]), reads=[kx], writes=[ks])
                    P.op(DVE, lambda e, S_=S_, M=M: e.bn_aggr(out=M[:, 0:2], in_=S_[:]), reads=[ks], writes=[km])
                    P.op(ACT, lambda e, M=M: e.activation(out=M[:, 1:2], in_=M[:, 1:2], func=AF.Sqrt, bias=1e-6, scale=1.0), writes=[km])
                    P.op(DVE, lambda e, M=M: e.reciprocal(out=M[:, 1:2], in_=M[:, 1:2]), writes=[km])
                    P.op(DVE, lambda e, M=M: e.tensor_scalar(out=M[:, 2:3], in0=M[:, 0:1], scalar1=M[:, 1:2], scalar2=-1.0, op0=ALU.mult, op1=ALU.mult),
                         writes=[km])
                    P.op(ACT, lambda e, X=X, M=M: e.activation(out=X[:], in_=X[:], func=AF.Identity, scale=M[:, 1:2], bias=M[:, 2:3]),
                         reads=[km], writes=[kx])
                    for g in range(8):
                        pt, pk = psr.next()
                        for k in range(4):
                            c = g * 4 + k
                            P.op(PE, lambda e, pt=pt, X=X, k=k, c=c: e.transpose(out=pt[:, k * 128:(k + 1) * 128], in_=X[:, c * 128:(c + 1) * 128],
                                                                                 identity=ident[:]), reads=[kx, "ident"], writes=[pk])
                        for k in range(4):
                            c = g * 4 + k
                            if k == 0:
                                P.op(ACT, lambda e, pt=pt, H=H, k=k, c=c, v=v: e.activation(
                                    out=H[:, c * 128:(c + 1) * 128], in_=pt[:, k * 128:(k + 1) * 128], func=AF.Identity,
                                    scale=s1v[:, c, v:v + 1], bias=shv[:, c, v:v + 1]), reads=[pk, "s1T", "shT"], writes=[kh])
                            else:
                                P.op(DVE, lambda e, pt=pt, H=H, k=k, c=c, v=v: e.tensor_scalar(
                                    out=H[:, c * 128:(c + 1) * 128], in0=pt[:, k * 128:(k + 1) * 128],
                                    scalar1=s1v[:, c, v:v + 1], scalar2=shv[:, c, v:v + 1], op0=ALU.mult, op1=ALU.add),
                                    reads=[pk, "s1T", "shT"], writes=[kh])
                    P.op(SP, lambda e, H=H, t=t: e.dma_start(out=hT_d[t], in_=H[:]), reads=[kh], writes=["hT_d%d" % t], dma=True)
                    if stage == 1 and t in (2, 20):
                        dt_ = sbuf(ph, "dbgt", [128, 1024], F32)
                        P.op(POOL, lambda e, H=H, dt_=dt_: e.tensor_copy(out=dt_[:], in_=H[:, 0:1024]), reads=[kh], writes=["dd%d" % t])
                        o0 = 0 if t == 2 else 1024
                        P.op(SP, lambda e, dt_=dt_, o0=o0: e.dma_start(out=dbg[:, o0:o0 + 1024], in_=dt_[:]), reads=["dd%d" % t], writes=["dbg"], dma=True)
                finish(P)

        def load_hT_block(bi, buf, key):
            P.op(SP, lambda e: e.dma_start(out=buf[:].rearrange("p (t d) -> p t d", t=2), in_=hT_d[2 * bi:2 * bi + 2].rearrange("t p d -> p t d")),
                 reads=["hT_d%d" % (2 * bi), "hT_d%d" % (2 * bi + 1)], writes=[key], dma=True)

        def proj_fm(pst, pkey, W, wkey, col0, hb, hkey):
            hv = hb[:].rearrange("p (t d) -> p t d", t=2)
            for c in range(32):
                P.op(PE, lambda e, c=c: e.matmul(pst[:, 0:256], lhsT=W[:, c, col0:col0 + 128], rhs=hv[:, :, c * 128:(c + 1) * 128],
                                                 start=(c == 0), stop=(c == 31)), reads=[wkey, hkey], writes=[pkey])

        if stage >= 2 and 2 in phases:
            with contextlib.ExitStack() as ph:
                P = Prog(ss)
                Wn = [sbuf(ph, "Wn%d" % i, [128, 32, 512], BF16) for i in range(2)]
                bt = [sbuf(ph, "bt%d" % i, [128, 5, 576], BF16) for i in range(3)]
                hb = [sbuf(ph, "hb%d" % i, [128, 2 * D], BF16) for i in range(2)]
                qTs = [sbuf(ph, "qT%d" % i, [128, TLOC], BF16) for i in range(2)]
                kTs = [sbuf(ph, "kT%d" % i, [128, NT * 128], BF16) for i in range(2)]
                szTs = [sbuf(ph, "szT%d" % i, [128, TLOC], BF16) for i in range(2)]
                vsbs = [sbuf(ph, "vsb%d" % i, [128, NT, 128], BF16) for i in range(2)]
                yT = [sbuf(ph, "yT%d" % i, [128, TLOC], BF16) for i in range(2)]
                mx = sbuf(ph, "mx", [128, 2], F32)
                rs = sbuf(ph, "rs", [128, 2], F32)
                pf = sbuf(ph, "pf", [128, 832], F32)
                pb = sbuf(ph, "pb", [128, 832], BF16)
                pT = sbuf(ph, "pT", [128, 896], BF16)
                psr = Rot([(ps0, "ps0"), (ps1, "ps1"), (ps7, "ps7")])
                nheads = NH if stage > 3 else 2
                blkc = [0]

                def load_w(h):
                    W, B = Wn[h % 2], bt[h % 3]
                    P.op(POOL, lambda e: e.dma_start(out=W[:].rearrange("p c n -> p (c n)"), in_=w_na[h]), writes=["Wn%d" % (h % 2)], dma=True)
                    P.op(POOL, lambda e: e.dma_start(out=B[:].rearrange("p v k -> p (v k)"), in_=biasT[h]), writes=["bt%d" % (h % 3)], dma=True)

                def proj_gen(h):
                    par = h % 2
                    W, wkey = Wn[par], "Wn%d" % par
                    qT, kT, szT, vsb = qTs[par], kTs[par], szTs[par], vsbs[par]
                    qk, kk_, zk, vk = "qT%d" % par, "kT%d" % par, "szT%d" % par, "vsb%d" % par
                    for bi in range(NBLK):
                        Hb, hkey = hb[blkc[0] % 2], "hb%d" % (blkc[0] % 2)
                        blkc[0] += 1
                        load_hT_block(bi, Hb, hkey)
                        local = 1 <= bi <= 8
                        lo = (bi - 1) * 256
                        if local:
                            pt, pk = psr.next()
                            proj_fm(pt, pk, W, wkey, 0, Hb, hkey)
                            P.op(ACT, lambda e, pt=pt, lo=lo: e.activation(out=qT[:, lo:lo + 256], in_=pt[:, 0:256], func=AF.Copy, scale=128.0 ** -0.5),
                                 reads=[pk], writes=[qk])
                            yield
                            pt, pk = psr.next()
                            proj_fm(pt, pk, W, wkey, 384, Hb, hkey)
                            P.op(ACT, lambda e, pt=pt, lo=lo: e.activation(out=szT[:, lo:lo + 256], in_=pt[:, 0:256], func=AF.Silu),
                                 reads=[pk], writes=[zk])
                            yield
                        pt, pk = psr.next()
                        proj_fm(pt, pk, W, wkey, 128, Hb, hkey)
                        P.op(DVE, lambda e, pt=pt, bi=bi: e.tensor_copy(out=kT[:, bi * 256:(bi + 1) * 256], in_=pt[:, 0:256]), reads=[pk], writes=[kk_])
                        yield
                        hv = Hb[:].rearrange("p (t d) -> p t d", t=2)
                        for tl in range(2):
                            pt, pk = psr.next()
                            for c in range(32):
                                P.op(PE, lambda e, pt=pt, hv=hv, tl=tl, c=c: e.matmul(
                                    pt[:, 0:128], lhsT=hv[:, tl, c * 128:(c + 1) * 128], rhs=W[:, c, 256:384], start=(c == 0), stop=(c == 31)),
                                    reads=[wkey, hkey], writes=[pk])
                            P.op(DVE, lambda e, pt=pt, bi=bi, tl=tl: e.tensor_copy(out=vsb[:, 2 * bi + tl, :], in_=pt[:, 0:128]), reads=[pk], writes=[vk])
                            yield

                def attn_gen(h):
                    par = h % 2
                    B, bkey = bt[h % 3], "bt%d" % (h % 3)
                    Y, ykey = yT[par], "yT%d" % par
                    qT, kT, szT, vsb = qTs[par], kTs[par], szTs[par], vsbs[par]
                    qk, kk_, zk, vk = "qT%d" % par, "kT%d" % par, "szT%d" % par, "vsb%d" % par

                    def scores(pr):
                        var = {0: 0, 1: 1, 14: 3, 15: 4}.get(pr, 2)
                        q0 = pr * 128
                        P.op(PE, lambda e: e.matmul(ps23[:, 0:512], lhsT=qT[:, q0:q0 + 128], rhs=kT[:, q0:q0 + 512], start=True, stop=False),
                             reads=[qk, kk_], writes=["ps23"])
                        P.op(PE, lambda e: e.matmul(ps23[:, 0:512], lhsT=identb[:], rhs=B[:, var, 0:512], start=False, stop=True),
                             reads=[bkey, "identb"], writes=["ps23"])
                        P.op(PE, lambda e: e.matmul(ps23[:, 512:576], lhsT=qT[:, q0:q0 + 128], rhs=kT[:, q0 + 512:q0 + 576], start=True, stop=False),
                             reads=[qk, kk_], writes=["ps23"])
                        P.op(PE, lambda e: e.matmul(ps23[:, 512:576], lhsT=identb[:], rhs=B[:, var, 512:576], start=False, stop=True),
                             reads=[bkey, "identb"], writes=["ps23"])
                        P.op(PE, lambda e: e.matmul(ps23[:, 576:832], lhsT=qT[:, q0:q0 + 128], rhs=kT[:, 2560:2816], start=True, stop=True),
                             reads=[qk, kk_], writes=["ps23"])

                    def chain(pr):
                        P.op(DVE, lambda e: e.reduce_max(out=mx[:, 0:1], in_=ps23[:, 0:832], axis=AX.X), reads=["ps23"], writes=["mx"])
                        P.op(DVE, lambda e: e.tensor_scalar(out=mx[:, 1:2], in0=mx[:, 0:1], scalar1=-1.0, scalar2=None, op0=ALU.mult), writes=["mx"])
                        P.op(ACT, lambda e: e.activation(out=pf[:], in_=ps23[:, 0:832], func=AF.Exp, bias=mx[:, 1:2], scale=1.0, accum_out=rs[:, 0:1]),
                             reads=["ps23", "mx"], writes=["pf", "rs"])
                        P.op(DVE, lambda e: e.reciprocal(out=rs[:, 1:2], in_=rs[:, 0:1]), writes=["rs"])
                        P.op(DVE, lambda e: e.tensor_scalar(out=pb[:], in0=pf[:], scalar1=rs[:, 1:2], scalar2=None, op0=ALU.mult),
                             reads=["pf", "rs"], writes=["pb"])

                    def transposes(pr):
                        for j in range(7):
                            if j < 4:
                                src, dst = pb[:, j * 128:(j + 1) * 128], ps6[:, j * 128:(j + 1) * 128]
                            elif j == 4:
                                src, dst = pb[:, 512:576], ps6[0:64, 512:640]
                            else:
                                src, dst = pb[:, 576 + (j - 5) * 128:576 + (j - 4) * 128], ps6[:, 640 + (j - 5) * 128:640 + (j - 4) * 128]
                            P.op(PE, lambda e, src=src, dst=dst: e.transpose(out=dst, in_=src, identity=identb[:]), reads=["pb", "identb"], writes=["ps6"])
                        P.op(ACT, lambda e: e.copy(out=pT[:, 0:512], in_=ps6[:, 0:512]), reads=["ps6"], writes=["pT"])
                        P.op(DVE, lambda e: e.tensor_copy(out=pT[0:64, 512:640], in_=ps6[0:64, 512:640]), reads=["ps6"], writes=["pT"])
                        P.op(DVE, lambda e: e.tensor_copy(out=pT[:, 640:896], in_=ps6[:, 640:896]), reads=["ps6"], writes=["pT"])

                    def pv(pr):
                        q0 = pr * 128
                        for j in range(7):
                            if j < 4:
                                lt, rh = vsb[:, pr + j, :], pT[:, j * 128:(j + 1) * 128]
                            elif j == 4:
                                lt, rh = vsb[0:64, pr + 4, :], pT[0:64, 512:640]
                            else:
                                lt, rh = vsb[:, 20 + (j - 5), :], pT[:, 640 + (j - 5) * 128:640 + (j - 4) * 128]
                            P.op(PE, lambda e, lt=lt, rh=rh, j=j: e.matmul(ps4[:, 0:128], lhsT=lt, rhs=rh, start=(j == 0), stop=(j == 6)),
                                 reads=[vk, "pT"], writes=["ps4"])
                        P.op(DVE, lambda e: e.tensor_tensor(out=Y[:, q0:q0 + 128], in0=ps4[:, 0:128], in1=szT[:, q0:q0 + 128], op=ALU.mult),
                             reads=["ps4", zk], writes=[ykey])

                    scores(0)
                    chain(0)
                    yield
                    for pr in range(16):
                        if pr + 1 < 16:
                            scores(pr + 1)
                            yield
                        transposes(pr)
                        if pr + 1 < 16:
                            chain(pr + 1)
                        yield
                        pv(pr)
                        yield
                    P.op(ACT, lambda e: e.dma_start(out=ya_d[h], in_=Y[:]), reads=[ykey], writes=["ya_d%d" % h], dma=True)
                    if stage in (2, 3) and h == 0:
                        dt_ = sbuf(ph, "dbgt", [128, 2048], F32)
                        P.op(POOL, lambda e: e.tensor_copy(out=dt_[:], in_=Y[:]), reads=[ykey], writes=["dd"])
                        P.op(SP, lambda e: e.dma_start(out=dbg[:, 0:2048], in_=dt_[:]), reads=["dd"], writes=["dbg"], dma=True)

                load_w(0)
                for it in range(nheads + 1):
                    if it + 1 < nheads:
                        load_w(it + 1)
                    gens = []
                    if it < nheads:
                        gens.append(proj_gen(it))
                    if it >= 1:
                        gens.append(attn_gen(it - 1))
                    while gens:
                        for g_ in list(gens):
                            try:
                                next(g_)
                            except StopIteration:
                                gens.remove(g_)
                finish(P)

        if stage >= 3 and 3 in phases:
            with contextlib.ExitStack() as ph:
                P = Prog(ss)
                Wh = sbuf(ph, "Wh", [128, 32, 640], BF16)
                hb = [sbuf(ph, "hb%d" % i, [128, 2 * D], BF16) for i in range(2)]
                sgT = sbuf(ph, "sgT", [128, TLOC], BF16)
                qd = [sbuf(ph, "qd%d" % i, [128, TLOC], BF16) for i in range(2)]
                kd = [sbuf(ph, "kd%d" % i, [128, NT * 128], BF16) for i in range(2)]
                kdl = [sbuf(ph, "kdl%d" % i, [128, NT * 128], BF16) for i in range(2)]
                el = [sbuf(ph, "el%d" % i, [128, 44], F32) for i in range(2)]
                eref = [sbuf(ph, "eref%d" % i, [128, 44], F32) for i in range(2)]
                t_sq2 = [sbuf(ph, "t_sq%d" % i, [128, 256], F32) for i in range(2)]
                t_sig2 = [[sbuf(ph, "t_sg%d%d" % (d_, i), [128, 256], F32) for i in range(2)] for d_ in range(2)]
                elr = [sbuf(ph, "elr%d" % i, [128, 44], F32) for i in range(2)]
                vh = sbuf(ph, "vh", [64, 44, 128], BF16)
                Sbf = [sbuf(ph, "Sbf%d" % i, [128, 33, 128], BF16) for i in range(2)]
                Sst = {n: sbuf(ph, "S" + n, [128, 128], F32) for n in ("f", "b", "cf", "tf", "cb", "bb", "tmp0", "tmp1")}
                tmp = {n: [sbuf(ph, "t_%s%d" % (n, i), [128, 256], F32) for i in range(2)] for n in ("lf", "kk", "cum", "A", "E1", "E2")}
                kdTg = [sbuf(ph, "kdTg%d" % i, [64, 512], BF16) for i in range(2)]
                aTs = [sbuf(ph, "aT%d" % i, [64, 64], BF16) for i in range(4)]
                sqs = sbuf(ph, "sqs", [128, 512], BF16)
                t_r = sbuf(ph, "t_r", [128, 512], F32)
                t_o = sbuf(ph, "t_o", [128, 512], F32)
                ybT = [sbuf(ph, "ybT%d" % i, [128, TLOC], BF16) for i in range(2)]
                psr = Rot([(ps0, "ps0"), (ps1, "ps1"), (ps7, "ps7"), (ps4, "ps4"), (ps5, "ps5")])
                nheads = NH if stage > 3 else 1
                for h in range(nheads):
                    Y, ykey = ybT[h % 2], "ybT%d" % (h % 2)
                    P.op(POOL, lambda e, h=h: e.dma_start(out=Wh[:].rearrange("p c n -> p (c n)"), in_=w_hg[h]), writes=["Wh"], dma=True)

                    def chain_evac(pt, pk, d, bi):
                        P.op(ACT, lambda e: e.activation(out=t_sig2[d][bi % 2][:], in_=pt[:, 0:256], func=AF.Sigmoid), reads=[pk], writes=["t_sg%d%d" % (d, bi % 2)])

                    def chain(d, bi, local):
                        T = {n: tmp[n][d] for n in tmp}
                        K = {n: "t_%s%d" % (n, d) for n in tmp}
                        T["sig"] = t_sig2[d][bi % 2]
                        K["sig"] = "t_sg%d%d" % (d, bi % 2)
                        t_sq, tsqk = t_sq2[bi % 2], "t_sq%d" % (bi % 2)
                        hd = h * 2 + d
                        P.op(ACT, lambda e: e.activation(out=T["lf"][:], in_=T["sig"][:], func=AF.Ln, scale=oml[:, hd:hd + 1], bias=lbs[:, hd:hd + 1]),
                             reads=[K["sig"], "oml", "lbs"], writes=[K["lf"]])
                        P.op(DVE, lambda e: e.tensor_scalar(out=T["kk"][:], in0=T["sig"][:], scalar1=noml[:, hd:hd + 1], scalar2=oml[:, hd:hd + 1],
                                                            op0=ALU.mult, op1=ALU.add), reads=[K["sig"], "oml", "noml"], writes=[K["kk"]])
                        P.op(DVE, lambda e: e.tensor_tensor_scan(out=T["cum"][:], data0=rmask[:], data1=T["lf"][:], initial=0.0, op0=ALU.mult, op1=ALU.add),
                             reads=[K["lf"], "rmask"], writes=[K["cum"]])
                        c3 = T["cum"][:].rearrange("p (n c) -> p n c", c=64)
                        l3 = T["lf"][:].rearrange("p (n c) -> p n c", c=64)
                        a3 = T["A"][:].rearrange("p (n c) -> p n c", c=64)
                        if d == 0:
                            G, Gk = T["cum"], K["cum"]
                            g3 = c3
                            refpos, lastpos = 31, 63
                        else:
                            P.op(DVE, lambda e: e.tensor_tensor(out=T["lf"][:], in0=T["cum"][:], in1=T["lf"][:], op=ALU.subtract),
                                 reads=[K["cum"]], writes=[K["lf"]])
                            P.op(DVE, lambda e: e.tensor_tensor(out=l3, in0=c3[:, :, 63:64].to_broadcast([128, 4, 64]), in1=l3, op=ALU.subtract),
                                 reads=[K["cum"]], writes=[K["lf"]])
                            G, Gk = T["lf"], K["lf"]
                            g3 = l3
                            refpos, lastpos = 32, 0
                        P.op(DVE, lambda e: e.tensor_tensor(out=a3, in0=g3, in1=g3[:, :, refpos:refpos + 1].to_broadcast([128, 4, 64]), op=ALU.subtract),
                             reads=[Gk], writes=[K["A"]])
                        P.op(ACT, lambda e: e.activation(out=T["E2"][:], in_=T["A"][:], func=AF.Exp, scale=-1.0), reads=[K["A"]], writes=[K["E2"]])
                        P.op(ACT, lambda e: e.activation(out=T["E1"][:], in_=T["A"][:], func=AF.Exp), reads=[K["A"]], writes=[K["E1"]])
                        e13 = T["E1"][:].rearrange("p (n c) -> p n c", c=64)
                        P.op(DVE, lambda e: e.tensor_tensor(out=T["kk"][:], in0=T["kk"][:], in1=T["E2"][:], op=ALU.mult),
                             reads=[K["E2"]], writes=[K["kk"]])
                        if local:
                            P.op(POOL, lambda e: e.tensor_copy(out=kd[d][:, bi * 256:(bi + 1) * 256], in_=T["kk"][:]), reads=[K["kk"]], writes=["kd%d" % d])
                        P.op(DVE, lambda e: e.tensor_tensor(out=kdl[d][:, bi * 256:(bi + 1) * 256].rearrange("p (n c) -> p n c", c=64),
                                                            in0=T["kk"][:].rearrange("p (n c) -> p n c", c=64),
                                                            in1=e13[:, :, lastpos:lastpos + 1].to_broadcast([128, 4, 64]), op=ALU.mult),
                             reads=[K["kk"], K["E1"]], writes=["kdl%d" % d])
                        P.op(ACT, lambda e: e.activation(out=el[d][:, bi * 4:bi * 4 + 4], in_=g3[:, :, lastpos], func=AF.Exp), reads=[Gk], writes=["el%d" % d])
                        if local:
                            P.op(ACT, lambda e: e.activation(out=eref[d][:, bi * 4:bi * 4 + 4], in_=g3[:, :, refpos], func=AF.Exp), reads=[Gk], writes=["eref%d" % d])
                            lo = (bi - 1) * 256
                            P.op(DVE, lambda e: e.tensor_tensor(out=qd[d][:, lo:lo + 256], in0=t_sq[:], in1=T["E1"][:], op=ALU.mult),
                                 reads=[tsqk, K["E1"]], writes=["qd%d" % d])

                    for bi in range(NBLK):
                        Hb, hkey = hb[bi % 2], "hb%d" % (bi % 2)
                        load_hT_block(bi, Hb, hkey)
                        local = 1 <= bi <= 8
                        lo = (bi - 1) * 256
                        if local:
                            pt, pk = psr.next()
                            proj_fm(pt, pk, Wh, "Wh", 0, Hb, hkey)
                            P.op(ACT, lambda e, pt=pt, bi=bi: e.activation(out=t_sq2[bi % 2][:], in_=pt[:, 0:256], func=AF.Silu), reads=[pk], writes=["t_sq%d" % (bi % 2)])
                            pt, pk = psr.next()
                            proj_fm(pt, pk, Wh, "Wh", 512, Hb, hkey)
                            P.op(ACT, lambda e, pt=pt, lo=lo: e.activation(out=sgT[:, lo:lo + 256], in_=pt[:, 0:256], func=AF.Silu), reads=[pk], writes=["sgT"])
                        if bi != 9:
                            pt, pk = psr.next()
                            proj_fm(pt, pk, Wh, "Wh", 128, Hb, hkey)
                            chain_evac(pt, pk, 0, bi)
                        if bi != 0:
                            pt, pk = psr.next()
                            proj_fm(pt, pk, Wh, "Wh", 256, Hb, hkey)
                            chain_evac(pt, pk, 1, bi)
                        hv = Hb[:].rearrange("p (t d) -> p t d", t=2)
                        for ch in range(4):
                            pt, pk = psr.next()
                            tl, off = ch // 2, (ch % 2) * 64
                            for c in range(32):
                                P.op(PE, lambda e, pt=pt, hv=hv, tl=tl, off=off, c=c: e.matmul(
                                    pt[0:64, 0:128], lhsT=hv[:, tl, c * 128 + off:c * 128 + off + 64], rhs=Wh[:, c, 384:512],
                                    start=(c == 0), stop=(c == 31)), reads=["Wh", hkey], writes=[pk])
                            P.op(DVE, lambda e, pt=pt, bi=bi, ch=ch: e.tensor_copy(out=vh[:, bi * 4 + ch, :], in_=pt[0:64, 0:128]), reads=[pk], writes=["vh"])
                        if bi != 9:
                            chain(0, bi, local)
                        if bi != 0:
                            chain(1, bi, local)

                    tasks = []
                    tasks.append((0, [40, 41, 42, 43], [(Sst["cf"], "Scf", Sst["cf"], "Scf", j == 0, None) for j in range(4)]))
                    tasks.append((1, [43, 42, 41, 40], [(Sst["cb"], "Scb", Sst["cb"], "Scb", j == 0, None) for j in range(4)]))
                    tasks.append((0, [0, 1, 2, 3], [(Sst["tf"], "Stf", Sst["tf"], "Stf", j == 0, None) for j in range(4)]))
                    tasks.append((1, [39, 38, 37, 36], [(Sst["bb"], "Sbb", Sst["bb"], "Sbb", j == 0, None) for j in range(4)]))
                    tasks.append("blend")
                    fpp = [(Sst["f"], "Sf"), (Sst["tmp0"], "Stmp0")]
                    bpp = [(Sst["b"], "Sb"), (Sst["tmp1"], "Stmp1")]
                    for q0_ in range(0, 31, 4):
                        ns = list(range(q0_, min(q0_ + 4, 31)))
                        tasks.append((0, [4 + n for n in ns], [(fpp[n % 2][0], fpp[n % 2][1], fpp[(n + 1) % 2][0], fpp[(n + 1) % 2][1], False, n + 1) for n in ns]))
                        tasks.append((1, [35 - n for n in ns], [(bpp[n % 2][0], bpp[n % 2][1], bpp[(n + 1) % 2][0], bpp[(n + 1) % 2][1], False, 30 - n) for n in ns]))
                    tcount = [0]
                    mcount = [0]

                    def prep(task, k):
                        d, chunks, _ = task
                        bank, bkey = (ps6[:, 0:512], "ps6") if k % 2 == 0 else (ps7b, "ps7")
                        for j, g in enumerate(chunks):
                            P.op(PE, lambda e, g=g, j=j: e.transpose(out=bank[0:64, j * 128:(j + 1) * 128], in_=kdl[d][:, g * 64:(g + 1) * 64],
                                                                 identity=identb[:]), reads=["kdl%d" % d, "identb"], writes=[bkey])
                        w = len(chunks) * 128
                        P.op(ACT, lambda e: e.copy(out=kdTg[k % 2][:, 0:w], in_=bank[0:64, 0:w]), reads=[bkey], writes=["kdTg%d" % (k % 2)])

                    def consume(task, k):
                        d, chunks, steps = task
                        for j, g in enumerate(chunks):
                            S_in, k_in, S_out, k_out, first, store = steps[j]
                            i = mcount[0] % 2
                            mcount[0] += 1
                            pu, pukey = (ps4, "ps4") if i == 0 else (ps5, "ps5")
                            P.op(PE, lambda e, pu=pu, j=j, g=g: e.matmul(pu[:, 0:128], lhsT=kdTg[k % 2][:, j * 128:(j + 1) * 128], rhs=vh[:, g, :], start=True, stop=True),
                                 reads=["kdTg%d" % (k % 2), "vh"], writes=[pukey])
                            if first:
                                P.op(DVE, lambda e, pu=pu, S_out=S_out: e.tensor_copy(out=S_out[:], in_=pu[:, 0:128]), reads=[pukey], writes=[k_out])
                            else:
                                P.op(DVE, lambda e, pu=pu, S_in=S_in, S_out=S_out, g=g: e.scalar_tensor_tensor(
                                    out=S_out[:], in0=S_in[:], scalar=el[d][:, g:g + 1], in1=pu[:, 0:128], op0=ALU.mult, op1=ALU.add),
                                    reads=[k_in, "el%d" % d, pukey], writes=[k_out])
                            if store is not None:
                                gc = 4 + store
                                P.op(ACT, lambda e, S_out=S_out, store=store, gc=gc: e.activation(out=Sbf[d][:, store, :], in_=S_out[:], func=AF.Copy,
                                                                                                 scale=eref[d][:, gc:gc + 1]),
                                     reads=[k_out, "eref%d" % d], writes=["Sbf%d" % d])

                    def blend():
                        P.op(DVE, lambda e: e.tensor_scalar(out=Sst["f"][:], in0=Sst["cf"][:], scalar1=cm[:, 0:1], scalar2=None, op0=ALU.mult),
                             reads=["Scf", "cm"], writes=["Sf"])
                        P.op(DVE, lambda e: e.scalar_tensor_tensor(out=Sst["f"][:], in0=Sst["tf"][:], scalar=cm[:, 1:2], in1=Sst["f"][:], op0=ALU.mult, op1=ALU.add),
                             reads=["Stf", "cm"], writes=["Sf"])
                        P.op(DVE, lambda e: e.tensor_scalar(out=Sst["b"][:], in0=Sst["cb"][:], scalar1=cm[:, 2:3], scalar2=None, op0=ALU.mult),
                             reads=["Scb", "cm"], writes=["Sb"])
                        P.op(DVE, lambda e: e.scalar_tensor_tensor(out=Sst["b"][:], in0=Sst["bb"][:], scalar=cm[:, 3:4], in1=Sst["b"][:], op0=ALU.mult, op1=ALU.add),
                             reads=["Sbb", "cm"], writes=["Sb"])
                        P.op(ACT, lambda e: e.activation(out=Sbf[0][:, 0, :], in_=Sst["f"][:], func=AF.Copy, scale=eref[0][:, 4:5]),
                             reads=["Sf", "eref0"], writes=["Sbf0"])
                        P.op(ACT, lambda e: e.activation(out=Sbf[1][:, 31, :], in_=Sst["b"][:], func=AF.Copy, scale=eref[1][:, 35:36]),
                             reads=["Sb", "eref1"], writes=["Sbf1"])

                    real = [t for t in tasks if t != "blend"]
                    idx_of = {id(t): i for i, t in enumerate(real)}
                    prep(real[0], 0)
                    for t in tasks:
                        if t == "blend":
                            blend()
                            continue
                        k = idx_of[id(t)]
                        if k + 1 < len(real):
                            prep(real[k + 1], k + 1)
                        consume(t, k)
                    def aT_part(n):
                        g = 4 + n
                        lo = n * 64
                        for d in range(2):
                            i = (n * 2 + d) % 4
                            pa_, pak = ((ps4, "ps4"), (ps5, "ps5"), (ps0, "ps0"), (ps1, "ps1"))[i]
                            P.op(PE, lambda e, pa_=pa_, d=d: e.matmul(pa_[0:64, 0:64], lhsT=kd[d][:, g * 64:(g + 1) * 64], rhs=qd[d][:, lo:lo + 64],
                                                                   start=True, stop=True), reads=["kd%d" % d, "qd%d" % d], writes=[pak])
                            mk = maskF if d == 0 else maskB
                            P.op(DVE, lambda e, pa_=pa_, i=i, mk=mk: e.tensor_tensor(out=aTs[i][:], in0=pa_[0:64, 0:64], in1=mk[:], op=ALU.mult),
                                 reads=[pak, "maskF", "maskB"], writes=["aT%d" % i])

                    def o_part(n, half, hk):
                        g = 4 + n
                        lo = n * 64
                        cn = n % 8
                        oc = half[:, cn * 64:(cn + 1) * 64]
                        a0, a1 = aTs[(n * 2) % 4], aTs[(n * 2 + 1) % 4]
                        k0, k1 = "aT%d" % ((n * 2) % 4), "aT%d" % ((n * 2 + 1) % 4)
                        P.op(PE, lambda e: e.matmul(oc, lhsT=vh[:, g, :], rhs=a0[:], start=True, stop=False), reads=["vh", k0], writes=[hk])
                        P.op(PE, lambda e: e.matmul(oc, lhsT=Sbf[0][:, n, :], rhs=qd[0][:, lo:lo + 64], start=False, stop=False), reads=["Sbf0", "qd0"], writes=[hk])
                        P.op(PE, lambda e: e.matmul(oc, lhsT=vh[:, g, :], rhs=a1[:], start=False, stop=False), reads=["vh", k1], writes=[hk])
                        P.op(PE, lambda e: e.matmul(oc, lhsT=Sbf[1][:, n, :], rhs=qd[1][:, lo:lo + 64], start=False, stop=True), reads=["Sbf1", "qd1"], writes=[hk])

                    aT_part(0)
                    for grp in range(4):
                        half = ps23[:, 0:512] if grp % 2 == 0 else ps23[:, 512:1024]
                        hk = "ps23a" if grp % 2 == 0 else "ps23b"
                        for cn in range(8):
                            n = grp * 8 + cn
                            if n + 1 < 32:
                                aT_part(n + 1)
                            o_part(n, half, hk)
                        g0 = grp * 512
                        P.op(ACT, lambda e, half=half: e.activation(out=sqs[:], in_=half, func=AF.Square), reads=[hk], writes=["sqs"])
                        P.op(PE, lambda e: e.matmul(ps7[:, 0:512], lhsT=onesb[:], rhs=sqs[:], start=True, stop=True), reads=["sqs", "onesb"], writes=["ps7"])
                        P.op(ACT, lambda e: e.activation(out=t_r[:], in_=ps7[:, 0:512], func=AF.Sqrt, scale=1.0 / 128.0, bias=1e-6), reads=["ps7"], writes=["t_r"])
                        P.op(DVE, lambda e: e.reciprocal(out=t_r[:], in_=t_r[:]), writes=["t_r"])
                        P.op(DVE, lambda e, half=half: e.tensor_tensor(out=t_o[:], in0=half, in1=t_r[:], op=ALU.mult), reads=[hk, "t_r"], writes=["t_o"])
                        P.op(DVE, lambda e, Y=Y, g0=g0: e.scalar_tensor_tensor(out=Y[:, g0:g0 + 512], in0=t_o[:], scalar=nw[:, 0:1], in1=sgT[:, g0:g0 + 512],
                                                                              op0=ALU.mult, op1=ALU.mult), reads=["t_o", "nw", "sgT"], writes=[ykey])
                    P.op(ACT, lambda e, Y=Y, h=h: e.dma_start(out=yb_d[h], in_=Y[:]), reads=[ykey], writes=["yb_d%d" % h], dma=True)
                    if stage == 3:
                        dt_ = sbuf(ph, "dbgt", [128, 2048], F32)
                        P.op(POOL, lambda e, Y=Y, dt_=dt_: e.tensor_copy(out=dt_[:], in_=Y[:]), reads=[ykey], writes=["dd"])
                        P.op(SP, lambda e, dt_=dt_: e.dma_start(out=dbg[:, 2048:4096], in_=dt_[:]), reads=["dd"], writes=["dbg"], dma=True)
                        P.op(SP, lambda e: e.dma_start(out=dbg[:, 4096:4224], in_=Sst["cf"][:]), reads=["Scf"], writes=["dbg"], dma=True)
                        P.op(SP, lambda e: e.dma_start(out=dbg[:, 4224:4352], in_=Sst["cb"][:]), reads=["Scb"], writes=["dbg"], dma=True)
                finish(P)

        if stage >= 4:
            with contextlib.ExitStack() as ph4:
                mT = sbuf(ph4, "mT", [128, 32, 512], BF16)
                for tb in range(4):
                    with contextlib.ExitStack() as ph:
                        P = Prog(ss)
                        h4 = sbuf(ph, "h4", [128, 4, D], BF16)
                        ya4 = sbuf(ph, "ya4", [128, NH, 512], BF16)
                        yb4 = sbuf(ph, "yb4", [128, NH, 512], BF16)
                        W4 = [sbuf(ph, "W4%d" % i, [128, 96, 128], BF16) for i in range(3)]
                        sg = [sbuf(ph, "sg%d" % i, [128, 512], F32) for i in range(2)]
                        tt = [sbuf(ph, "tt%d" % i, [128, 512], F32) for i in range(2)]
                        P.op(SP, lambda e, tb=tb: e.dma_start(out=h4[:], in_=hT_d[2 + 4 * tb:6 + 4 * tb].rearrange("t p d -> p t d")),
                             reads=["hT_d%d" % (2 + 4 * tb + i) for i in range(4)], writes=["h4"], dma=True)
                        P.op(SP, lambda e, tb=tb: e.dma_start(out=ya4[:], in_=ya_d[:, :, tb * 512:(tb + 1) * 512].rearrange("h p t -> p h t")),
                             reads=["ya_d%d" % i for i in range(NH)], writes=["ya4"], dma=True)
                        P.op(SP, lambda e, tb=tb: e.dma_start(out=yb4[:], in_=yb_d[:, :, tb * 512:(tb + 1) * 512].rearrange("h p t -> p h t")),
                             reads=["yb_d%d" % i for i in range(NH)], writes=["yb4"], dma=True)
                        for jb in range(32):
                            W, wkey = W4[jb % 3], "W4%d" % (jb % 3)
                            P.op(POOL, lambda e, W=W, jb=jb: e.dma_start(out=W[:].rearrange("p c n -> p (c n)"), in_=w_4a[jb]), writes=[wkey], dma=True)
                            for c in range(32):
                                P.op(PE, lambda e, W=W, c=c: e.matmul(ps0[:], lhsT=W[:, c, :], rhs=h4[:, :, c * 128:(c + 1) * 128], start=(c == 0), stop=(c == 31)),
                                     reads=[wkey, "h4"], writes=["ps0"])
                            for c in range(32):
                                P.op(PE, lambda e, W=W, c=c: e.matmul(ps1[:], lhsT=W[:, 32 + c, :], rhs=h4[:, :, c * 128:(c + 1) * 128], start=(c == 0), stop=(c == 31)),
                                     reads=[wkey, "h4"], writes=["ps1"])
                            for hh in range(NH):
                                P.op(PE, lambda e, W=W, hh=hh: e.matmul(ps23[:, 0:512], lhsT=W[:, 64 + hh, :], rhs=ya4[:, hh, :], start=(hh == 0), stop=(hh == NH - 1)),
                                     reads=[wkey, "ya4"], writes=["ps23a"])
                            for hh in range(NH):
                                P.op(PE, lambda e, W=W, hh=hh: e.matmul(ps23[:, 512:1024], lhsT=W[:, 80 + hh, :], rhs=yb4[:, hh, :], start=(hh == 0), stop=(hh == NH - 1)),
                                     reads=[wkey, "yb4"], writes=["ps23b"])
                            P.op(ACT, lambda e: e.activation(out=sg[0][:], in_=ps0[:], func=AF.Sigmoid), reads=["ps0"], writes=["sg0"])
                            P.op(ACT, lambda e: e.activation(out=sg[1][:], in_=ps1[:], func=AF.Sigmoid), reads=["ps1"], writes=["sg1"])
                            P.op(DVE, lambda e: e.tensor_tensor(out=tt[0][:], in0=ps23[:, 0:512], in1=sg[0][:], op=ALU.mult), reads=["ps23a", "sg0"], writes=["tt0"])
                            P.op(DVE, lambda e: e.tensor_tensor(out=tt[1][:], in0=ps23[:, 512:1024], in1=sg[1][:], op=ALU.mult), reads=["ps23b", "sg1"], writes=["tt1"])
                            P.op(DVE, lambda e, jb=jb: e.tensor_tensor(out=mT[:, jb, :], in0=tt[0][:], in1=tt[1][:], op=ALU.add), reads=["tt0", "tt1"], writes=["mT"])
                        finish(P)
                    with contextlib.ExitStack() as ph:
                        P = Prog(ss)
                        zt = sbuf(ph, "zt", [128, 4, D], F32)
                        Wo = [sbuf(ph, "Wo%d" % i, [128, 32, 256], BF16) for i in range(3)]
                        gbc = sbuf(ph, "gbc", [128, D], F32)
                        lgbc = sbuf(ph, "lgbc", [128, D], F32)
                        lbbc = sbuf(ph, "lbbc", [128, D], F32)
                        t2 = [sbuf(ph, "t2%d" % i, [128, 256], F32) for i in range(2)]
                        stt = sbuf(ph, "stt4", [128, 8, 6], F32)
                        mv = sbuf(ph, "mv4", [128, 4], F32)
                        P.op(SP, lambda e: e.dma_start(out=gbc[:], in_=gate_d.rearrange("j p -> (j p)").partition_broadcast(128)), writes=["gbc"], dma=True)
                        P.op(SP, lambda e: e.dma_start(out=lgbc[:], in_=lng.rearrange("o d -> (o d)").partition_broadcast(128)), writes=["lgbc"], dma=True)
                        P.op(SP, lambda e: e.dma_start(out=lbbc[:], in_=lnb.rearrange("o d -> (o d)").partition_broadcast(128)), writes=["lbbc"], dma=True)
                        for tl in range(4):
                            r0 = (2 + 4 * tb + tl) * 128
                            P.op(SP, lambda e, tl=tl, r0=r0: e.dma_start(out=zt[:, tl, :], in_=xe[r0:r0 + 128, :]), writes=["zt%d" % tl], dma=True)
                        pso = Rot([(ps4, "ps4"), (ps5, "ps5"), (ps7, "ps7")])
                        for nb in range(16):
                            W, wkey = Wo[nb % 3], "Wo%d" % (nb % 3)
                            P.op(POOL, lambda e, W=W, nb=nb: e.dma_start(out=W[:].rearrange("p c n -> p (c n)"), in_=w_4b[nb]), writes=[wkey], dma=True)
                            for tl in range(4):
                                pt, pk = pso.next()
                                for jb in range(32):
                                    P.op(PE, lambda e, pt=pt, W=W, jb=jb, tl=tl: e.matmul(pt[:, 0:256], lhsT=mT[:, jb, tl * 128:(tl + 1) * 128], rhs=W[:, jb, :],
                                                                                         start=(jb == 0), stop=(jb == 31)), reads=[wkey, "mT"], writes=[pk])
                                T2, tk = t2[(nb * 4 + tl) % 2], "t2%d" % ((nb * 4 + tl) % 2)
                                c0 = nb * 256
                                P.op(DVE, lambda e, pt=pt, T2=T2, c0=c0: e.tensor_tensor(out=T2[:], in0=pt[:, 0:256], in1=gbc[:, c0:c0 + 256], op=ALU.mult),
                                     reads=[pk, "gbc"], writes=[tk])
                                P.op(DVE, lambda e, T2=T2, tl=tl, c0=c0: e.scalar_tensor_tensor(out=zt[:, tl, c0:c0 + 256], in0=zt[:, tl, c0:c0 + 256], scalar=ALPHA,
                                                                                                in1=T2[:], op0=ALU.mult, op1=ALU.add), reads=[tk], writes=["zt%d" % tl])
                        for tl in range(4):
                            zk = "zt%d" % tl
                            for g in range(8):
                                P.op(DVE, lambda e, tl=tl, g=g: e.bn_stats(out=stt[:, g, :], in_=zt[:, tl, g * 512:(g + 1) * 512]), reads=[zk], writes=["stt4"])
                            P.op(DVE, lambda e: e.bn_aggr(out=mv[:, 0:2], in_=stt[:]), reads=["stt4"], writes=["mv4"])
                            P.op(ACT, lambda e: e.activation(out=mv[:, 1:2], in_=mv[:, 1:2], func=AF.Sqrt, bias=1e-6, scale=1.0), writes=["mv4"])
                            P.op(DVE, lambda e: e.reciprocal(out=mv[:, 1:2], in_=mv[:, 1:2]), writes=["mv4"])
                            P.op(DVE, lambda e: e.tensor_scalar(out=mv[:, 2:3], in0=mv[:, 0:1], scalar1=mv[:, 1:2], scalar2=-1.0, op0=ALU.mult, op1=ALU.mult),
                                 writes=["mv4"])
                            P.op(ACT, lambda e, tl=tl: e.activation(out=zt[:, tl, :], in_=zt[:, tl, :], func=AF.Identity, scale=mv[:, 1:2], bias=mv[:, 2:3]),
                                 reads=["mv4"], writes=[zk])
                            P.op(DVE, lambda e, tl=tl: e.tensor_tensor(out=zt[:, tl, :], in0=zt[:, tl, :], in1=lgbc[:], op=ALU.mult), reads=["lgbc"], writes=[zk])
                            P.op(DVE, lambda e, tl=tl: e.tensor_tensor(out=zt[:, tl, :], in0=zt[:, tl, :], in1=lbbc[:], op=ALU.add), reads=["lbbc"], writes=[zk])
                            r0 = (4 * tb + tl) * 128
                            P.op(SP, lambda e, tl=tl, r0=r0: e.dma_start(out=out[r0:r0 + 128, :], in_=zt[:, tl, :]), reads=[zk], writes=["out"], dma=True)
                        finish(P)
    return nc, nops[0]


def _bias_tables(rpb):
    qc = np.arange(64)
    cs = np.clip(qc - 8, 0, 48)
    kc = np.arange(64)
    col_in = (kc[None, :] >= cs[:, None]) & (kc[None, :] < cs[:, None] + 16)
    dcol = np.clip(kc[None, :] - qc[:, None], -15, 15) + 15
    tabs = np.full((4, NH, 128, 5, 576), NEG, np.float32)
    for seg in range(4):
        for vi, pr in enumerate((0, 1, 2, 14, 15)):
            for a in range(2):
                lr = 2 * pr + a
                r = seg * 32 + lr
                rs = min(max(r - 4, 0), 120)
                for j in range(9):
                    if not (a <= j <= a + 7):
                        continue
                    kap = 2 * pr + j
                    g = seg * 32 - 4 + kap
                    if seg == 0 and kap < 4:
                        g = 4 + kap
                    if seg == 3 and kap >= 36:
                        g = 120 + (kap - 36)
                    assert rs <= g < rs + 8, (seg, pr, a, j, g, rs)
                    dr = g - r + 7
                    blk = rpb[:, dr][:, dcol]
                    blk = np.where(col_in[None], blk, np.float32(NEG))
                    tabs[seg, :, a * 64:(a + 1) * 64, vi, j * 64:(j + 1) * 64] = blk
    return tabs.reshape(4, NH, 128, 5 * 576)


_CACHE = {}


def _prep_shared(w_ada, b_ada, w_in, hg_lb_fwd, hg_lb_bwd, hg_norm_w, w_pa, w_pb, w_out, ln_g, ln_b):
    w_in0 = w_in[0]
    wv = w_in0.reshape(32, 128, 26624)

    def grp(g):
        s = [0, 2048, 4096, 6144, 8192, 10240, 12288, 14336, 16384, 18432, 22528][g]
        return s

    w_na = np.empty((NH, 128, 32, 512), np.float32)
    w_hg = np.empty((NH, 128, 32, 640), np.float32)
    for h in range(NH):
        for gi in range(4):
            s = grp(gi) + h * 128
            w_na[h, :, :, gi * 128:(gi + 1) * 128] = wv[:, :, s:s + 128].transpose(1, 0, 2)
        for gi in range(5):
            s = grp(4 + gi) + h * 128
            w_hg[h, :, :, gi * 128:(gi + 1) * 128] = wv[:, :, s:s + 128].transpose(1, 0, 2)
    w_4a = np.empty((32, 128, 96, 128), np.float32)
    pav = w_pa[0].reshape(16, 128, D)
    pbv = w_pb[0].reshape(16, 128, D)
    for jb in range(32):
        w_4a[jb, :, 0:32, :] = wv[:, :, 18432 + jb * 128:18432 + (jb + 1) * 128].transpose(1, 0, 2)
        w_4a[jb, :, 32:64, :] = wv[:, :, 22528 + jb * 128:22528 + (jb + 1) * 128].transpose(1, 0, 2)
        w_4a[jb, :, 64:80, :] = pav[:, :, jb * 128:(jb + 1) * 128].transpose(1, 0, 2)
        w_4a[jb, :, 80:96, :] = pbv[:, :, jb * 128:(jb + 1) * 128].transpose(1, 0, 2)
    wov = w_out[0].reshape(32, 128, D)
    w_4b = np.empty((16, 128, 32, 256), np.float32)
    for nb in range(16):
        w_4b[nb] = wov[:, :, nb * 256:(nb + 1) * 256].transpose(1, 0, 2)
    lbl = np.empty((128, NH, 2, 2), np.float32)
    lbl[:, :, 0, :] = hg_lb_fwd.reshape(2, NH, 128).transpose(2, 1, 0)
    lbl[:, :, 1, :] = hg_lb_bwd.reshape(2, NH, 128).transpose(2, 1, 0)
    return dict(
        w_ada=np.ascontiguousarray(w_ada[0]),
        b_adaT=np.ascontiguousarray(b_ada[0].reshape(96, 128).T),
        w_na=w_na.reshape(NH, 128, 32 * 512), w_hg=w_hg.reshape(NH, 128, 32 * 640),
        w_4a=w_4a.reshape(32, 128, 96 * 128), w_4b=w_4b.reshape(16, 128, 32 * 256),
        lbl=lbl.reshape(128, NH * 4), nw=np.ascontiguousarray(hg_norm_w[0].reshape(128, 1)),
        lng=np.ascontiguousarray(ln_g[0].reshape(1, D)), lnb=np.ascontiguousarray(ln_b[0].reshape(1, D)),
    )


def make_in_maps(x, c, ctx, c_ctx, w_ada, b_ada, w_in, na_rpb, hg_lb_fwd, hg_lb_bwd, hg_norm_w, w_pa, w_pb, w_out, ln_g, ln_b):
    f = lambda a: np.asarray(a, dtype=np.float32)
    x, c, ctx, c_ctx = f(x), f(c), f(ctx), f(c_ctx)
    shared = _prep_shared(f(w_ada), f(b_ada), f(w_in), f(hg_lb_fwd), f(hg_lb_bwd), f(hg_norm_w), f(w_pa), f(w_pb), f(w_out), f(ln_g), f(ln_b))
    tabs = _bias_tables(f(na_rpb)[0])
    maps = []
    for core in range(8):
        b, seg = core // 4, core % 4
        t0 = seg * TLOC
        top = x[b, t0 - 256:t0] if seg > 0 else x[b, 256:512]
        bot = x[b, t0 + TLOC:t0 + TLOC + 256] if seg < 3 else x[b, 7680:7936]
        xe = np.concatenate([top, x[b, t0:t0 + TLOC], bot, ctx[b]], axis=0)
        condT = np.concatenate([c[b].reshape(32, 128).T, c_ctx.reshape(32, 128).T], axis=1)
        m0 = 1.0 if seg == 0 else 0.0
        m3 = 1.0 if seg == 3 else 0.0
        cmask = np.tile(np.array([[m0, 1.0 - m0, m3, 1.0 - m3]], np.float32), (128, 1))
        d = dict(shared)
        d.update(xe=np.ascontiguousarray(xe), condT=np.ascontiguousarray(condT), biasT=np.ascontiguousarray(tabs[seg]), cmask=cmask)
        maps.append(d)
    return maps


def kernel(**inputs):
    if "nc" not in _CACHE:
        _CACHE["nc"] = build_program()[0]
    nc = _CACHE["nc"]
    maps = make_in_maps(**inputs)
    res = run_bass_kernel_spmd(nc, maps, core_ids=list(range(8)))
    outp = np.empty((2, 8192, D), np.float32)
    for core in range(8):
        b, seg = core // 4, core % 4
        outp[b, seg * TLOC:(seg + 1) * TLOC] = res.results[core]["out"]
    return outp
```

```python
import contextlib
import numpy as np
import concourse.bass as bass
import concourse.mybir as mybir
from concourse.bass_utils import run_bass_kernel_spmd

F32 = mybir.dt.float32
BF16 = mybir.dt.bfloat16
AF = mybir.ActivationFunctionType
ALU = mybir.AluOpType
AX = mybir.AxisListType

PE, ACT, DVE, POOL, SP = "pe", "act", "dve", "pool", "sp"
ENGS = (PE, ACT, DVE, POOL, SP)
NDMA_SEM = {SP: 8, POOL: 6, ACT: 2}

D = 4096
NH = 16
NT = 22
NBLK = 11
TLOC = 2048
ALPHA = 2.0 ** 0.25
NEG = -30000.0


class SemState:
    def __init__(self, nc, stack):
        self.esem = {e: stack.enter_context(nc.semaphore("s_" + e)) for e in ENGS}
        self.dsem = {}
        for e, K in NDMA_SEM.items():
            for i in range(K):
                self.dsem[(e, i)] = stack.enter_context(nc.semaphore("d_%s%d" % (e, i)))
        self.cnt = {e: 0 for e in ENGS}
        self.ndma = {e: 0 for e in ENGS}


class Prog:
    def __init__(self, ss, same_engine_sync=True):
        self.ss = ss
        self.ops = []
        self.last_w = {}
        self.readers = {}
        self.same_engine_sync = same_engine_sync
        self.ndma = ss.ndma
        self.dma_hist = {e: [] for e in ENGS}
        self.last_of = {}

    def op(self, eng, fn, reads=(), writes=(), dma=False, extra_deps=()):
        idx = len(self.ops)
        deps = set(extra_deps)
        for k in list(reads) + list(writes):
            w = self.last_w.get(k)
            if w is not None:
                deps.add(w)
        for k in writes:
            for r in self.readers.get(k, ()):
                deps.add(r)
        deps.discard(idx)
        rec = dict(idx=idx, eng=eng, fn=fn, deps=deps, dma=dma, signal=dma, dslot=None)
        if dma:
            n = self.ndma[eng]
            self.ndma[eng] += 1
            K = NDMA_SEM[eng]
            rec["dslot"] = (eng, n % K, 16 * (n // K + 1))
            rec["dprev"] = 16 * (n // K)
            self.dma_hist[eng].append(idx)
        elif fn is not None:
            self.last_of[eng] = idx
        self.ops.append(rec)
        for k in writes:
            self.last_w[k] = idx
            self.readers[k] = []
        for k in reads:
            if k not in writes:
                self.readers.setdefault(k, []).append(idx)
        return idx

    def barrier(self):
        deps = set(self.last_of.values())
        for e, K in NDMA_SEM.items():
            deps.update(self.dma_hist[e][-K:])
        for e in ENGS:
            self.op(e, None, extra_deps=deps)

    def emit(self, nc):
        ops = self.ops
        esem, dsem, cnt = self.ss.esem, self.ss.dsem, self.ss.cnt
        for o in ops:
            for d in o["deps"]:
                p = ops[d]
                if p["dma"]:
                    continue
                if p["eng"] == o["eng"] and not o["dma"]:
                    if p["eng"] == PE or not self.same_engine_sync:
                        continue
                p["signal"] = True
        for o in ops:
            if o["dma"]:
                e, s, v = o["dslot"]
                o["sig"] = (dsem[(e, s)], v, 16, ("d", e, s))
            elif o["signal"]:
                cnt[o["eng"]] += 1
                o["sig"] = (esem[o["eng"]], cnt[o["eng"]], 1, ("e", o["eng"]))
        waited = {e: {} for e in ENGS}
        for o in ops:
            e = o["eng"]
            waits = {}
            for d in o["deps"]:
                p = ops[d]
                if not p["signal"] or p["fn"] is None:
                    continue
                if (not p["dma"]) and p["eng"] == e and not o["dma"]:
                    if e == PE or not self.same_engine_sync:
                        continue
                sem, val, _, key = p["sig"]
                if waits.get(key, (None, 0))[1] < val:
                    waits[key] = (sem, val)
            if o["dma"] and o["dprev"] > 0:
                sem, val, _, key = o["sig"]
                if waits.get(key, (None, 0))[1] < o["dprev"]:
                    waits[key] = (sem, o["dprev"])
            wl = []
            for key, (sem, val) in waits.items():
                if waited[e].get(key, 0) >= val:
                    continue
                waited[e][key] = val
                wl.append((sem, val))
            o["waits"] = wl
        per = {e: [o for o in ops if o["eng"] == e] for e in ENGS}
        self.stats = {e: len(per[e]) for e in ENGS}
        def run(eng_ops):
            def body(engine):
                for o in eng_ops:
                    for sem, val in o["waits"]:
                        engine.wait_ge(sem, val)
                    if o["fn"] is None:
                        continue
                    inst = o["fn"](engine)
                    if o["signal"]:
                        sem, val, inc, _ = o["sig"]
                        inst.then_inc(sem, inc)
            return body

        with nc.Block() as block:
            block.sync(run(per[SP]))
            block.tensor(run(per[PE]))
            block.scalar(run(per[ACT]))
            block.vector(run(per[DVE]))
            block.gpsimd(run(per[POOL]))


class Rot:
    def __init__(self, items):
        self.items = items
        self.i = 0

    def next(self):
        it = self.items[self.i % len(self.items)]
        self.i += 1
        return it


def build_program(stage=99, phases=(2, 3)):
    nc = bass.Bass("TRN2", target_bir_lowering=False)
    dtn = nc.dram_tensor
    xe = dtn("xe", [NT * 128, D], F32, kind="ExternalInput").ap()
    condT = dtn("condT", [128, 64], F32, kind="ExternalInput").ap()
    w_ada = dtn("w_ada", [D, 3 * D], F32, kind="ExternalInput").ap()
    b_adaT = dtn("b_adaT", [128, 96], F32, kind="ExternalInput").ap()
    w_na = dtn("w_na", [NH, 128, 32 * 512], F32, kind="ExternalInput").ap()
    w_hg = dtn("w_hg", [NH, 128, 32 * 640], F32, kind="ExternalInput").ap()
    w_4a = dtn("w_4a", [32, 128, 96 * 128], F32, kind="ExternalInput").ap()
    w_4b = dtn("w_4b", [16, 128, 32 * 256], F32, kind="ExternalInput").ap()
    biasT = dtn("biasT", [NH, 128, 5 * 576], F32, kind="ExternalInput").ap()
    lbl = dtn("lbl", [128, NH * 4], F32, kind="ExternalInput").ap()
    nw_in = dtn("nw", [128, 1], F32, kind="ExternalInput").ap()
    lng = dtn("lng", [1, D], F32, kind="ExternalInput").ap()
    lnb = dtn("lnb", [1, D], F32, kind="ExternalInput").ap()
    cmask = dtn("cmask", [128, 4], F32, kind="ExternalInput").ap()
    out = dtn("out", [TLOC, D], F32, kind="ExternalOutput").ap()
    hT_d = dtn("hT_d", [NT, 128, D], BF16, kind="Internal").ap()
    ya_d = dtn("ya_d", [NH, 128, TLOC], BF16, kind="Internal").ap()
    yb_d = dtn("yb_d", [NH, 128, TLOC], BF16, kind="Internal").ap()
    gate_d = dtn("gate_d", [32, 128], F32, kind="Internal").ap()
    mT_d = dtn("mT_d", [4, 128, 32 * 512], BF16, kind="Internal").ap()
    dbg = None
    if stage < 99:
        dbg = dtn("dbg", [128, 8192], F32, kind="ExternalOutput").ap()

    uid = [0]
    nops = [0]

    def nm(s):
        uid[0] += 1
        return "%s_%d" % (s, uid[0])

    with contextlib.ExitStack() as st:
        def sbuf(stack, name, shape, dt):
            return stack.enter_context(nc.sbuf_tensor(nm(name), shape, dt))

        ps0 = st.enter_context(nc.psum_tensor("ps0", [128, 512], F32))
        ps1 = st.enter_context(nc.psum_tensor("ps1", [128, 512], F32))
        ps23 = st.enter_context(nc.psum_tensor("ps23", [128, 1024], F32))
        ps4 = st.enter_context(nc.psum_tensor("ps4", [128, 512], F32))
        ps5 = st.enter_context(nc.psum_tensor("ps5", [128, 512], F32))
        ps6 = st.enter_context(nc.psum_tensor("ps6", [128, 1024], BF16))
        ps7 = st.enter_context(nc.psum_tensor("ps7", [128, 512], F32))
        ps7b = ps7[:].bitcast(BF16)
        ss = SemState(nc, st)

        def finish(P):
            P.barrier()
            P.emit(nc)
            nops[0] += len(P.ops)

        P = Prog(ss)

        ident = sbuf(st, "ident", [128, 128], F32)
        identb = sbuf(st, "identb", [128, 128], BF16)
        onesb = sbuf(st, "onesb", [128, 128], BF16)
        maskF = sbuf(st, "maskF", [64, 64], F32)
        maskB = sbuf(st, "maskB", [64, 64], F32)
        rmask = sbuf(st, "rmask", [128, 256], F32)
        modT = sbuf(st, "modT", [128, 192], F32)
        s1T = sbuf(st, "s1T", [128, 64], F32)
        shT = sbuf(st, "shT", [128, 64], F32)
        lbs = sbuf(st, "lbs", [128, NH * 2], F32)
        oml = sbuf(st, "oml", [128, NH * 2], F32)
        noml = sbuf(st, "noml", [128, NH * 2], F32)
        lbl_sb = sbuf(st, "lbl_sb", [128, NH * 4], F32)
        nw = sbuf(st, "nw_sb", [128, 1], F32)
        cm = sbuf(st, "cm", [128, 4], F32)

        P.op(POOL, lambda e: e.memset(ident[:], 0.0), writes=["ident"])
        P.op(POOL, lambda e: e.affine_select(out=ident[:], in_=ident[:], pattern=[[-1, 128]], compare_op=ALU.not_equal,
                                             fill=1.0, base=0, channel_multiplier=1), writes=["ident"])
        P.op(POOL, lambda e: e.tensor_copy(out=identb[:], in_=ident[:]), reads=["ident"], writes=["identb"])
        P.op(POOL, lambda e: e.memset(onesb[:], 1.0), writes=["onesb"])
        P.op(POOL, lambda e: e.memset(maskF[:], 1.0), writes=["maskF"])
        P.op(POOL, lambda e: e.affine_select(out=maskF[:], in_=maskF[:], pattern=[[1, 64]], compare_op=ALU.is_ge,
                                             fill=0.0, base=0, channel_multiplier=-1), writes=["maskF"])
        P.op(POOL, lambda e: e.memset(maskB[:], 1.0), writes=["maskB"])
        P.op(POOL, lambda e: e.affine_select(out=maskB[:], in_=maskB[:], pattern=[[-1, 64]], compare_op=ALU.is_ge,
                                             fill=0.0, base=0, channel_multiplier=1), writes=["maskB"])
        P.op(POOL, lambda e: e.memset(rmask[:], 1.0), writes=["rmask"])
        P.op(POOL, lambda e: e.memset(rmask[:].rearrange("p (n c) -> p n c", c=64)[:, :, 0:1], 0.0), writes=["rmask"])
        P.op(SP, lambda e: e.dma_start(out=lbl_sb[:], in_=lbl), writes=["lbl"], dma=True)
        P.op(SP, lambda e: e.dma_start(out=nw[:], in_=nw_in), writes=["nw"], dma=True)
        P.op(SP, lambda e: e.dma_start(out=cm[:], in_=cmask), writes=["cm"], dma=True)
        lv = lbl_sb[:].rearrange("p (a s) -> p a s", s=2)
        P.op(DVE, lambda e: e.tensor_tensor(out=oml[:], in0=lv[:, :, 0], in1=lv[:, :, 1], op=ALU.subtract), reads=["lbl"], writes=["oml"])
        P.op(ACT, lambda e: e.activation(out=lbs[:], in_=oml[:], func=AF.Sigmoid), reads=["oml"], writes=["lbs"])
        P.op(DVE, lambda e: e.tensor_scalar(out=oml[:], in0=lbs[:], scalar1=-1.0, scalar2=1.0, op0=ALU.mult, op1=ALU.add),
             reads=["lbs"], writes=["oml"])
        P.op(DVE, lambda e: e.tensor_scalar(out=noml[:], in0=lbs[:], scalar1=1.0, scalar2=-1.0, op0=ALU.mult, op1=ALU.add),
             reads=["lbs"], writes=["noml"])

        finish(P)

        with contextlib.ExitStack() as ph:
            P = Prog(ss)
            cT = sbuf(ph, "cT", [128, 64], F32)
            sc = sbuf(ph, "sc", [128, 64], F32)
            bT = sbuf(ph, "bT", [128, 96], F32)
            wa = [sbuf(ph, "wa%d" % i, [128, 32, 512], F32) for i in range(2)]
            mod2 = sbuf(ph, "mod2", [2, 3 * D], F32)
            P.op(SP, lambda e: e.dma_start(out=cT[:], in_=condT), writes=["cT"], dma=True)
            P.op(SP, lambda e: e.dma_start(out=bT[:], in_=b_adaT), writes=["bT"], dma=True)
            P.op(ACT, lambda e: e.activation(out=sc[:], in_=cT[:], func=AF.Silu), reads=["cT"], writes=["sc"])
            scv = sc[:].rearrange("p (v j) -> p j v", v=2)
            psm = Rot([(ps1, "ps1"), (ps4, "ps4"), (ps5, "ps5")])
            for n in range(24):
                wb, key = wa[n % 2], "wa%d" % (n % 2)
                P.op(SP, lambda e, wb=wb, n=n: e.dma_start(out=wb[:], in_=w_ada[:, n * 512:(n + 1) * 512].rearrange("(c p) n -> p c n", p=128)),
                     writes=[key], dma=True)
                pt, pk = psm.next()
                for c in range(32):
                    P.op(PE, lambda e, pt=pt, wb=wb, c=c: e.matmul(pt[0:2, 0:512], lhsT=scv[:, c, :], rhs=wb[:, c, :], start=(c == 0), stop=(c == 31)),
                         reads=[key, "sc"], writes=[pk])
                P.op(DVE, lambda e, pt=pt, n=n: e.tensor_copy(out=mod2[:, n * 512:(n + 1) * 512], in_=pt[0:2, 0:512]), reads=[pk], writes=["mod2"])
            for jb in range(96):
                P.op(PE, lambda e, jb=jb: e.transpose(out=ps0[:, jb * 2:jb * 2 + 2], in_=mod2[0:2, jb * 128:(jb + 1) * 128], identity=ident[0:2, 0:2]),
                     reads=["mod2", "ident"], writes=["ps0"])
            P.op(DVE, lambda e: e.tensor_tensor(out=modT[:].rearrange("p (j v) -> p j v", v=2), in0=ps0[:, 0:192].rearrange("p (j v) -> p j v", v=2),
                                                in1=bT[:].unsqueeze(2).to_broadcast([128, 96, 2]), op=ALU.add),
                 reads=["ps0", "bT"], writes=["modT"])
            P.op(DVE, lambda e: e.tensor_copy(out=shT[:], in_=modT[:, 0:64]), reads=["modT"], writes=["shT"])
            P.op(DVE, lambda e: e.tensor_scalar(out=s1T[:], in0=modT[:, 64:128], scalar1=1.0, scalar2=None, op0=ALU.add),
                 reads=["modT"], writes=["s1T"])
            gsb = sbuf(ph, "gsb", [128, 32], F32)
            P.op(DVE, lambda e: e.tensor_copy(out=gsb[:], in_=modT[:].rearrange("p (j v) -> p j v", v=2)[:, 64:96, 0]),
                 reads=["modT"], writes=["gsb"])
            P.op(SP, lambda e: e.dma_start(out=gate_d.rearrange("j p -> p j"), in_=gsb[:], allow_slow_non_contiguous=True),
                 reads=["gsb"], writes=["gate_d"], dma=True)
            if stage == 0:
                P.op(SP, lambda e: e.dma_start(out=dbg[:, 0:192], in_=modT[:]), reads=["modT"], writes=["dbg"], dma=True)
            finish(P)
        s1v = s1T[:].rearrange("p (c v) -> p c v", v=2)
        shv = shT[:].rearrange("p (c v) -> p c v", v=2)

        if stage >= 1:
            with contextlib.ExitStack() as ph:
                P = Prog(ss)
                xt = [sbuf(ph, "xt%d" % i, [128, D], F32) for i in range(3)]
                ht = [sbuf(ph, "ht%d" % i, [128, D], BF16) for i in range(3)]
                stt = [sbuf(ph, "stt%d" % i, [128, 8, 6], F32) for i in range(3)]
                mv = [sbuf(ph, "mv%d" % i, [128, 4], F32) for i in range(3)]
                psr = Rot([(ps0, "ps0"), (ps1, "ps1"), (ps4, "ps4"), (ps5, "ps5"), (ps7, "ps7")])
                for t in range(NT):
                    b = t % 3
                    X, H, S_, M = xt[b], ht[b], stt[b], mv[b]
                    kx, kh, ks, km = "xt%d" % b, "ht%d" % b, "stt%d" % b, "mv%d" % b
                    v = 1 if t >= 20 else 0
                    P.op(SP, lambda e, X=X, t=t: e.dma_start(out=X[:], in_=xe[t * 128:(t + 1) * 128, :]), writes=[kx], dma=True)
                    for g in range(8):
                        P.op(DVE, lambda e, X=X, S_=S_, g=g: e.bn_stats(out=S_[:, g, :], in_=X[:, g * 512:(g + 1) * 512]), reads=[kx], writes=[ks])
                    P.op(DVE, lambda e, S_=S_, M=M: e.bn_aggr(out=M[:, 0:2], in_=S_[:]), reads=[ks], writes=[km])
                    P.op(ACT, lambda e, M=M: e.activation(out=M[:, 1:2], in_=M[:, 1:2], func=AF.Sqrt, bias=1e-6, scale=1.0), writes=[km])
                    P.op(DVE, lambda e, M=M: e.reciprocal(out=M[:, 1:2], in_=M[:, 1:2]), writes=[km])
                    P.op(DVE, lambda e, M=M: e.tensor_scalar(out=M[:, 2:3], in0=M[:, 0:1], scalar1=M[:, 1:2], scalar2=-1.0, op0=ALU.mult, op1=ALU.mult),
                         writes=[km])
                    P.op(ACT, lambda e, X=X, M=M: e.activation(out=X[:], in_=X[:], func=AF.Identity, scale=M[:, 1:2], bias=M[:, 2:3]),
                         reads=[km], writes=[kx])
                    for g in range(8):
                        pt, pk = psr.next()
                        for k in range(4):
                            c = g * 4 + k
                            P.op(PE, lambda e, pt=pt, X=X, k=k, c=c: e.transpose(out=pt[:, k * 128:(k + 1) * 128], in_=X[:, c * 128:(c + 1) * 128],
                                                                                 identity=ident[:]), reads=[kx, "ident"], writes=[pk])
                        for k in range(4):
                            c = g * 4 + k
                            if k == 0:
                                P.op(ACT, lambda e, pt=pt, H=H, k=k, c=c, v=v: e.activation(
                                    out=H[:, c * 128:(c + 1) * 128], in_=pt[:, k * 128:(k + 1) * 128], func=AF.Identity,
                                    scale=s1v[:, c, v:v + 1], bias=shv[:, c, v:v + 1]), reads=[pk, "s1T", "shT"], writes=[kh])
                            else:
                                P.op(DVE, lambda e, pt=pt, H=H, k=k, c=c, v=v: e.tensor_scalar(
                                    out=H[:, c * 128:(c + 1) * 128], in0=pt[:, k * 128:(k + 1) * 128],
                                    scalar1=s1v[:, c, v:v + 1], scalar2=shv[:, c, v:v + 1], op0=ALU.mult, op1=ALU.add),
                                    reads=[pk, "s1T", "shT"], writes=[kh])
                    P.op(POOL, lambda e, H=H, t=t: e.dma_start(out=hT_d[t], in_=H[:]), reads=[kh], writes=["hT_d%d" % t], dma=True)
                    if stage == 1 and t in (2, 20):
                        dt_ = sbuf(ph, "dbgt", [128, 1024], F32)
                        P.op(POOL, lambda e, H=H, dt_=dt_: e.tensor_copy(out=dt_[:], in_=H[:, 0:1024]), reads=[kh], writes=["dd%d" % t])
                        o0 = 0 if t == 2 else 1024
                        P.op(SP, lambda e, dt_=dt_, o0=o0: e.dma_start(out=dbg[:, o0:o0 + 1024], in_=dt_[:]), reads=["dd%d" % t], writes=["dbg"], dma=True)
                finish(P)

        def load_hT_block(bi, buf, key):
            P.op(SP, lambda e: e.dma_start(out=buf[:].rearrange("p (t d) -> p t d", t=2), in_=hT_d[2 * bi:2 * bi + 2].rearrange("t p d -> p t d")),
                 reads=["hT_d%d" % (2 * bi), "hT_d%d" % (2 * bi + 1)], writes=[key], dma=True)

        def proj_fm(pst, pkey, W, wkey, col0, hb, hkey):
            hv = hb[:].rearrange("p (t d) -> p t d", t=2)
            for c in range(32):
                P.op(PE, lambda e, c=c: e.matmul(pst[:, 0:256], lhsT=W[:, c, col0:col0 + 128], rhs=hv[:, :, c * 128:(c + 1) * 128],
                                                 start=(c == 0), stop=(c == 31)), reads=[wkey, hkey], writes=[pkey])

        if stage >= 2 and 2 in phases:
            with contextlib.ExitStack() as ph:
                P = Prog(ss)
                Wn = [sbuf(ph, "Wn%d" % i, [128, 32, 512], BF16) for i in range(2)]
                bt = [sbuf(ph, "bt%d" % i, [128, 5, 576], BF16) for i in range(3)]
                hb = [sbuf(ph, "hb%d" % i, [128, 2 * D], BF16) for i in range(2)]
                qTs = [sbuf(ph, "qT%d" % i, [128, TLOC], BF16) for i in range(2)]
                kTs = [sbuf(ph, "kT%d" % i, [128, NT * 128], BF16) for i in range(2)]
                szTs = [sbuf(ph, "szT%d" % i, [128, TLOC], BF16) for i in range(2)]
                vsbs = [sbuf(ph, "vsb%d" % i, [128, NT, 128], BF16) for i in range(2)]
                yT = [sbuf(ph, "yT%d" % i, [128, TLOC], BF16) for i in range(2)]
                mx = sbuf(ph, "mx", [128, 2], F32)
                rs = sbuf(ph, "rs", [128, 2], F32)
                pf = sbuf(ph, "pf", [128, 832], F32)
                pb = sbuf(ph, "pb", [128, 832], BF16)
                pT = sbuf(ph, "pT", [128, 896], BF16)
                psr = Rot([(ps0, "ps0"), (ps1, "ps1"), (ps7, "ps7")])
                nheads = NH if stage > 3 else 2
                blkc = [0]

                def load_w(h):
                    W, B = Wn[h % 2], bt[h % 3]
                    P.op(POOL, lambda e: e.dma_start(out=W[:].rearrange("p c n -> p (c n)"), in_=w_na[h]), writes=["Wn%d" % (h % 2)], dma=True)
                    P.op(POOL, lambda e: e.dma_start(out=B[:].rearrange("p v k -> p (v k)"), in_=biasT[h]), writes=["bt%d" % (h % 3)], dma=True)

                def proj_gen(h):
                    par = h % 2
                    W, wkey = Wn[par], "Wn%d" % par
                    qT, kT, szT, vsb = qTs[par], kTs[par], szTs[par], vsbs[par]
                    qk, kk_, zk, vk = "qT%d" % par, "kT%d" % par, "szT%d" % par, "vsb%d" % par
                    for bi in range(NBLK):
                        Hb, hkey = hb[blkc[0] % 2], "hb%d" % (blkc[0] % 2)
                        blkc[0] += 1
                        load_hT_block(bi, Hb, hkey)
                        local = 1 <= bi <= 8
                        lo = (bi - 1) * 256
                        if local:
                            pt, pk = psr.next()
                            proj_fm(pt, pk, W, wkey, 0, Hb, hkey)
                            P.op(ACT, lambda e, pt=pt, lo=lo: e.activation(out=qT[:, lo:lo + 256], in_=pt[:, 0:256], func=AF.Copy, scale=128.0 ** -0.5),
                                 reads=[pk], writes=[qk])
                            yield
                            pt, pk = psr.next()
                            proj_fm(pt, pk, W, wkey, 384, Hb, hkey)
                            P.op(ACT, lambda e, pt=pt, lo=lo: e.activation(out=szT[:, lo:lo + 256], in_=pt[:, 0:256], func=AF.Silu),
                                 reads=[pk], writes=[zk])
                            yield
                        pt, pk = psr.next()
                        proj_fm(pt, pk, W, wkey, 128, Hb, hkey)
                        P.op(DVE, lambda e, pt=pt, bi=bi: e.tensor_copy(out=kT[:, bi * 256:(bi + 1) * 256], in_=pt[:, 0:256]), reads=[pk], writes=[kk_])
                        yield
                        hv = Hb[:].rearrange("p (t d) -> p t d", t=2)
                        for tl in range(2):
                            pt, pk = psr.next()
                            for c in range(32):
                                P.op(PE, lambda e, pt=pt, hv=hv, tl=tl, c=c: e.matmul(
                                    pt[:, 0:128], lhsT=hv[:, tl, c * 128:(c + 1) * 128], rhs=W[:, c, 256:384], start=(c == 0), stop=(c == 31)),
                                    reads=[wkey, hkey], writes=[pk])
                            P.op(DVE, lambda e, pt=pt, bi=bi, tl=tl: e.tensor_copy(out=vsb[:, 2 * bi + tl, :], in_=pt[:, 0:128]), reads=[pk], writes=[vk])
                            yield

                def attn_gen(h):
                    par = h % 2
                    B, bkey = bt[h % 3], "bt%d" % (h % 3)
                    Y, ykey = yT[par], "yT%d" % par
                    qT, kT, szT, vsb = qTs[par], kTs[par], szTs[par], vsbs[par]
                    qk, kk_, zk, vk = "qT%d" % par, "kT%d" % par, "szT%d" % par, "vsb%d" % par

                    def scores(pr):
                        var = {0: 0, 1: 1, 14: 3, 15: 4}.get(pr, 2)
                        q0 = pr * 128
                        P.op(PE, lambda e: e.matmul(ps23[:, 0:512], lhsT=qT[:, q0:q0 + 128], rhs=kT[:, q0:q0 + 512], start=True, stop=False),
                             reads=[qk, kk_], writes=["ps23"])
                        P.op(PE, lambda e: e.matmul(ps23[:, 0:512], lhsT=identb[:], rhs=B[:, var, 0:512], start=False, stop=True),
                             reads=[bkey, "identb"], writes=["ps23"])
                        P.op(PE, lambda e: e.matmul(ps23[:, 512:576], lhsT=qT[:, q0:q0 + 128], rhs=kT[:, q0 + 512:q0 + 576], start=True, stop=False),
                             reads=[qk, kk_], writes=["ps23"])
                        P.op(PE, lambda e: e.matmul(ps23[:, 512:576], lhsT=identb[:], rhs=B[:, var, 512:576], start=False, stop=True),
                             reads=[bkey, "identb"], writes=["ps23"])
                        P.op(PE, lambda e: e.matmul(ps23[:, 576:832], lhsT=qT[:, q0:q0 + 128], rhs=kT[:, 2560:2816], start=True, stop=True),
                             reads=[qk, kk_], writes=["ps23"])

                    def chain(pr):
                        P.op(DVE, lambda e: e.reduce_max(out=mx[:, 0:1], in_=ps23[:, 0:832], axis=AX.X), reads=["ps23"], writes=["mx"])
                        P.op(DVE, lambda e: e.tensor_scalar(out=mx[:, 1:2], in0=mx[:, 0:1], scalar1=-1.0, scalar2=None, op0=ALU.mult), writes=["mx"])
                        P.op(ACT, lambda e: e.activation(out=pf[:], in_=ps23[:, 0:832], func=AF.Exp, bias=mx[:, 1:2], scale=1.0, accum_out=rs[:, 0:1]),
                             reads=["ps23", "mx"], writes=["pf", "rs"])
                        P.op(DVE, lambda e: e.reciprocal(out=rs[:, 1:2], in_=rs[:, 0:1]), writes=["rs"])
                        P.op(DVE, lambda e: e.tensor_scalar(out=pb[:], in0=pf[:], scalar1=rs[:, 1:2], scalar2=None, op0=ALU.mult),
                             reads=["pf", "rs"], writes=["pb"])

                    def transposes(pr):
                        for j in range(7):
                            if j < 4:
                                src, dst = pb[:, j * 128:(j + 1) * 128], ps6[:, j * 128:(j + 1) * 128]
                            elif j == 4:
                                src, dst = pb[:, 512:576], ps6[0:64, 512:640]
                            else:
                                src, dst = pb[:, 576 + (j - 5) * 128:576 + (j - 4) * 128], ps6[:, 640 + (j - 5) * 128:640 + (j - 4) * 128]
                            P.op(PE, lambda e, src=src, dst=dst: e.transpose(out=dst, in_=src, identity=identb[:]), reads=["pb", "identb"], writes=["ps6"])
                        P.op(ACT, lambda e: e.copy(out=pT[:, 0:512], in_=ps6[:, 0:512]), reads=["ps6"], writes=["pT"])
                        P.op(DVE, lambda e: e.tensor_copy(out=pT[0:64, 512:640], in_=ps6[0:64, 512:640]), reads=["ps6"], writes=["pT"])
                        P.op(DVE, lambda e: e.tensor_copy(out=pT[:, 640:896], in_=ps6[:, 640:896]), reads=["ps6"], writes=["pT"])

                    def pv(pr):
                        q0 = pr * 128
                        for j in range(7):
                            if j < 4:
                                lt, rh = vsb[:, pr + j, :], pT[:, j * 128:(j + 1) * 128]
                            elif j == 4:
                                lt, rh = vsb[0:64, pr + 4, :], pT[0:64, 512:640]
                            else:
                                lt, rh = vsb[:, 20 + (j - 5), :], pT[:, 640 + (j - 5) * 128:640 + (j - 4) * 128]
                            P.op(PE, lambda e, lt=lt, rh=rh, j=j: e.matmul(ps4[:, 0:128], lhsT=lt, rhs=rh, start=(j == 0), stop=(j == 6)),
                                 reads=[vk, "pT"], writes=["ps4"])
                        P.op(DVE, lambda e: e.tensor_tensor(out=Y[:, q0:q0 + 128], in0=ps4[:, 0:128], in1=szT[:, q0:q0 + 128], op=ALU.mult),
                             reads=["ps4", zk], writes=[ykey])

                    scores(0)
                    chain(0)
                    yield
                    for pr in range(16):
                        if pr + 1 < 16:
                            scores(pr + 1)
                            yield
                        transposes(pr)
                        if pr + 1 < 16:
                            chain(pr + 1)
                        yield
                        pv(pr)
                        yield
                    P.op(ACT, lambda e: e.dma_start(out=ya_d[h], in_=Y[:]), reads=[ykey], writes=["ya_d%d" % h], dma=True)
                    if stage in (2, 3) and h == 0:
                        dt_ = sbuf(ph, "dbgt", [128, 2048], F32)
                        P.op(POOL, lambda e: e.tensor_copy(out=dt_[:], in_=Y[:]), reads=[ykey], writes=["dd"])
                        P.op(SP, lambda e: e.dma_start(out=dbg[:, 0:2048], in_=dt_[:]), reads=["dd"], writes=["dbg"], dma=True)

                load_w(0)
                for it in range(nheads + 1):
                    if it + 1 < nheads:
                        load_w(it + 1)
                    gens = []
                    if it < nheads:
                        gens.append(proj_gen(it))
                    if it >= 1:
                        gens.append(attn_gen(it - 1))
                    while gens:
                        for g_ in list(gens):
                            try:
                                next(g_)
                            except StopIteration:
                                gens.remove(g_)
                finish(P)

        if stage >= 3 and 3 in phases:
            with contextlib.ExitStack() as ph:
                P = Prog(ss)
                Wh = sbuf(ph, "Wh", [128, 32, 640], BF16)
                hb = [sbuf(ph, "hb%d" % i, [128, 2 * D], BF16) for i in range(2)]
                sgT = sbuf(ph, "sgT", [128, TLOC], BF16)
                qd = [sbuf(ph, "qd%d" % i, [128, TLOC], BF16) for i in range(2)]
                kd = [sbuf(ph, "kd%d" % i, [128, NT * 128], BF16) for i in range(2)]
                kdl = [sbuf(ph, "kdl%d" % i, [128, NT * 128], BF16) for i in range(2)]
                el = [sbuf(ph, "el%d" % i, [128, 44], F32) for i in range(2)]
                eref = [sbuf(ph, "eref%d" % i, [128, 44], F32) for i in range(2)]
                t_sq2 = [sbuf(ph, "t_sq%d" % i, [128, 256], F32) for i in range(2)]
                t_sig2 = [[sbuf(ph, "t_sg%d%d" % (d_, i), [128, 256], F32) for i in range(2)] for d_ in range(2)]
                elr = [sbuf(ph, "elr%d" % i, [128, 44], F32) for i in range(2)]
                vh = sbuf(ph, "vh", [64, 44, 128], BF16)
                Sbf = [sbuf(ph, "Sbf%d" % i, [128, 33, 128], BF16) for i in range(2)]
                Sst = {n: sbuf(ph, "S" + n, [128, 128], F32) for n in ("f", "b", "cf", "tf", "cb", "bb", "tmp0", "tmp1")}
                tmp = {n: [sbuf(ph, "t_%s%d" % (n, i), [128, 256], F32) for i in range(2)] for n in ("lf", "kk", "cum", "A", "E1", "E2")}
                kdTg = [sbuf(ph, "kdTg%d" % i, [64, 512], BF16) for i in range(2)]
                aTs = [sbuf(ph, "aT%d" % i, [64, 64], BF16) for i in range(4)]
                sqs = sbuf(ph, "sqs", [128, 512], BF16)
                t_r = sbuf(ph, "t_r", [128, 512], F32)
                t_o = sbuf(ph, "t_o", [128, 512], F32)
                ybT = [sbuf(ph, "ybT%d" % i, [128, TLOC], BF16) for i in range(2)]
                psr = Rot([(ps0, "ps0"), (ps1, "ps1"), (ps7, "ps7"), (ps4, "ps4"), (ps5, "ps5")])
                nheads = NH if stage > 3 else 1
                for h in range(nheads):
                    Y, ykey = ybT[h % 2], "ybT%d" % (h % 2)
                    P.op(POOL, lambda e, h=h: e.dma_start(out=Wh[:].rearrange("p c n -> p (c n)"), in_=w_hg[h]), writes=["Wh"], dma=True)

                    def chain_evac(pt, pk, d, bi):
                        P.op(ACT, lambda e: e.activation(out=t_sig2[d][bi % 2][:], in_=pt[:, 0:256], func=AF.Sigmoid), reads=[pk], writes=["t_sg%d%d" % (d, bi % 2)])

                    def chain(d, bi, local):
                        T = {n: tmp[n][d] for n in tmp}
                        K = {n: "t_%s%d" % (n, d) for n in tmp}
                        T["sig"] = t_sig2[d][bi % 2]
                        K["sig"] = "t_sg%d%d" % (d, bi % 2)
                        t_sq, tsqk = t_sq2[bi % 2], "t_sq%d" % (bi % 2)
                        hd = h * 2 + d
                        P.op(ACT, lambda e: e.activation(out=T["lf"][:], in_=T["sig"][:], func=AF.Ln, scale=oml[:, hd:hd + 1], bias=lbs[:, hd:hd + 1]),
                             reads=[K["sig"], "oml", "lbs"], writes=[K["lf"]])
                        P.op(DVE, lambda e: e.tensor_scalar(out=T["kk"][:], in0=T["sig"][:], scalar1=noml[:, hd:hd + 1], scalar2=oml[:, hd:hd + 1],
                                                            op0=ALU.mult, op1=ALU.add), reads=[K["sig"], "oml", "noml"], writes=[K["kk"]])
                        P.op(DVE, lambda e: e.tensor_tensor_scan(out=T["cum"][:], data0=rmask[:], data1=T["lf"][:], initial=0.0, op0=ALU.mult, op1=ALU.add),
                             reads=[K["lf"], "rmask"], writes=[K["cum"]])
                        c3 = T["cum"][:].rearrange("p (n c) -> p n c", c=64)
                        l3 = T["lf"][:].rearrange("p (n c) -> p n c", c=64)
                        a3 = T["A"][:].rearrange("p (n c) -> p n c", c=64)
                        if d == 0:
                            G, Gk = T["cum"], K["cum"]
                            g3 = c3
                            refpos, lastpos = 31, 63
                        else:
                            P.op(DVE, lambda e: e.tensor_tensor(out=T["lf"][:], in0=T["cum"][:], in1=T["lf"][:], op=ALU.subtract),
                                 reads=[K["cum"]], writes=[K["lf"]])
                            P.op(DVE, lambda e: e.tensor_tensor(out=l3, in0=c3[:, :, 63:64].to_broadcast([128, 4, 64]), in1=l3, op=ALU.subtract),
                                 reads=[K["cum"]], writes=[K["lf"]])
                            G, Gk = T["lf"], K["lf"]
                            g3 = l3
                            refpos, lastpos = 32, 0
                        P.op(DVE, lambda e: e.tensor_tensor(out=a3, in0=g3, in1=g3[:, :, refpos:refpos + 1].to_broadcast([128, 4, 64]), op=ALU.subtract),
                             reads=[Gk], writes=[K["A"]])
                        P.op(ACT, lambda e: e.activation(out=T["E2"][:], in_=T["A"][:], func=AF.Exp, scale=-1.0), reads=[K["A"]], writes=[K["E2"]])
                        P.op(ACT, lambda e: e.activation(out=T["E1"][:], in_=T["A"][:], func=AF.Exp), reads=[K["A"]], writes=[K["E1"]])
                        e13 = T["E1"][:].rearrange("p (n c) -> p n c", c=64)
                        P.op(DVE, lambda e: e.tensor_tensor(out=T["kk"][:], in0=T["kk"][:], in1=T["E2"][:], op=ALU.mult),
                             reads=[K["E2"]], writes=[K["kk"]])
                        if local:
                            P.op(POOL, lambda e: e.tensor_copy(out=kd[d][:, bi * 256:(bi + 1) * 256], in_=T["kk"][:]), reads=[K["kk"]], writes=["kd%d" % d])
                        P.op(DVE, lambda e: e.tensor_tensor(out=kdl[d][:, bi * 256:(bi + 1) * 256].rearrange("p (n c) -> p n c", c=64),
                                                            in0=T["kk"][:].rearrange("p (n c) -> p n c", c=64),
                                                            in1=e13[:, :, lastpos:lastpos + 1].to_broadcast([128, 4, 64]), op=ALU.mult),
                             reads=[K["kk"], K["E1"]], writes=["kdl%d" % d])
                        P.op(ACT, lambda e: e.activation(out=el[d][:, bi * 4:bi * 4 + 4], in_=g3[:, :, lastpos], func=AF.Exp), reads=[Gk], writes=["el%d" % d])
                        if local:
                            P.op(ACT, lambda e: e.activation(out=eref[d][:, bi * 4:bi * 4 + 4], in_=g3[:, :, refpos], func=AF.Exp), reads=[Gk], writes=["eref%d" % d])
                            lo = (bi - 1) * 256
                            P.op(DVE, lambda e: e.tensor_tensor(out=qd[d][:, lo:lo + 256], in0=t_sq[:], in1=T["E1"][:], op=ALU.mult),
                                 reads=[tsqk, K["E1"]], writes=["qd%d" % d])

                    for bi in range(NBLK):
                        Hb, hkey = hb[bi % 2], "hb%d" % (bi % 2)
                        load_hT_block(bi, Hb, hkey)
                        local = 1 <= bi <= 8
                        lo = (bi - 1) * 256
                        if local:
                            pt, pk = psr.next()
                            proj_fm(pt, pk, Wh, "Wh", 0, Hb, hkey)
                            P.op(ACT, lambda e, pt=pt, bi=bi: e.activation(out=t_sq2[bi % 2][:], in_=pt[:, 0:256], func=AF.Silu), reads=[pk], writes=["t_sq%d" % (bi % 2)])
                            pt, pk = psr.next()
                            proj_fm(pt, pk, Wh, "Wh", 512, Hb, hkey)
                            P.op(ACT, lambda e, pt=pt, lo=lo: e.activation(out=sgT[:, lo:lo + 256], in_=pt[:, 0:256], func=AF.Silu), reads=[pk], writes=["sgT"])
                        if bi != 9:
                            pt, pk = psr.next()
                            proj_fm(pt, pk, Wh, "Wh", 128, Hb, hkey)
                            chain_evac(pt, pk, 0, bi)
                        if bi != 0:
                            pt, pk = psr.next()
                            proj_fm(pt, pk, Wh, "Wh", 256, Hb, hkey)
                            chain_evac(pt, pk, 1, bi)
                        hv = Hb[:].rearrange("p (t d) -> p t d", t=2)
                        for ch in range(4):
                            pt, pk = psr.next()
                            tl, off = ch // 2, (ch % 2) * 64
                            for c in range(32):
                                P.op(PE, lambda e, pt=pt, hv=hv, tl=tl, off=off, c=c: e.matmul(
                                    pt[0:64, 0:128], lhsT=hv[:, tl, c * 128 + off:c * 128 + off + 64], rhs=Wh[:, c, 384:512],
                                    start=(c == 0), stop=(c == 31)), reads=["Wh", hkey], writes=[pk])
                            P.op(DVE, lambda e, pt=pt, bi=bi, ch=ch: e.tensor_copy(out=vh[:, bi * 4 + ch, :], in_=pt[0:64, 0:128]), reads=[pk], writes=["vh"])
                        if bi != 9:
                            chain(0, bi, local)
                        if bi != 0:
                            chain(1, bi, local)

                    tasks = []
                    tasks.append((0, [40, 41, 42, 43], [(Sst["cf"], "Scf", Sst["cf"], "Scf", j == 0, None) for j in range(4)]))
                    tasks.append((1, [43, 42, 41, 40], [(Sst["cb"], "Scb", Sst["cb"], "Scb", j == 0, None) for j in range(4)]))
                    tasks.append((0, [0, 1, 2, 3], [(Sst["tf"], "Stf", Sst["tf"], "Stf", j == 0, None) for j in range(4)]))
                    tasks.append((1, [39, 38, 37, 36], [(Sst["bb"], "Sbb", Sst["bb"], "Sbb", j == 0, None) for j in range(4)]))
                    tasks.append("blend")
                    fpp = [(Sst["f"], "Sf"), (Sst["tmp0"], "Stmp0")]
                    bpp = [(Sst["b"], "Sb"), (Sst["tmp1"], "Stmp1")]
                    for q0_ in range(0, 31, 4):
                        ns = list(range(q0_, min(q0_ + 4, 31)))
                        tasks.append((0, [4 + n for n in ns], [(fpp[n % 2][0], fpp[n % 2][1], fpp[(n + 1) % 2][0], fpp[(n + 1) % 2][1], False, n + 1) for n in ns]))
                        tasks.append((1, [35 - n for n in ns], [(bpp[n % 2][0], bpp[n % 2][1], bpp[(n + 1) % 2][0], bpp[(n + 1) % 2][1], False, 30 - n) for n in ns]))
                    tcount = [0]
                    mcount = [0]

                    def prep(task, k):
                        d, chunks, _ = task
                        bank, bkey = (ps6[:, 0:512], "ps6") if k % 2 == 0 else (ps7b, "ps7")
                        for j, g in enumerate(chunks):
                            P.op(PE, lambda e, g=g, j=j: e.transpose(out=bank[0:64, j * 128:(j + 1) * 128], in_=kdl[d][:, g * 64:(g + 1) * 64],
                                                                 identity=identb[:]), reads=["kdl%d" % d, "identb"], writes=[bkey])
                        w = len(chunks) * 128
                        P.op(ACT, lambda e: e.copy(out=kdTg[k % 2][:, 0:w], in_=bank[0:64, 0:w]), reads=[bkey], writes=["kdTg%d" % (k % 2)])

                    def consume(task, k):
                        d, chunks, steps = task
                        for j, g in enumerate(chunks):
                            S_in, k_in, S_out, k_out, first, store = steps[j]
                            i = mcount[0] % 2
                            mcount[0] += 1
                            pu, pukey = (ps4, "ps4") if i == 0 else (ps5, "ps5")
                            P.op(PE, lambda e, pu=pu, j=j, g=g: e.matmul(pu[:, 0:128], lhsT=kdTg[k % 2][:, j * 128:(j + 1) * 128], rhs=vh[:, g, :], start=True, stop=True),
                                 reads=["kdTg%d" % (k % 2), "vh"], writes=[pukey])
                            if first:
                                P.op(DVE, lambda e, pu=pu, S_out=S_out: e.tensor_copy(out=S_out[:], in_=pu[:, 0:128]), reads=[pukey], writes=[k_out])
                            else:
                                P.op(DVE, lambda e, pu=pu, S_in=S_in, S_out=S_out, g=g: e.scalar_tensor_tensor(
                                    out=S_out[:], in0=S_in[:], scalar=el[d][:, g:g + 1], in1=pu[:, 0:128], op0=ALU.mult, op1=ALU.add),
                                    reads=[k_in, "el%d" % d, pukey], writes=[k_out])
                            if store is not None:
                                gc = 4 + store
                                if d == 0:
                                    P.op(ACT, lambda e, S_out=S_out, store=store, gc=gc: e.activation(out=Sbf[d][:, store, :], in_=S_out[:], func=AF.Copy,
                                                                                                     scale=eref[d][:, gc:gc + 1]),
                                         reads=[k_out, "eref%d" % d], writes=["Sbf%d" % d])
                                else:
                                    P.op(DVE, lambda e, S_out=S_out, store=store, gc=gc: e.tensor_scalar(out=Sbf[d][:, store, :], in0=S_out[:],
                                                                                                        scalar1=eref[d][:, gc:gc + 1], scalar2=None, op0=ALU.mult),
                                         reads=[k_out, "eref%d" % d], writes=["Sbf%d" % d])

                    def blend():
                        P.op(DVE, lambda e: e.tensor_scalar(out=Sst["f"][:], in0=Sst["cf"][:], scalar1=cm[:, 0:1], scalar2=None, op0=ALU.mult),
                             reads=["Scf", "cm"], writes=["Sf"])
                        P.op(DVE, lambda e: e.scalar_tensor_tensor(out=Sst["f"][:], in0=Sst["tf"][:], scalar=cm[:, 1:2], in1=Sst["f"][:], op0=ALU.mult, op1=ALU.add),
                             reads=["Stf", "cm"], writes=["Sf"])
                        P.op(DVE, lambda e: e.tensor_scalar(out=Sst["b"][:], in0=Sst["cb"][:], scalar1=cm[:, 2:3], scalar2=None, op0=ALU.mult),
                             reads=["Scb", "cm"], writes=["Sb"])
                        P.op(DVE, lambda e: e.scalar_tensor_tensor(out=Sst["b"][:], in0=Sst["bb"][:], scalar=cm[:, 3:4], in1=Sst["b"][:], op0=ALU.mult, op1=ALU.add),
                             reads=["Sbb", "cm"], writes=["Sb"])
                        P.op(ACT, lambda e: e.activation(out=Sbf[0][:, 0, :], in_=Sst["f"][:], func=AF.Copy, scale=eref[0][:, 4:5]),
                             reads=["Sf", "eref0"], writes=["Sbf0"])
                        P.op(ACT, lambda e: e.activation(out=Sbf[1][:, 31, :], in_=Sst["b"][:], func=AF.Copy, scale=eref[1][:, 35:36]),
                             reads=["Sb", "eref1"], writes=["Sbf1"])

                    real = [t for t in tasks if t != "blend"]
                    idx_of = {id(t): i for i, t in enumerate(real)}
                    prep(real[0], 0)
                    for t in tasks:
                        if t == "blend":
                            blend()
                            continue
                        k = idx_of[id(t)]
                        if k + 1 < len(real):
                            prep(real[k + 1], k + 1)
                        consume(t, k)
                    def aT_part(n):
                        g = 4 + n
                        lo = n * 64
                        for d in range(2):
                            i = (n * 2 + d) % 4
                            pa_, pak = ((ps4, "ps4"), (ps5, "ps5"), (ps0, "ps0"), (ps1, "ps1"))[i]
                            P.op(PE, lambda e, pa_=pa_, d=d: e.matmul(pa_[0:64, 0:64], lhsT=kd[d][:, g * 64:(g + 1) * 64], rhs=qd[d][:, lo:lo + 64],
                                                                   start=True, stop=True), reads=["kd%d" % d, "qd%d" % d], writes=[pak])
                            mk = maskF if d == 0 else maskB
                            P.op(DVE, lambda e, pa_=pa_, i=i, mk=mk: e.tensor_tensor(out=aTs[i][:], in0=pa_[0:64, 0:64], in1=mk[:], op=ALU.mult),
                                 reads=[pak, "maskF", "maskB"], writes=["aT%d" % i])

                    def o_part(n, half, hk):
                        g = 4 + n
                        lo = n * 64
                        cn = n % 8
                        oc = half[:, cn * 64:(cn + 1) * 64]
                        a0, a1 = aTs[(n * 2) % 4], aTs[(n * 2 + 1) % 4]
                        k0, k1 = "aT%d" % ((n * 2) % 4), "aT%d" % ((n * 2 + 1) % 4)
                        P.op(PE, lambda e: e.matmul(oc, lhsT=vh[:, g, :], rhs=a0[:], start=True, stop=False), reads=["vh", k0], writes=[hk])
                        P.op(PE, lambda e: e.matmul(oc, lhsT=Sbf[0][:, n, :], rhs=qd[0][:, lo:lo + 64], start=False, stop=False), reads=["Sbf0", "qd0"], writes=[hk])
                        P.op(PE, lambda e: e.matmul(oc, lhsT=vh[:, g, :], rhs=a1[:], start=False, stop=False), reads=["vh", k1], writes=[hk])
                        P.op(PE, lambda e: e.matmul(oc, lhsT=Sbf[1][:, n, :], rhs=qd[1][:, lo:lo + 64], start=False, stop=True), reads=["Sbf1", "qd1"], writes=[hk])

                    aT_part(0)
                    for grp in range(4):
                        half = ps23[:, 0:512] if grp % 2 == 0 else ps23[:, 512:1024]
                        hk = "ps23a" if grp % 2 == 0 else "ps23b"
                        for cn in range(8):
                            n = grp * 8 + cn
                            if n + 1 < 32:
                                aT_part(n + 1)
                            o_part(n, half, hk)
                        g0 = grp * 512
                        P.op(ACT, lambda e, half=half: e.activation(out=sqs[:], in_=half, func=AF.Square), reads=[hk], writes=["sqs"])
                        P.op(PE, lambda e: e.matmul(ps7[:, 0:512], lhsT=onesb[:], rhs=sqs[:], start=True, stop=True), reads=["sqs", "onesb"], writes=["ps7"])
                        P.op(ACT, lambda e: e.activation(out=t_r[:], in_=ps7[:, 0:512], func=AF.Sqrt, scale=1.0 / 128.0, bias=1e-6), reads=["ps7"], writes=["t_r"])
                        P.op(DVE, lambda e: e.reciprocal(out=t_r[:], in_=t_r[:]), writes=["t_r"])
                        P.op(DVE, lambda e, half=half: e.tensor_tensor(out=t_o[:], in0=half, in1=t_r[:], op=ALU.mult), reads=[hk, "t_r"], writes=["t_o"])
                        P.op(DVE, lambda e, Y=Y, g0=g0: e.scalar_tensor_tensor(out=Y[:, g0:g0 + 512], in0=t_o[:], scalar=nw[:, 0:1], in1=sgT[:, g0:g0 + 512],
                                                                              op0=ALU.mult, op1=ALU.mult), reads=["t_o", "nw", "sgT"], writes=[ykey])
                    P.op(ACT, lambda e, Y=Y, h=h: e.dma_start(out=yb_d[h], in_=Y[:]), reads=[ykey], writes=["yb_d%d" % h], dma=True)
                    if stage == 3:
                        dt_ = sbuf(ph, "dbgt", [128, 2048], F32)
                        P.op(POOL, lambda e, Y=Y, dt_=dt_: e.tensor_copy(out=dt_[:], in_=Y[:]), reads=[ykey], writes=["dd"])
                        P.op(SP, lambda e, dt_=dt_: e.dma_start(out=dbg[:, 2048:4096], in_=dt_[:]), reads=["dd"], writes=["dbg"], dma=True)
                        P.op(SP, lambda e: e.dma_start(out=dbg[:, 4096:4224], in_=Sst["cf"][:]), reads=["Scf"], writes=["dbg"], dma=True)
                        P.op(SP, lambda e: e.dma_start(out=dbg[:, 4224:4352], in_=Sst["cb"][:]), reads=["Scb"], writes=["dbg"], dma=True)
                finish(P)

        if stage >= 4:
            ps6f = ps6[:].bitcast(F32)
            banksets = [((ps0[:], "ps0"), (ps1[:], "ps1"), (ps23[:, 0:512], "ps23a"), (ps23[:, 512:1024], "ps23b")),
                        ((ps4[:], "ps4"), (ps5[:], "ps5"), (ps7[:], "ps7"), (ps6f, "ps6"))]
            for pp in range(2):
                with contextlib.ExitStack() as ph:
                    P = Prog(ss)
                    h4s = [sbuf(ph, "h4%d" % i, [128, 4, D], BF16) for i in range(2)]
                    ya4s = [sbuf(ph, "ya4%d" % i, [128, NH, 512], BF16) for i in range(2)]
                    yb4s = [sbuf(ph, "yb4%d" % i, [128, NH, 512], BF16) for i in range(2)]
                    W4 = [sbuf(ph, "W4%d" % i, [128, 96, 128], BF16) for i in range(2)]
                    sgs = [[sbuf(ph, "sg%d%d" % (t2, i), [128, 512], F32) for i in range(2)] for t2 in range(2)]
                    tts = [[sbuf(ph, "tt%d%d" % (t2, i), [128, 512], F32) for i in range(2)] for t2 in range(2)]
                    mss = [sbuf(ph, "ms%d" % i, [128, 512], BF16) for i in range(2)]
                    for t2 in range(2):
                        tb = 2 * pp + t2
                        P.op(SP, lambda e, tb=tb, t2=t2: e.dma_start(out=h4s[t2][:], in_=hT_d[2 + 4 * tb:6 + 4 * tb].rearrange("t p d -> p t d")),
                             writes=["h4%d" % t2], dma=True)
                        P.op(SP, lambda e, tb=tb, t2=t2: e.dma_start(out=ya4s[t2][:], in_=ya_d[:, :, tb * 512:(tb + 1) * 512].rearrange("h p t -> p h t")),
                             writes=["ya4%d" % t2], dma=True)
                        P.op(SP, lambda e, tb=tb, t2=t2: e.dma_start(out=yb4s[t2][:], in_=yb_d[:, :, tb * 512:(tb + 1) * 512].rearrange("h p t -> p h t")),
                             writes=["yb4%d" % t2], dma=True)
                    for jb in range(32):
                        W, wkey = W4[jb % 2], "W4%d" % (jb % 2)
                        P.op(POOL, lambda e, W=W, jb=jb: e.dma_start(out=W[:].rearrange("p c n -> p (c n)"), in_=w_4a[jb]), writes=[wkey], dma=True)
                        for t2 in range(2):
                            tb = 2 * pp + t2
                            (pga, kga), (pgb, kgb), (ppa, kpa), (ppb, kpb) = banksets[t2]
                            h4, ya4, yb4 = h4s[t2], ya4s[t2], yb4s[t2]
                            sg, tt, ms = sgs[t2], tts[t2], mss[t2]
                            for c in range(32):
                                P.op(PE, lambda e, W=W, c=c, pga=pga, h4=h4: e.matmul(pga, lhsT=W[:, c, :], rhs=h4[:, :, c * 128:(c + 1) * 128], start=(c == 0), stop=(c == 31)),
                                     reads=[wkey, "h4%d" % t2], writes=[kga])
                            for c in range(32):
                                P.op(PE, lambda e, W=W, c=c, pgb=pgb, h4=h4: e.matmul(pgb, lhsT=W[:, 32 + c, :], rhs=h4[:, :, c * 128:(c + 1) * 128], start=(c == 0), stop=(c == 31)),
                                     reads=[wkey, "h4%d" % t2], writes=[kgb])
                            for hh in range(NH):
                                P.op(PE, lambda e, W=W, hh=hh, ppa=ppa, ya4=ya4: e.matmul(ppa, lhsT=W[:, 64 + hh, :], rhs=ya4[:, hh, :], start=(hh == 0), stop=(hh == NH - 1)),
                                     reads=[wkey, "ya4%d" % t2], writes=[kpa])
                            for hh in range(NH):
                                P.op(PE, lambda e, W=W, hh=hh, ppb=ppb, yb4=yb4: e.matmul(ppb, lhsT=W[:, 80 + hh, :], rhs=yb4[:, hh, :], start=(hh == 0), stop=(hh == NH - 1)),
                                     reads=[wkey, "yb4%d" % t2], writes=[kpb])
                            P.op(ACT, lambda e, sg=sg, pga=pga: e.activation(out=sg[0][:], in_=pga, func=AF.Sigmoid), reads=[kga], writes=["sg%d0" % t2])
                            P.op(ACT, lambda e, sg=sg, pgb=pgb: e.activation(out=sg[1][:], in_=pgb, func=AF.Sigmoid), reads=[kgb], writes=["sg%d1" % t2])
                            P.op(DVE, lambda e, tt=tt, sg=sg, ppa=ppa: e.tensor_tensor(out=tt[0][:], in0=ppa, in1=sg[0][:], op=ALU.mult), reads=[kpa, "sg%d0" % t2], writes=["tt%d0" % t2])
                            P.op(DVE, lambda e, tt=tt, sg=sg, ppb=ppb: e.tensor_tensor(out=tt[1][:], in0=ppb, in1=sg[1][:], op=ALU.mult), reads=[kpb, "sg%d1" % t2], writes=["tt%d1" % t2])
                            P.op(DVE, lambda e, tt=tt, ms=ms: e.tensor_tensor(out=ms[:], in0=tt[0][:], in1=tt[1][:], op=ALU.add), reads=["tt%d0" % t2, "tt%d1" % t2], writes=["ms%d" % t2])
                            P.op(SP, lambda e, ms=ms, tb=tb, jb=jb: e.dma_start(out=mT_d[tb][:, jb * 512:(jb + 1) * 512], in_=ms[:]), reads=["ms%d" % t2], writes=["mT_d%d" % tb], dma=True)
                    finish(P)
            with contextlib.ExitStack() as ph4:
                for tb in range(4):
                    with contextlib.ExitStack() as ph:
                        P = Prog(ss)
                        zt = sbuf(ph, "zt", [128, 4, D], F32)
                        mT = sbuf(ph, "mT", [128, 32, 512], BF16)
                        P.op(SP, lambda e, tb=tb: e.dma_start(out=mT[:].rearrange("p j t -> p (j t)"), in_=mT_d[tb]), writes=["mT"], dma=True)
                        Wo = [sbuf(ph, "Wo%d" % i, [128, 32, 256], BF16) for i in range(3)]
                        gbc = sbuf(ph, "gbc", [128, D], F32)
                        lgbc = sbuf(ph, "lgbc", [128, D], F32)
                        lbbc = sbuf(ph, "lbbc", [128, D], F32)
                        t2 = [sbuf(ph, "t2%d" % i, [128, 256], F32) for i in range(2)]
                        stt = sbuf(ph, "stt4", [128, 8, 6], F32)
                        mv = sbuf(ph, "mv4", [128, 4], F32)
                        P.op(SP, lambda e: e.dma_start(out=gbc[:], in_=gate_d.rearrange("j p -> (j p)").partition_broadcast(128)), writes=["gbc"], dma=True)
                        P.op(SP, lambda e: e.dma_start(out=lgbc[:], in_=lng.rearrange("o d -> (o d)").partition_broadcast(128)), writes=["lgbc"], dma=True)
                        P.op(SP, lambda e: e.dma_start(out=lbbc[:], in_=lnb.rearrange("o d -> (o d)").partition_broadcast(128)), writes=["lbbc"], dma=True)
                        for tl in range(4):
                            r0 = (2 + 4 * tb + tl) * 128
                            P.op(SP, lambda e, tl=tl, r0=r0: e.dma_start(out=zt[:, tl, :], in_=xe[r0:r0 + 128, :]), writes=["zt%d" % tl], dma=True)
                        pso = Rot([(ps4, "ps4"), (ps5, "ps5"), (ps7, "ps7")])
                        for nb in range(16):
                            W, wkey = Wo[nb % 3], "Wo%d" % (nb % 3)
                            P.op(POOL, lambda e, W=W, nb=nb: e.dma_start(out=W[:].rearrange("p c n -> p (c n)"), in_=w_4b[nb]), writes=[wkey], dma=True)
                            for tl in range(4):
                                pt, pk = pso.next()
                                for jb in range(32):
                                    P.op(PE, lambda e, pt=pt, W=W, jb=jb, tl=tl: e.matmul(pt[:, 0:256], lhsT=mT[:, jb, tl * 128:(tl + 1) * 128], rhs=W[:, jb, :],
                                                                                         start=(jb == 0), stop=(jb == 31)), reads=[wkey, "mT"], writes=[pk])
                                T2, tk = t2[(nb * 4 + tl) % 2], "t2%d" % ((nb * 4 + tl) % 2)
                                c0 = nb * 256
                                P.op(DVE, lambda e, pt=pt, T2=T2, c0=c0: e.tensor_tensor(out=T2[:], in0=pt[:, 0:256], in1=gbc[:, c0:c0 + 256], op=ALU.mult),
                                     reads=[pk, "gbc"], writes=[tk])
                                P.op(DVE, lambda e, T2=T2, tl=tl, c0=c0: e.scalar_tensor_tensor(out=zt[:, tl, c0:c0 + 256], in0=zt[:, tl, c0:c0 + 256], scalar=ALPHA,
                                                                                                in1=T2[:], op0=ALU.mult, op1=ALU.add), reads=[tk], writes=["zt%d" % tl])
                        for tl in range(4):
                            zk = "zt%d" % tl
                            for g in range(8):
                                P.op(DVE, lambda e, tl=tl, g=g: e.bn_stats(out=stt[:, g, :], in_=zt[:, tl, g * 512:(g + 1) * 512]), reads=[zk], writes=["stt4"])
                            P.op(DVE, lambda e: e.bn_aggr(out=mv[:, 0:2], in_=stt[:]), reads=["stt4"], writes=["mv4"])
                            P.op(ACT, lambda e: e.activation(out=mv[:, 1:2], in_=mv[:, 1:2], func=AF.Sqrt, bias=1e-6, scale=1.0), writes=["mv4"])
                            P.op(DVE, lambda e: e.reciprocal(out=mv[:, 1:2], in_=mv[:, 1:2]), writes=["mv4"])
                            P.op(DVE, lambda e: e.tensor_scalar(out=mv[:, 2:3], in0=mv[:, 0:1], scalar1=mv[:, 1:2], scalar2=-1.0, op0=ALU.mult, op1=ALU.mult),
                                 writes=["mv4"])
                            P.op(ACT, lambda e, tl=tl: e.activation(out=zt[:, tl, :], in_=zt[:, tl, :], func=AF.Identity, scale=mv[:, 1:2], bias=mv[:, 2:3]),
                                 reads=["mv4"], writes=[zk])
                            P.op(DVE, lambda e, tl=tl: e.tensor_tensor(out=zt[:, tl, :], in0=zt[:, tl, :], in1=lgbc[:], op=ALU.mult), reads=["lgbc"], writes=[zk])
                            P.op(DVE, lambda e, tl=tl: e.tensor_tensor(out=zt[:, tl, :], in0=zt[:, tl, :], in1=lbbc[:], op=ALU.add), reads=["lbbc"], writes=[zk])
                            r0 = (4 * tb + tl) * 128
                            P.op(SP, lambda e, tl=tl, r0=r0: e.dma_start(out=out[r0:r0 + 128, :], in_=zt[:, tl, :]), reads=[zk], writes=["out"], dma=True)
                        finish(P)
    return nc, nops[0]


def _bias_tables(rpb):
    qc = np.arange(64)
    cs = np.clip(qc - 8, 0, 48)
    kc = np.arange(64)
    col_in = (kc[None, :] >= cs[:, None]) & (kc[None, :] < cs[:, None] + 16)
    dcol = np.clip(kc[None, :] - qc[:, None], -15, 15) + 15
    tabs = np.full((4, NH, 128, 5, 576), NEG, np.float32)
    for seg in range(4):
        for vi, pr in enumerate((0, 1, 2, 14, 15)):
            for a in range(2):
                lr = 2 * pr + a
                r = seg * 32 + lr
                rs = min(max(r - 4, 0), 120)
                for j in range(9):
                    if not (a <= j <= a + 7):
                        continue
                    kap = 2 * pr + j
                    g = seg * 32 - 4 + kap
                    if seg == 0 and kap < 4:
                        g = 4 + kap
                    if seg == 3 and kap >= 36:
                        g = 120 + (kap - 36)
                    assert rs <= g < rs + 8, (seg, pr, a, j, g, rs)
                    dr = g - r + 7
                    blk = rpb[:, dr][:, dcol]
                    blk = np.where(col_in[None], blk, np.float32(NEG))
                    tabs[seg, :, a * 64:(a + 1) * 64, vi, j * 64:(j + 1) * 64] = blk
    return tabs.reshape(4, NH, 128, 5 * 576)


_CACHE = {}


def _prep_shared(w_ada, b_ada, w_in, hg_lb_fwd, hg_lb_bwd, hg_norm_w, w_pa, w_pb, w_out, ln_g, ln_b):
    w_in0 = w_in[0]
    wv = w_in0.reshape(32, 128, 26624)

    def grp(g):
        s = [0, 2048, 4096, 6144, 8192, 10240, 12288, 14336, 16384, 18432, 22528][g]
        return s

    w_na = np.empty((NH, 128, 32, 512), np.float32)
    w_hg = np.empty((NH, 128, 32, 640), np.float32)
    for h in range(NH):
        for gi in range(4):
            s = grp(gi) + h * 128
            w_na[h, :, :, gi * 128:(gi + 1) * 128] = wv[:, :, s:s + 128].transpose(1, 0, 2)
        for gi in range(5):
            s = grp(4 + gi) + h * 128
            w_hg[h, :, :, gi * 128:(gi + 1) * 128] = wv[:, :, s:s + 128].transpose(1, 0, 2)
    w_4a = np.empty((32, 128, 96, 128), np.float32)
    pav = w_pa[0].reshape(16, 128, D)
    pbv = w_pb[0].reshape(16, 128, D)
    for jb in range(32):
        w_4a[jb, :, 0:32, :] = wv[:, :, 18432 + jb * 128:18432 + (jb + 1) * 128].transpose(1, 0, 2)
        w_4a[jb, :, 32:64, :] = wv[:, :, 22528 + jb * 128:22528 + (jb + 1) * 128].transpose(1, 0, 2)
        w_4a[jb, :, 64:80, :] = pav[:, :, jb * 128:(jb + 1) * 128].transpose(1, 0, 2)
        w_4a[jb, :, 80:96, :] = pbv[:, :, jb * 128:(jb + 1) * 128].transpose(1, 0, 2)
    wov = w_out[0].reshape(32, 128, D)
    w_4b = np.empty((16, 128, 32, 256), np.float32)
    for nb in range(16):
        w_4b[nb] = wov[:, :, nb * 256:(nb + 1) * 256].transpose(1, 0, 2)
    lbl = np.empty((128, NH, 2, 2), np.float32)
    lbl[:, :, 0, :] = hg_lb_fwd.reshape(2, NH, 128).transpose(2, 1, 0)
    lbl[:, :, 1, :] = hg_lb_bwd.reshape(2, NH, 128).transpose(2, 1, 0)
    return dict(
        w_ada=np.ascontiguousarray(w_ada[0]),
        b_adaT=np.ascontiguousarray(b_ada[0].reshape(96, 128).T),
        w_na=w_na.reshape(NH, 128, 32 * 512), w_hg=w_hg.reshape(NH, 128, 32 * 640),
        w_4a=w_4a.reshape(32, 128, 96 * 128), w_4b=w_4b.reshape(16, 128, 32 * 256),
        lbl=lbl.reshape(128, NH * 4), nw=np.ascontiguousarray(hg_norm_w[0].reshape(128, 1)),
        lng=np.ascontiguousarray(ln_g[0].reshape(1, D)), lnb=np.ascontiguousarray(ln_b[0].reshape(1, D)),
    )


def make_in_maps(x, c, ctx, c_ctx, w_ada, b_ada, w_in, na_rpb, hg_lb_fwd, hg_lb_bwd, hg_norm_w, w_pa, w_pb, w_out, ln_g, ln_b):
    f = lambda a: np.asarray(a, dtype=np.float32)
    x, c, ctx, c_ctx = f(x), f(c), f(ctx), f(c_ctx)
    shared = _prep_shared(f(w_ada), f(b_ada), f(w_in), f(hg_lb_fwd), f(hg_lb_bwd), f(hg_norm_w), f(w_pa), f(w_pb), f(w_out), f(ln_g), f(ln_b))
    tabs = _bias_tables(f(na_rpb)[0])
    maps = []
    for core in range(8):
        b, seg = core // 4, core % 4
        t0 = seg * TLOC
        top = x[b, t0 - 256:t0] if seg > 0 else x[b, 256:512]
        bot = x[b, t0 + TLOC:t0 + TLOC + 256] if seg < 3 else x[b, 7680:7936]
        xe = np.concatenate([top, x[b, t0:t0 + TLOC], bot, ctx[b]], axis=0)
        condT = np.concatenate([c[b].reshape(32, 128).T, c_ctx.reshape(32, 128).T], axis=1)
        m0 = 1.0 if seg == 0 else 0.0
        m3 = 1.0 if seg == 3 else 0.0
        cmask = np.tile(np.array([[m0, 1.0 - m0, m3, 1.0 - m3]], np.float32), (128, 1))
        d = dict(shared)
        d.update(xe=np.ascontiguousarray(xe), condT=np.ascontiguousarray(condT), biasT=np.ascontiguousarray(tabs[seg]), cmask=cmask)
        maps.append(d)
    return maps


def kernel(**inputs):
    if "nc" not in _CACHE:
        _CACHE["nc"] = build_program()[0]
    nc = _CACHE["nc"]
    maps = make_in_maps(**inputs)
    res = run_bass_kernel_spmd(nc, maps, core_ids=list(range(8)))
    outp = np.empty((2, 8192, D), np.float32)
    for core in range(8):
        b, seg = core // 4, core % 4
        outp[b, seg * TLOC:(seg + 1) * TLOC] = res.results[core]["out"]
    return outp
```

```python
import contextlib
import numpy as np
import concourse.bass as bass
import concourse.mybir as mybir
from concourse.bass_utils import run_bass_kernel_spmd

F32 = mybir.dt.float32
BF16 = mybir.dt.bfloat16
AF = mybir.ActivationFunctionType
ALU = mybir.AluOpType
AX = mybir.AxisListType

PE, ACT, DVE, POOL, SP = "pe", "act", "dve", "pool", "sp"
ENGS = (PE, ACT, DVE, POOL, SP)
NDMA_SEM = {SP: 8, POOL: 6, ACT: 2}

D = 4096
NH = 16
NT = 22
NBLK = 11
TLOC = 2048
ALPHA = 2.0 ** 0.25
NEG = -30000.0


class SemState:
    def __init__(self, nc, stack):
        self.esem = {e: stack.enter_context(nc.semaphore("s_" + e)) for e in ENGS}
        self.dsem = {}
        for e, K in NDMA_SEM.items():
            for i in range(K):
                self.dsem[(e, i)] = stack.enter_context(nc.semaphore("d_%s%d" % (e, i)))
        self.cnt = {e: 0 for e in ENGS}
        self.ndma = {e: 0 for e in ENGS}


class Prog:
    def __init__(self, ss, same_engine_sync=True):
        self.ss = ss
        self.ops = []
        self.last_w = {}
        self.readers = {}
        self.same_engine_sync = same_engine_sync
        self.ndma = ss.ndma
        self.dma_hist = {e: [] for e in ENGS}
        self.last_of = {}

    def op(self, eng, fn, reads=(), writes=(), dma=False, extra_deps=()):
        idx = len(self.ops)
        deps = set(extra_deps)
        for k in list(reads) + list(writes):
            w = self.last_w.get(k)
            if w is not None:
                deps.add(w)
        for k in writes:
            for r in self.readers.get(k, ()):
                deps.add(r)
        deps.discard(idx)
        rec = dict(idx=idx, eng=eng, fn=fn, deps=deps, dma=dma, signal=dma, dslot=None)
        if dma:
            n = self.ndma[eng]
            self.ndma[eng] += 1
            K = NDMA_SEM[eng]
            rec["dslot"] = (eng, n % K, 16 * (n // K + 1))
            rec["dprev"] = 16 * (n // K)
            self.dma_hist[eng].append(idx)
        elif fn is not None:
            self.last_of[eng] = idx
        self.ops.append(rec)
        for k in writes:
            self.last_w[k] = idx
            self.readers[k] = []
        for k in reads:
            if k not in writes:
                self.readers.setdefault(k, []).append(idx)
        return idx

    def barrier(self):
        deps = set(self.last_of.values())
        for e, K in NDMA_SEM.items():
            deps.update(self.dma_hist[e][-K:])
        for e in ENGS:
            self.op(e, None, extra_deps=deps)

    def emit(self, nc):
        ops = self.ops
        esem, dsem, cnt = self.ss.esem, self.ss.dsem, self.ss.cnt
        for o in ops:
            for d in o["deps"]:
                p = ops[d]
                if p["dma"]:
                    continue
                if p["eng"] == o["eng"] and not o["dma"]:
                    if p["eng"] == PE or not self.same_engine_sync:
                        continue
                p["signal"] = True
        for o in ops:
            if o["dma"]:
                e, s, v = o["dslot"]
                o["sig"] = (dsem[(e, s)], v, 16, ("d", e, s))
            elif o["signal"]:
                cnt[o["eng"]] += 1
                o["sig"] = (esem[o["eng"]], cnt[o["eng"]], 1, ("e", o["eng"]))
        waited = {e: {} for e in ENGS}
        for o in ops:
            e = o["eng"]
            waits = {}
            for d in o["deps"]:
                p = ops[d]
                if not p["signal"] or p["fn"] is None:
                    continue
                if (not p["dma"]) and p["eng"] == e and not o["dma"]:
                    if e == PE or not self.same_engine_sync:
                        continue
                sem, val, _, key = p["sig"]
                if waits.get(key, (None, 0))[1] < val:
                    waits[key] = (sem, val)
            if o["dma"] and o["dprev"] > 0:
                sem, val, _, key = o["sig"]
                if waits.get(key, (None, 0))[1] < o["dprev"]:
                    waits[key] = (sem, o["dprev"])
            wl = []
            for key, (sem, val) in waits.items():
                if waited[e].get(key, 0) >= val:
                    continue
                waited[e][key] = val
                wl.append((sem, val))
            o["waits"] = wl
        per = {e: [o for o in ops if o["eng"] == e] for e in ENGS}
        self.stats = {e: len(per[e]) for e in ENGS}
        def run(eng_ops):
            def body(engine):
                for o in eng_ops:
                    for sem, val in o["waits"]:
                        engine.wait_ge(sem, val)
                    if o["fn"] is None:
                        continue
                    inst = o["fn"](engine)
                    if o["signal"]:
                        sem, val, inc, _ = o["sig"]
                        inst.then_inc(sem, inc)
            return body

        with nc.Block() as block:
            block.sync(run(per[SP]))
            block.tensor(run(per[PE]))
            block.scalar(run(per[ACT]))
            block.vector(run(per[DVE]))
            block.gpsimd(run(per[POOL]))


class Rot:
    def __init__(self, items):
        self.items = items
        self.i = 0

    def next(self):
        it = self.items[self.i % len(self.items)]
        self.i += 1
        return it


def build_program(stage=99, phases=(2, 3)):
    nc = bass.Bass("TRN2", target_bir_lowering=False)
    dtn = nc.dram_tensor
    xe = dtn("xe", [NT * 128, D], F32, kind="ExternalInput").ap()
    condT = dtn("condT", [128, 64], F32, kind="ExternalInput").ap()
    w_ada = dtn("w_ada", [D, 3 * D], F32, kind="ExternalInput").ap()
    b_adaT = dtn("b_adaT", [128, 96], F32, kind="ExternalInput").ap()
    w_na = dtn("w_na", [NH, 128, 32 * 512], F32, kind="ExternalInput").ap()
    w_hg = dtn("w_hg", [NH, 128, 32 * 640], F32, kind="ExternalInput").ap()
    w_4a = dtn("w_4a", [32, 128, 96 * 128], F32, kind="ExternalInput").ap()
    w_4b = dtn("w_4b", [16, 128, 32 * 256], F32, kind="ExternalInput").ap()
    biasT = dtn("biasT", [NH, 128, 5 * 576], F32, kind="ExternalInput").ap()
    lbl = dtn("lbl", [128, NH * 4], F32, kind="ExternalInput").ap()
    nw_in = dtn("nw", [128, 1], F32, kind="ExternalInput").ap()
    lng = dtn("lng", [1, D], F32, kind="ExternalInput").ap()
    lnb = dtn("lnb", [1, D], F32, kind="ExternalInput").ap()
    cmask = dtn("cmask", [128, 4], F32, kind="ExternalInput").ap()
    out = dtn("out", [TLOC, D], F32, kind="ExternalOutput").ap()
    hT_d = dtn("hT_d", [NT, 128, D], BF16, kind="Internal").ap()
    ya_d = dtn("ya_d", [NH, 128, TLOC], BF16, kind="Internal").ap()
    yb_d = dtn("yb_d", [NH, 128, TLOC], BF16, kind="Internal").ap()
    gate_d = dtn("gate_d", [32, 128], F32, kind="Internal").ap()
    mT_d = dtn("mT_d", [4, 128, 32 * 512], BF16, kind="Internal").ap()
    dbg = None
    if stage < 99:
        dbg = dtn("dbg", [128, 8192], F32, kind="ExternalOutput").ap()

    uid = [0]
    nops = [0]

    def nm(s):
        uid[0] += 1
        return "%s_%d" % (s, uid[0])

    with contextlib.ExitStack() as st:
        def sbuf(stack, name, shape, dt):
            return stack.enter_context(nc.sbuf_tensor(nm(name), shape, dt))

        ps0 = st.enter_context(nc.psum_tensor("ps0", [128, 512], F32))
        ps1 = st.enter_context(nc.psum_tensor("ps1", [128, 512], F32))
        ps23 = st.enter_context(nc.psum_tensor("ps23", [128, 1024], F32))
        ps4 = st.enter_context(nc.psum_tensor("ps4", [128, 512], F32))
        ps5 = st.enter_context(nc.psum_tensor("ps5", [128, 512], F32))
        ps6 = st.enter_context(nc.psum_tensor("ps6", [128, 1024], BF16))
        ps7 = st.enter_context(nc.psum_tensor("ps7", [128, 512], F32))
        ps7b = ps7[:].bitcast(BF16)
        ss = SemState(nc, st)

        def finish(P):
            P.barrier()
            P.emit(nc)
            nops[0] += len(P.ops)

        P = Prog(ss)

        ident = sbuf(st, "ident", [128, 128], F32)
        identb = sbuf(st, "identb", [128, 128], BF16)
        onesb = sbuf(st, "onesb", [128, 128], BF16)
        maskF = sbuf(st, "maskF", [64, 64], F32)
        maskB = sbuf(st, "maskB", [64, 64], F32)
        rmask = sbuf(st, "rmask", [128, 256], F32)
        modT = sbuf(st, "modT", [128, 192], F32)
        s1T = sbuf(st, "s1T", [128, 64], F32)
        shT = sbuf(st, "shT", [128, 64], F32)
        lbs = sbuf(st, "lbs", [128, NH * 2], F32)
        oml = sbuf(st, "oml", [128, NH * 2], F32)
        noml = sbuf(st, "noml", [128, NH * 2], F32)
        lbl_sb = sbuf(st, "lbl_sb", [128, NH * 4], F32)
        nw = sbuf(st, "nw_sb", [128, 1], F32)
        cm = sbuf(st, "cm", [128, 4], F32)

        P.op(POOL, lambda e: e.memset(ident[:], 0.0), writes=["ident"])
        P.op(POOL, lambda e: e.affine_select(out=ident[:], in_=ident[:], pattern=[[-1, 128]], compare_op=ALU.not_equal,
                                             fill=1.0, base=0, channel_multiplier=1), writes=["ident"])
        P.op(POOL, lambda e: e.tensor_copy(out=identb[:], in_=ident[:]), reads=["ident"], writes=["identb"])
        P.op(POOL, lambda e: e.memset(onesb[:], 1.0), writes=["onesb"])
        P.op(POOL, lambda e: e.memset(maskF[:], 1.0), writes=["maskF"])
        P.op(POOL, lambda e: e.affine_select(out=maskF[:], in_=maskF[:], pattern=[[1, 64]], compare_op=ALU.is_ge,
                                             fill=0.0, base=0, channel_multiplier=-1), writes=["maskF"])
        P.op(POOL, lambda e: e.memset(maskB[:], 1.0), writes=["maskB"])
        P.op(POOL, lambda e: e.affine_select(out=maskB[:], in_=maskB[:], pattern=[[-1, 64]], compare_op=ALU.is_ge,
                                             fill=0.0, base=0, channel_multiplier=1), writes=["maskB"])
        P.op(POOL, lambda e: e.memset(rmask[:], 1.0), writes=["rmask"])
        P.op(POOL, lambda e: e.memset(rmask[:].rearrange("p (n c) -> p n c", c=64)[:, :, 0:1], 0.0), writes=["rmask"])
        P.op(SP, lambda e: e.dma_start(out=lbl_sb[:], in_=lbl), writes=["lbl"], dma=True)
        P.op(SP, lambda e: e.dma_start(out=nw[:], in_=nw_in), writes=["nw"], dma=True)
        P.op(SP, lambda e: e.dma_start(out=cm[:], in_=cmask), writes=["cm"], dma=True)
        lv = lbl_sb[:].rearrange("p (a s) -> p a s", s=2)
        P.op(DVE, lambda e: e.tensor_tensor(out=oml[:], in0=lv[:, :, 0], in1=lv[:, :, 1], op=ALU.subtract), reads=["lbl"], writes=["oml"])
        P.op(ACT, lambda e: e.activation(out=lbs[:], in_=oml[:], func=AF.Sigmoid), reads=["oml"], writes=["lbs"])
        P.op(DVE, lambda e: e.tensor_scalar(out=oml[:], in0=lbs[:], scalar1=-1.0, scalar2=1.0, op0=ALU.mult, op1=ALU.add),
             reads=["lbs"], writes=["oml"])
        P.op(DVE, lambda e: e.tensor_scalar(out=noml[:], in0=lbs[:], scalar1=1.0, scalar2=-1.0, op0=ALU.mult, op1=ALU.add),
             reads=["lbs"], writes=["noml"])

        finish(P)

        with contextlib.ExitStack() as ph:
            P = Prog(ss)
            cT = sbuf(ph, "cT", [128, 64], F32)
            sc = sbuf(ph, "sc", [128, 64], F32)
            bT = sbuf(ph, "bT", [128, 96], F32)
            wa = [sbuf(ph, "wa%d" % i, [128, 32, 512], F32) for i in range(2)]
            mod2 = sbuf(ph, "mod2", [2, 3 * D], F32)
            P.op(SP, lambda e: e.dma_start(out=cT[:], in_=condT), writes=["cT"], dma=True)
            P.op(SP, lambda e: e.dma_start(out=bT[:], in_=b_adaT), writes=["bT"], dma=True)
            P.op(ACT, lambda e: e.activation(out=sc[:], in_=cT[:], func=AF.Silu), reads=["cT"], writes=["sc"])
            scv = sc[:].rearrange("p (v j) -> p j v", v=2)
            psm = Rot([(ps1, "ps1"), (ps4, "ps4"), (ps5, "ps5")])
            for n in range(24):
                wb, key = wa[n % 2], "wa%d" % (n % 2)
                P.op(SP, lambda e, wb=wb, n=n: e.dma_start(out=wb[:], in_=w_ada[:, n * 512:(n + 1) * 512].rearrange("(c p) n -> p c n", p=128)),
                     writes=[key], dma=True)
                pt, pk = psm.next()
                for c in range(32):
                    P.op(PE, lambda e, pt=pt, wb=wb, c=c: e.matmul(pt[0:2, 0:512], lhsT=scv[:, c, :], rhs=wb[:, c, :], start=(c == 0), stop=(c == 31)),
                         reads=[key, "sc"], writes=[pk])
                P.op(DVE, lambda e, pt=pt, n=n: e.tensor_copy(out=mod2[:, n * 512:(n + 1) * 512], in_=pt[0:2, 0:512]), reads=[pk], writes=["mod2"])
            for jb in range(96):
                P.op(PE, lambda e, jb=jb: e.transpose(out=ps0[:, jb * 2:jb * 2 + 2], in_=mod2[0:2, jb * 128:(jb + 1) * 128], identity=ident[0:2, 0:2]),
                     reads=["mod2", "ident"], writes=["ps0"])
            P.op(DVE, lambda e: e.tensor_tensor(out=modT[:].rearrange("p (j v) -> p j v", v=2), in0=ps0[:, 0:192].rearrange("p (j v) -> p j v", v=2),
                                                in1=bT[:].unsqueeze(2).to_broadcast([128, 96, 2]), op=ALU.add),
                 reads=["ps0", "bT"], writes=["modT"])
            P.op(DVE, lambda e: e.tensor_copy(out=shT[:], in_=modT[:, 0:64]), reads=["modT"], writes=["shT"])
            P.op(DVE, lambda e: e.tensor_scalar(out=s1T[:], in0=modT[:, 64:128], scalar1=1.0, scalar2=None, op0=ALU.add),
                 reads=["modT"], writes=["s1T"])
            gsb = sbuf(ph, "gsb", [128, 32], F32)
            P.op(DVE, lambda e: e.tensor_copy(out=gsb[:], in_=modT[:].rearrange("p (j v) -> p j v", v=2)[:, 64:96, 0]),
                 reads=["modT"], writes=["gsb"])
            P.op(SP, lambda e: e.dma_start(out=gate_d.rearrange("j p -> p j"), in_=gsb[:], allow_slow_non_contiguous=True),
                 reads=["gsb"], writes=["gate_d"], dma=True)
            if stage == 0:
                P.op(SP, lambda e: e.dma_start(out=dbg[:, 0:192], in_=modT[:]), reads=["modT"], writes=["dbg"], dma=True)
            finish(P)
        s1v = s1T[:].rearrange("p (c v) -> p c v", v=2)
        shv = shT[:].rearrange("p (c v) -> p c v", v=2)

        if stage >= 1:
            with contextlib.ExitStack() as ph:
                P = Prog(ss)
                xt = [sbuf(ph, "xt%d" % i, [128, D], F32) for i in range(3)]
                ht = [sbuf(ph, "ht%d" % i, [128, D], BF16) for i in range(3)]
                stt = [sbuf(ph, "stt%d" % i, [128, 8, 6], F32) for i in range(3)]
                mv = [sbuf(ph, "mv%d" % i, [128, 4], F32) for i in range(3)]
                psr = Rot([(ps0, "ps0"), (ps1, "ps1"), (ps4, "ps4"), (ps5, "ps5"), (ps7, "ps7")])
                for t in range(NT):
                    b = t % 3
                    X, H, S_, M = xt[b], ht[b], stt[b], mv[b]
                    kx, kh, ks, km = "xt%d" % b, "ht%d" % b, "stt%d" % b, "mv%d" % b
                    v = 1 if t >= 20 else 0
                    P.op(SP, lambda e, X=X, t=t: e.dma_start(out=X[:], in_=xe[t * 128:(t + 1) * 128, :]), writes=[kx], dma=True)
                    for g in range(8):
                        P.op(DVE, lambda e, X=X, S_=S_, g=g: e.bn_stats(out=S_[:, g, :], in_=X[:, g * 512:(g + 1) * 512]), reads=[kx], writes=[ks])
                    P.op(DVE, lambda e, S_=S_, M=M: e.bn_aggr(out=M[:, 0:2], in_=S_[:]), reads=[ks], writes=[km])
                    P.op(ACT, lambda e, M=M: e.activation(out=M[:, 1:2], in_=M[:, 1:2], func=AF.Sqrt, bias=1e-6, scale=1.0), writes=[km])
                    P.op(DVE, lambda e, M=M: e.reciprocal(out=M[:, 1:2], in_=M[:, 1:2]), writes=[km])
                    P.op(DVE, lambda e, M=M: e.tensor_scalar(out=M[:, 2:3], in0=M[:, 0:1], scalar1=M[:, 1:2], scalar2=-1.0, op0=ALU.mult, op1=ALU.mult),
                         writes=[km])
                    P.op(ACT, lambda e, X=X, M=M: e.activation(out=X[:], in_=X[:], func=AF.Identity, scale=M[:, 1:2], bias=M[:, 2:3]),
                         reads=[km], writes=[kx])
                    for g in range(8):
                        pt, pk = psr.next()
                        for k in range(4):
                            c = g * 4 + k
                            P.op(PE, lambda e, pt=pt, X=X, k=k, c=c: e.transpose(out=pt[:, k * 128:(k + 1) * 128], in_=X[:, c * 128:(c + 1) * 128],
                                                                                 identity=ident[:]), reads=[kx, "ident"], writes=[pk])
                        for k in range(4):
                            c = g * 4 + k
                            if k == 0:
                                P.op(ACT, lambda e, pt=pt, H=H, k=k, c=c, v=v: e.activation(
                                    out=H[:, c * 128:(c + 1) * 128], in_=pt[:, k * 128:(k + 1) * 128], func=AF.Identity,
                                    scale=s1v[:, c, v:v + 1], bias=shv[:, c, v:v + 1]), reads=[pk, "s1T", "shT"], writes=[kh])
                            else:
                                P.op(DVE, lambda e, pt=pt, H=H, k=k, c=c, v=v: e.tensor_scalar(
                                    out=H[:, c * 128:(c + 1) * 128], in0=pt[:, k * 128:(k + 1) * 128],
                                    scalar1=s1v[:, c, v:v + 1], scalar2=shv[:, c, v:v + 1], op0=ALU.mult, op1=ALU.add),
                                    reads=[pk, "s1T", "shT"], writes=[kh])
                    P.op(POOL, lambda e, H=H, t=t: e.dma_start(out=hT_d[t], in_=H[:]), reads=[kh], writes=["hT_d%d" % t], dma=True)
                    if stage == 1 and t in (2, 20):
                        dt_ = sbuf(ph, "dbgt", [128, 1024], F32)
                        P.op(POOL, lambda e, H=H, dt_=dt_: e.tensor_copy(out=dt_[:], in_=H[:, 0:1024]), reads=[kh], writes=["dd%d" % t])
                        o0 = 0 if t == 2 else 1024
                        P.op(SP, lambda e, dt_=dt_, o0=o0: e.dma_start(out=dbg[:, o0:o0 + 1024], in_=dt_[:]), reads=["dd%d" % t], writes=["dbg"], dma=True)
                finish(P)

        def load_hT_block(bi, buf, key):
            P.op(SP, lambda e: e.dma_start(out=buf[:].rearrange("p (t d) -> p t d", t=2), in_=hT_d[2 * bi:2 * bi + 2].rearrange("t p d -> p t d")),
                 reads=["hT_d%d" % (2 * bi), "hT_d%d" % (2 * bi + 1)], writes=[key], dma=True)

        def proj_fm(pst, pkey, W, wkey, col0, hb, hkey):
            hv = hb[:].rearrange("p (t d) -> p t d", t=2)
            for c in range(32):
                P.op(PE, lambda e, c=c: e.matmul(pst[:, 0:256], lhsT=W[:, c, col0:col0 + 128], rhs=hv[:, :, c * 128:(c + 1) * 128],
                                                 start=(c == 0), stop=(c == 31)), reads=[wkey, hkey], writes=[pkey])

        if stage >= 2 and 2 in phases:
            with contextlib.ExitStack() as ph:
                P = Prog(ss)
                Wn = [sbuf(ph, "Wn%d" % i, [128, 32, 512], BF16) for i in range(2)]
                bt = [sbuf(ph, "bt%d" % i, [128, 5, 576], BF16) for i in range(3)]
                hb = [sbuf(ph, "hb%d" % i, [128, 2 * D], BF16) for i in range(3)]
                qTs = [sbuf(ph, "qT%d" % i, [128, TLOC], BF16) for i in range(2)]
                kTs = [sbuf(ph, "kT%d" % i, [128, NT * 128], BF16) for i in range(2)]
                szTs = [sbuf(ph, "szT%d" % i, [128, TLOC], BF16) for i in range(2)]
                vsbs = [sbuf(ph, "vsb%d" % i, [128, NT, 128], BF16) for i in range(2)]
                yT = [sbuf(ph, "yT%d" % i, [128, TLOC], BF16) for i in range(2)]
                mx = sbuf(ph, "mx", [128, 2], F32)
                rs = sbuf(ph, "rs", [128, 2], F32)
                pf = sbuf(ph, "pf", [128, 832], F32)
                pb = sbuf(ph, "pb", [128, 832], BF16)
                pT = sbuf(ph, "pT", [128, 896], BF16)
                psr = Rot([(ps0, "ps0"), (ps1, "ps1"), (ps7, "ps7")])
                nheads = NH if stage > 3 else 2
                blkc = [0]

                def load_w(h):
                    W, B = Wn[h % 2], bt[h % 3]
                    P.op(POOL, lambda e: e.dma_start(out=W[:].rearrange("p c n -> p (c n)"), in_=w_na[h]), writes=["Wn%d" % (h % 2)], dma=True)
                    P.op(POOL, lambda e: e.dma_start(out=B[:].rearrange("p v k -> p (v k)"), in_=biasT[h]), writes=["bt%d" % (h % 3)], dma=True)

                def proj_gen(h):
                    par = h % 2
                    W, wkey = Wn[par], "Wn%d" % par
                    qT, kT, szT, vsb = qTs[par], kTs[par], szTs[par], vsbs[par]
                    qk, kk_, zk, vk = "qT%d" % par, "kT%d" % par, "szT%d" % par, "vsb%d" % par
                    for bi in range(NBLK):
                        Hb, hkey = hb[blkc[0] % 3], "hb%d" % (blkc[0] % 3)
                        blkc[0] += 1
                        load_hT_block(bi, Hb, hkey)
                        local = 1 <= bi <= 8
                        lo = (bi - 1) * 256
                        if local:
                            pt, pk = psr.next()
                            proj_fm(pt, pk, W, wkey, 0, Hb, hkey)
                            P.op(ACT, lambda e, pt=pt, lo=lo: e.activation(out=qT[:, lo:lo + 256], in_=pt[:, 0:256], func=AF.Copy, scale=128.0 ** -0.5),
                                 reads=[pk], writes=[qk])
                            yield
                            pt, pk = psr.next()
                            proj_fm(pt, pk, W, wkey, 384, Hb, hkey)
                            P.op(ACT, lambda e, pt=pt, lo=lo: e.activation(out=szT[:, lo:lo + 256], in_=pt[:, 0:256], func=AF.Silu),
                                 reads=[pk], writes=[zk])
                            yield
                        pt, pk = psr.next()
                        proj_fm(pt, pk, W, wkey, 128, Hb, hkey)
                        P.op(DVE, lambda e, pt=pt, bi=bi: e.tensor_copy(out=kT[:, bi * 256:(bi + 1) * 256], in_=pt[:, 0:256]), reads=[pk], writes=[kk_])
                        yield
                        hv = Hb[:].rearrange("p (t d) -> p t d", t=2)
                        for tl in range(2):
                            pt, pk = psr.next()
                            for c in range(32):
                                P.op(PE, lambda e, pt=pt, hv=hv, tl=tl, c=c: e.matmul(
                                    pt[:, 0:128], lhsT=hv[:, tl, c * 128:(c + 1) * 128], rhs=W[:, c, 256:384], start=(c == 0), stop=(c == 31)),
                                    reads=[wkey, hkey], writes=[pk])
                            P.op(DVE, lambda e, pt=pt, bi=bi, tl=tl: e.tensor_copy(out=vsb[:, 2 * bi + tl, :], in_=pt[:, 0:128]), reads=[pk], writes=[vk])
                            yield

                def attn_gen(h):
                    par = h % 2
                    B, bkey = bt[h % 3], "bt%d" % (h % 3)
                    Y, ykey = yT[par], "yT%d" % par
                    qT, kT, szT, vsb = qTs[par], kTs[par], szTs[par], vsbs[par]
                    qk, kk_, zk, vk = "qT%d" % par, "kT%d" % par, "szT%d" % par, "vsb%d" % par

                    def scores(pr):
                        var = {0: 0, 1: 1, 14: 3, 15: 4}.get(pr, 2)
                        q0 = pr * 128
                        P.op(PE, lambda e: e.matmul(ps23[:, 0:512], lhsT=qT[:, q0:q0 + 128], rhs=kT[:, q0:q0 + 512], start=True, stop=False),
                             reads=[qk, kk_], writes=["ps23"])
                        P.op(PE, lambda e: e.matmul(ps23[:, 0:512], lhsT=identb[:], rhs=B[:, var, 0:512], start=False, stop=True),
                             reads=[bkey, "identb"], writes=["ps23"])
                        P.op(PE, lambda e: e.matmul(ps23[:, 512:576], lhsT=qT[:, q0:q0 + 128], rhs=kT[:, q0 + 512:q0 + 576], start=True, stop=False),
                             reads=[qk, kk_], writes=["ps23"])
                        P.op(PE, lambda e: e.matmul(ps23[:, 512:576], lhsT=identb[:], rhs=B[:, var, 512:576], start=False, stop=True),
                             reads=[bkey, "identb"], writes=["ps23"])
                        P.op(PE, lambda e: e.matmul(ps23[:, 576:832], lhsT=qT[:, q0:q0 + 128], rhs=kT[:, 2560:2816], start=True, stop=True),
                             reads=[qk, kk_], writes=["ps23"])

                    def chain(pr):
                        P.op(DVE, lambda e: e.reduce_max(out=mx[:, 0:1], in_=ps23[:, 0:832], axis=AX.X), reads=["ps23"], writes=["mx"])
                        P.op(DVE, lambda e: e.tensor_scalar(out=mx[:, 1:2], in0=mx[:, 0:1], scalar1=-1.0, scalar2=None, op0=ALU.mult), writes=["mx"])
                        P.op(ACT, lambda e: e.activation(out=pf[:], in_=ps23[:, 0:832], func=AF.Exp, bias=mx[:, 1:2], scale=1.0, accum_out=rs[:, 0:1]),
                             reads=["ps23", "mx"], writes=["pf", "rs"])
                        P.op(DVE, lambda e: e.reciprocal(out=rs[:, 1:2], in_=rs[:, 0:1]), writes=["rs"])
                        P.op(DVE, lambda e: e.tensor_scalar(out=pb[:], in0=pf[:], scalar1=rs[:, 1:2], scalar2=None, op0=ALU.mult),
                             reads=["pf", "rs"], writes=["pb"])

                    def transposes(pr):
                        for j in range(7):
                            if j < 4:
                                src, dst = pb[:, j * 128:(j + 1) * 128], ps6[:, j * 128:(j + 1) * 128]
                            elif j == 4:
                                src, dst = pb[:, 512:576], ps6[0:64, 512:640]
                            else:
                                src, dst = pb[:, 576 + (j - 5) * 128:576 + (j - 4) * 128], ps6[:, 640 + (j - 5) * 128:640 + (j - 4) * 128]
                            P.op(PE, lambda e, src=src, dst=dst: e.transpose(out=dst, in_=src, identity=identb[:]), reads=["pb", "identb"], writes=["ps6"])
                        P.op(ACT, lambda e: e.copy(out=pT[:, 0:512], in_=ps6[:, 0:512]), reads=["ps6"], writes=["pT"])
                        P.op(DVE, lambda e: e.tensor_copy(out=pT[0:64, 512:640], in_=ps6[0:64, 512:640]), reads=["ps6"], writes=["pT"])
                        P.op(DVE, lambda e: e.tensor_copy(out=pT[:, 640:896], in_=ps6[:, 640:896]), reads=["ps6"], writes=["pT"])

                    def pv(pr):
                        q0 = pr * 128
                        for j in range(7):
                            if j < 4:
                                lt, rh = vsb[:, pr + j, :], pT[:, j * 128:(j + 1) * 128]
                            elif j == 4:
                                lt, rh = vsb[0:64, pr + 4, :], pT[0:64, 512:640]
                            else:
                                lt, rh = vsb[:, 20 + (j - 5), :], pT[:, 640 + (j - 5) * 128:640 + (j - 4) * 128]
                            P.op(PE, lambda e, lt=lt, rh=rh, j=j: e.matmul(ps4[:, 0:128], lhsT=lt, rhs=rh, start=(j == 0), stop=(j == 6)),
                                 reads=[vk, "pT"], writes=["ps4"])
                        P.op(DVE, lambda e: e.tensor_tensor(out=Y[:, q0:q0 + 128], in0=ps4[:, 0:128], in1=szT[:, q0:q0 + 128], op=ALU.mult),
                             reads=["ps4", zk], writes=[ykey])

                    scores(0)
                    chain(0)
                    yield
                    for pr in range(16):
                        if pr + 1 < 16:
                            scores(pr + 1)
                            yield
                        transposes(pr)
                        if pr + 1 < 16:
                            chain(pr + 1)
                        yield
                        pv(pr)
                        yield
                    P.op(ACT, lambda e: e.dma_start(out=ya_d[h], in_=Y[:]), reads=[ykey], writes=["ya_d%d" % h], dma=True)
                    if stage in (2, 3) and h == 0:
                        dt_ = sbuf(ph, "dbgt", [128, 2048], F32)
                        P.op(POOL, lambda e: e.tensor_copy(out=dt_[:], in_=Y[:]), reads=[ykey], writes=["dd"])
                        P.op(SP, lambda e: e.dma_start(out=dbg[:, 0:2048], in_=dt_[:]), reads=["dd"], writes=["dbg"], dma=True)

                load_w(0)
                for it in range(nheads + 1):
                    if it + 1 < nheads:
                        load_w(it + 1)
                    gens = []
                    if it < nheads:
                        gens.append(proj_gen(it))
                    if it >= 1:
                        gens.append(attn_gen(it - 1))
                    while gens:
                        for g_ in list(gens):
                            try:
                                next(g_)
                            except StopIteration:
                                gens.remove(g_)
                finish(P)

        if stage >= 3 and 3 in phases:
            with contextlib.ExitStack() as ph:
                P = Prog(ss)
                Wh = sbuf(ph, "Wh", [128, 32, 640], BF16)
                hb = [sbuf(ph, "hb%d" % i, [128, 2 * D], BF16) for i in range(2)]
                sgT = sbuf(ph, "sgT", [128, TLOC], BF16)
                qd = [sbuf(ph, "qd%d" % i, [128, TLOC], BF16) for i in range(2)]
                kd = [sbuf(ph, "kd%d" % i, [128, NT * 128], BF16) for i in range(2)]
                kdl = [sbuf(ph, "kdl%d" % i, [128, NT * 128], BF16) for i in range(2)]
                el = [sbuf(ph, "el%d" % i, [128, 44], F32) for i in range(2)]
                eref = [sbuf(ph, "eref%d" % i, [128, 44], F32) for i in range(2)]
                t_sq2 = [sbuf(ph, "t_sq%d" % i, [128, 256], F32) for i in range(2)]
                t_sig2 = [[sbuf(ph, "t_sg%d%d" % (d_, i), [128, 256], F32) for i in range(2)] for d_ in range(2)]
                elr = [sbuf(ph, "elr%d" % i, [128, 44], F32) for i in range(2)]
                vh = sbuf(ph, "vh", [64, 44, 128], BF16)
                Sbf = [sbuf(ph, "Sbf%d" % i, [128, 33, 128], BF16) for i in range(2)]
                Sst = {n: sbuf(ph, "S" + n, [128, 128], F32) for n in ("f", "b", "cf", "tf", "cb", "bb", "tmp0", "tmp1")}
                tmp = {n: [sbuf(ph, "t_%s%d" % (n, i), [128, 256], F32) for i in range(2)] for n in ("lf", "kk", "cum", "A", "E1", "E2")}
                kdTg = [sbuf(ph, "kdTg%d" % i, [64, 512], BF16) for i in range(2)]
                aTs = [sbuf(ph, "aT%d" % i, [64, 64], BF16) for i in range(4)]
                sqs = sbuf(ph, "sqs", [128, 512], BF16)
                t_r = sbuf(ph, "t_r", [128, 512], F32)
                t_o = sbuf(ph, "t_o", [128, 512], F32)
                ybT = [sbuf(ph, "ybT%d" % i, [128, TLOC], BF16) for i in range(2)]
                psr = Rot([(ps0, "ps0"), (ps1, "ps1"), (ps7, "ps7"), (ps4, "ps4"), (ps5, "ps5")])
                nheads = NH if stage > 3 else 1
                for h in range(nheads):
                    Y, ykey = ybT[h % 2], "ybT%d" % (h % 2)
                    P.op(POOL, lambda e, h=h: e.dma_start(out=Wh[:].rearrange("p c n -> p (c n)"), in_=w_hg[h]), writes=["Wh"], dma=True)

                    def chain_evac(pt, pk, d, bi):
                        P.op(ACT, lambda e: e.activation(out=t_sig2[d][bi % 2][:], in_=pt[:, 0:256], func=AF.Sigmoid), reads=[pk], writes=["t_sg%d%d" % (d, bi % 2)])

                    def chain(d, bi, local):
                        T = {n: tmp[n][d] for n in tmp}
                        K = {n: "t_%s%d" % (n, d) for n in tmp}
                        T["sig"] = t_sig2[d][bi % 2]
                        K["sig"] = "t_sg%d%d" % (d, bi % 2)
                        t_sq, tsqk = t_sq2[bi % 2], "t_sq%d" % (bi % 2)
                        hd = h * 2 + d
                        P.op(ACT, lambda e: e.activation(out=T["lf"][:], in_=T["sig"][:], func=AF.Ln, scale=oml[:, hd:hd + 1], bias=lbs[:, hd:hd + 1]),
                             reads=[K["sig"], "oml", "lbs"], writes=[K["lf"]])
                        P.op(DVE, lambda e: e.tensor_scalar(out=T["kk"][:], in0=T["sig"][:], scalar1=noml[:, hd:hd + 1], scalar2=oml[:, hd:hd + 1],
                                                            op0=ALU.mult, op1=ALU.add), reads=[K["sig"], "oml", "noml"], writes=[K["kk"]])
                        P.op(DVE, lambda e: e.tensor_tensor_scan(out=T["cum"][:], data0=rmask[:], data1=T["lf"][:], initial=0.0, op0=ALU.mult, op1=ALU.add),
                             reads=[K["lf"], "rmask"], writes=[K["cum"]])
                        c3 = T["cum"][:].rearrange("p (n c) -> p n c", c=64)
                        l3 = T["lf"][:].rearrange("p (n c) -> p n c", c=64)
                        a3 = T["A"][:].rearrange("p (n c) -> p n c", c=64)
                        if d == 0:
                            G, Gk = T["cum"], K["cum"]
                            g3 = c3
                            refpos, lastpos = 31, 63
                        else:
                            P.op(DVE, lambda e: e.tensor_tensor(out=T["lf"][:], in0=T["cum"][:], in1=T["lf"][:], op=ALU.subtract),
                                 reads=[K["cum"]], writes=[K["lf"]])
                            P.op(DVE, lambda e: e.tensor_tensor(out=l3, in0=c3[:, :, 63:64].to_broadcast([128, 4, 64]), in1=l3, op=ALU.subtract),
                                 reads=[K["cum"]], writes=[K["lf"]])
                            G, Gk = T["lf"], K["lf"]
                            g3 = l3
                            refpos, lastpos = 32, 0
                        P.op(DVE, lambda e: e.tensor_tensor(out=a3, in0=g3, in1=g3[:, :, refpos:refpos + 1].to_broadcast([128, 4, 64]), op=ALU.subtract),
                             reads=[Gk], writes=[K["A"]])
                        P.op(ACT, lambda e: e.activation(out=T["E2"][:], in_=T["A"][:], func=AF.Exp, scale=-1.0), reads=[K["A"]], writes=[K["E2"]])
                        P.op(ACT, lambda e: e.activation(out=T["E1"][:], in_=T["A"][:], func=AF.Exp), reads=[K["A"]], writes=[K["E1"]])
                        e13 = T["E1"][:].rearrange("p (n c) -> p n c", c=64)
                        P.op(DVE, lambda e: e.tensor_tensor(out=T["kk"][:], in0=T["kk"][:], in1=T["E2"][:], op=ALU.mult),
                             reads=[K["E2"]], writes=[K["kk"]])
                        if local:
                            P.op(POOL, lambda e: e.tensor_copy(out=kd[d][:, bi * 256:(bi + 1) * 256], in_=T["kk"][:]), reads=[K["kk"]], writes=["kd%d" % d])
                        P.op(DVE, lambda e: e.tensor_tensor(out=kdl[d][:, bi * 256:(bi + 1) * 256].rearrange("p (n c) -> p n c", c=64),
                                                            in0=T["kk"][:].rearrange("p (n c) -> p n c", c=64),
                                                            in1=e13[:, :, lastpos:lastpos + 1].to_broadcast([128, 4, 64]), op=ALU.mult),
                             reads=[K["kk"], K["E1"]], writes=["kdl%d" % d])
                        P.op(ACT, lambda e: e.activation(out=el[d][:, bi * 4:bi * 4 + 4], in_=g3[:, :, lastpos], func=AF.Exp), reads=[Gk], writes=["el%d" % d])
                        if local:
                            P.op(ACT, lambda e: e.activation(out=eref[d][:, bi * 4:bi * 4 + 4], in_=g3[:, :, refpos], func=AF.Exp), reads=[Gk], writes=["eref%d" % d])
                            lo = (bi - 1) * 256
                            P.op(DVE, lambda e: e.tensor_tensor(out=qd[d][:, lo:lo + 256], in0=t_sq[:], in1=T["E1"][:], op=ALU.mult),
                                 reads=[tsqk, K["E1"]], writes=["qd%d" % d])

                    for bi in range(NBLK):
                        Hb, hkey = hb[bi % 2], "hb%d" % (bi % 2)
                        load_hT_block(bi, Hb, hkey)
                        local = 1 <= bi <= 8
                        lo = (bi - 1) * 256
                        if local:
                            pt, pk = psr.next()
                            proj_fm(pt, pk, Wh, "Wh", 0, Hb, hkey)
                            P.op(ACT, lambda e, pt=pt, bi=bi: e.activation(out=t_sq2[bi % 2][:], in_=pt[:, 0:256], func=AF.Silu), reads=[pk], writes=["t_sq%d" % (bi % 2)])
                            pt, pk = psr.next()
                            proj_fm(pt, pk, Wh, "Wh", 512, Hb, hkey)
                            P.op(ACT, lambda e, pt=pt, lo=lo: e.activation(out=sgT[:, lo:lo + 256], in_=pt[:, 0:256], func=AF.Silu), reads=[pk], writes=["sgT"])
                        if bi != 9:
                            pt, pk = psr.next()
                            proj_fm(pt, pk, Wh, "Wh", 128, Hb, hkey)
                            chain_evac(pt, pk, 0, bi)
                        if bi != 0:
                            pt, pk = psr.next()
                            proj_fm(pt, pk, Wh, "Wh", 256, Hb, hkey)
                            chain_evac(pt, pk, 1, bi)
                        hv = Hb[:].rearrange("p (t d) -> p t d", t=2)
                        for ch in range(4):
                            pt, pk = psr.next()
                            tl, off = ch // 2, (ch % 2) * 64
                            for c in range(32):
                                P.op(PE, lambda e, pt=pt, hv=hv, tl=tl, off=off, c=c: e.matmul(
                                    pt[0:64, 0:128], lhsT=hv[:, tl, c * 128 + off:c * 128 + off + 64], rhs=Wh[:, c, 384:512],
                                    start=(c == 0), stop=(c == 31)), reads=["Wh", hkey], writes=[pk])
                            P.op(DVE, lambda e, pt=pt, bi=bi, ch=ch: e.tensor_copy(out=vh[:, bi * 4 + ch, :], in_=pt[0:64, 0:128]), reads=[pk], writes=["vh"])
                        if bi != 9:
                            chain(0, bi, local)
                        if bi != 0:
                            chain(1, bi, local)

                    tasks = []
                    tasks.append((0, [40, 41, 42, 43], [(Sst["cf"], "Scf", Sst["cf"], "Scf", j == 0, None) for j in range(4)]))
                    tasks.append((1, [43, 42, 41, 40], [(Sst["cb"], "Scb", Sst["cb"], "Scb", j == 0, None) for j in range(4)]))
                    tasks.append((0, [0, 1, 2, 3], [(Sst["tf"], "Stf", Sst["tf"], "Stf", j == 0, None) for j in range(4)]))
                    tasks.append((1, [39, 38, 37, 36], [(Sst["bb"], "Sbb", Sst["bb"], "Sbb", j == 0, None) for j in range(4)]))
                    tasks.append("blend")
                    fpp = [(Sst["f"], "Sf"), (Sst["tmp0"], "Stmp0")]
                    bpp = [(Sst["b"], "Sb"), (Sst["tmp1"], "Stmp1")]
                    for q0_ in range(0, 31, 4):
                        ns = list(range(q0_, min(q0_ + 4, 31)))
                        tasks.append((0, [4 + n for n in ns], [(fpp[n % 2][0], fpp[n % 2][1], fpp[(n + 1) % 2][0], fpp[(n + 1) % 2][1], False, n + 1) for n in ns]))
                        tasks.append((1, [35 - n for n in ns], [(bpp[n % 2][0], bpp[n % 2][1], bpp[(n + 1) % 2][0], bpp[(n + 1) % 2][1], False, 30 - n) for n in ns]))
                    tcount = [0]
                    mcount = [0]

                    def prep(task, k):
                        d, chunks, _ = task
                        bank, bkey = (ps6[:, 0:512], "ps6") if k % 2 == 0 else (ps7b, "ps7")
                        for j, g in enumerate(chunks):
                            P.op(PE, lambda e, g=g, j=j: e.transpose(out=bank[0:64, j * 128:(j + 1) * 128], in_=kdl[d][:, g * 64:(g + 1) * 64],
                                                                 identity=identb[:]), reads=["kdl%d" % d, "identb"], writes=[bkey])
                        w = len(chunks) * 128
                        P.op(ACT, lambda e: e.copy(out=kdTg[k % 2][:, 0:w], in_=bank[0:64, 0:w]), reads=[bkey], writes=["kdTg%d" % (k % 2)])

                    def consume(task, k):
                        d, chunks, steps = task
                        for j, g in enumerate(chunks):
                            S_in, k_in, S_out, k_out, first, store = steps[j]
                            i = mcount[0] % 2
                            mcount[0] += 1
                            pu, pukey = (ps4, "ps4") if i == 0 else (ps5, "ps5")
                            P.op(PE, lambda e, pu=pu, j=j, g=g: e.matmul(pu[:, 0:128], lhsT=kdTg[k % 2][:, j * 128:(j + 1) * 128], rhs=vh[:, g, :], start=True, stop=True),
                                 reads=["kdTg%d" % (k % 2), "vh"], writes=[pukey])
                            if first:
                                P.op(DVE, lambda e, pu=pu, S_out=S_out: e.tensor_copy(out=S_out[:], in_=pu[:, 0:128]), reads=[pukey], writes=[k_out])
                            else:
                                P.op(DVE, lambda e, pu=pu, S_in=S_in, S_out=S_out, g=g: e.scalar_tensor_tensor(
                                    out=S_out[:], in0=S_in[:], scalar=el[d][:, g:g + 1], in1=pu[:, 0:128], op0=ALU.mult, op1=ALU.add),
                                    reads=[k_in, "el%d" % d, pukey], writes=[k_out])
                            if store is not None:
                                gc = 4 + store
                                if d == 0:
                                    P.op(ACT, lambda e, S_out=S_out, store=store, gc=gc: e.activation(out=Sbf[d][:, store, :], in_=S_out[:], func=AF.Copy,
                                                                                                     scale=eref[d][:, gc:gc + 1]),
                                         reads=[k_out, "eref%d" % d], writes=["Sbf%d" % d])
                                else:
                                    P.op(DVE, lambda e, S_out=S_out, store=store, gc=gc: e.tensor_scalar(out=Sbf[d][:, store, :], in0=S_out[:],
                                                                                                        scalar1=eref[d][:, gc:gc + 1], scalar2=None, op0=ALU.mult),
                                         reads=[k_out, "eref%d" % d], writes=["Sbf%d" % d])

                    def blend():
                        P.op(DVE, lambda e: e.tensor_scalar(out=Sst["f"][:], in0=Sst["cf"][:], scalar1=cm[:, 0:1], scalar2=None, op0=ALU.mult),
                             reads=["Scf", "cm"], writes=["Sf"])
                        P.op(DVE, lambda e: e.scalar_tensor_tensor(out=Sst["f"][:], in0=Sst["tf"][:], scalar=cm[:, 1:2], in1=Sst["f"][:], op0=ALU.mult, op1=ALU.add),
                             reads=["Stf", "cm"], writes=["Sf"])
                        P.op(DVE, lambda e: e.tensor_scalar(out=Sst["b"][:], in0=Sst["cb"][:], scalar1=cm[:, 2:3], scalar2=None, op0=ALU.mult),
                             reads=["Scb", "cm"], writes=["Sb"])
                        P.op(DVE, lambda e: e.scalar_tensor_tensor(out=Sst["b"][:], in0=Sst["bb"][:], scalar=cm[:, 3:4], in1=Sst["b"][:], op0=ALU.mult, op1=ALU.add),
                             reads=["Sbb", "cm"], writes=["Sb"])
                        P.op(ACT, lambda e: e.activation(out=Sbf[0][:, 0, :], in_=Sst["f"][:], func=AF.Copy, scale=eref[0][:, 4:5]),
                             reads=["Sf", "eref0"], writes=["Sbf0"])
                        P.op(ACT, lambda e: e.activation(out=Sbf[1][:, 31, :], in_=Sst["b"][:], func=AF.Copy, scale=eref[1][:, 35:36]),
                             reads=["Sb", "eref1"], writes=["Sbf1"])

                    real = [t for t in tasks if t != "blend"]
                    idx_of = {id(t): i for i, t in enumerate(real)}
                    prep(real[0], 0)
                    for t in tasks:
                        if t == "blend":
                            blend()
                            continue
                        k = idx_of[id(t)]
                        if k + 1 < len(real):
                            prep(real[k + 1], k + 1)
                        consume(t, k)
                    def aT_part(n):
                        g = 4 + n
                        lo = n * 64
                        for d in range(2):
                            i = (n * 2 + d) % 4
                            pa_, pak = ((ps4, "ps4"), (ps5, "ps5"), (ps0, "ps0"), (ps1, "ps1"))[i]
                            P.op(PE, lambda e, pa_=pa_, d=d: e.matmul(pa_[0:64, 0:64], lhsT=kd[d][:, g * 64:(g + 1) * 64], rhs=qd[d][:, lo:lo + 64],
                                                                   start=True, stop=True), reads=["kd%d" % d, "qd%d" % d], writes=[pak])
                            mk = maskF if d == 0 else maskB
                            P.op(DVE, lambda e, pa_=pa_, i=i, mk=mk: e.tensor_tensor(out=aTs[i][:], in0=pa_[0:64, 0:64], in1=mk[:], op=ALU.mult),
                                 reads=[pak, "maskF", "maskB"], writes=["aT%d" % i])

                    def o_part(n, half, hk):
                        g = 4 + n
                        lo = n * 64
                        cn = n % 8
                        oc = half[:, cn * 64:(cn + 1) * 64]
                        a0, a1 = aTs[(n * 2) % 4], aTs[(n * 2 + 1) % 4]
                        k0, k1 = "aT%d" % ((n * 2) % 4), "aT%d" % ((n * 2 + 1) % 4)
                        P.op(PE, lambda e: e.matmul(oc, lhsT=vh[:, g, :], rhs=a0[:], start=True, stop=False), reads=["vh", k0], writes=[hk])
                        P.op(PE, lambda e: e.matmul(oc, lhsT=Sbf[0][:, n, :], rhs=qd[0][:, lo:lo + 64], start=False, stop=False), reads=["Sbf0", "qd0"], writes=[hk])
                        P.op(PE, lambda e: e.matmul(oc, lhsT=vh[:, g, :], rhs=a1[:], start=False, stop=False), reads=["vh", k1], writes=[hk])
                        P.op(PE, lambda e: e.matmul(oc, lhsT=Sbf[1][:, n, :], rhs=qd[1][:, lo:lo + 64], start=False, stop=True), reads=["Sbf1", "qd1"], writes=[hk])

                    aT_part(0)
                    for grp in range(4):
                        half = ps23[:, 0:512] if grp % 2 == 0 else ps23[:, 512:1024]
                        hk = "ps23a" if grp % 2 == 0 else "ps23b"
                        for cn in range(8):
                            n = grp * 8 + cn
                            if n + 1 < 32:
                                aT_part(n + 1)
                            o_part(n, half, hk)
                        g0 = grp * 512
                        P.op(ACT, lambda e, half=half: e.activation(out=sqs[:], in_=half, func=AF.Square), reads=[hk], writes=["sqs"])
                        P.op(PE, lambda e: e.matmul(ps7[:, 0:512], lhsT=onesb[:], rhs=sqs[:], start=True, stop=True), reads=["sqs", "onesb"], writes=["ps7"])
                        P.op(ACT, lambda e: e.activation(out=t_r[:], in_=ps7[:, 0:512], func=AF.Sqrt, scale=1.0 / 128.0, bias=1e-6), reads=["ps7"], writes=["t_r"])
                        P.op(DVE, lambda e: e.reciprocal(out=t_r[:], in_=t_r[:]), writes=["t_r"])
                        P.op(DVE, lambda e, half=half: e.tensor_tensor(out=t_o[:], in0=half, in1=t_r[:], op=ALU.mult), reads=[hk, "t_r"], writes=["t_o"])
                        P.op(DVE, lambda e, Y=Y, g0=g0: e.scalar_tensor_tensor(out=Y[:, g0:g0 + 512], in0=t_o[:], scalar=nw[:, 0:1], in1=sgT[:, g0:g0 + 512],
                                                                              op0=ALU.mult, op1=ALU.mult), reads=["t_o", "nw", "sgT"], writes=[ykey])
                    P.op(ACT, lambda e, Y=Y, h=h: e.dma_start(out=yb_d[h], in_=Y[:]), reads=[ykey], writes=["yb_d%d" % h], dma=True)
                    if stage == 3:
                        dt_ = sbuf(ph, "dbgt", [128, 2048], F32)
                        P.op(POOL, lambda e, Y=Y, dt_=dt_: e.tensor_copy(out=dt_[:], in_=Y[:]), reads=[ykey], writes=["dd"])
                        P.op(SP, lambda e, dt_=dt_: e.dma_start(out=dbg[:, 2048:4096], in_=dt_[:]), reads=["dd"], writes=["dbg"], dma=True)
                        P.op(SP, lambda e: e.dma_start(out=dbg[:, 4096:4224], in_=Sst["cf"][:]), reads=["Scf"], writes=["dbg"], dma=True)
                        P.op(SP, lambda e: e.dma_start(out=dbg[:, 4224:4352], in_=Sst["cb"][:]), reads=["Scb"], writes=["dbg"], dma=True)
                finish(P)

        if stage >= 4:
            ps6f = ps6[:].bitcast(F32)
            banksets = [((ps0[:], "ps0"), (ps1[:], "ps1"), (ps23[:, 0:512], "ps23a"), (ps23[:, 512:1024], "ps23b")),
                        ((ps4[:], "ps4"), (ps5[:], "ps5"), (ps7[:], "ps7"), (ps6f, "ps6"))]
            for pp in range(2):
                with contextlib.ExitStack() as ph:
                    P = Prog(ss)
                    h4s = [sbuf(ph, "h4%d" % i, [128, 4, D], BF16) for i in range(2)]
                    ya4s = [sbuf(ph, "ya4%d" % i, [128, NH, 512], BF16) for i in range(2)]
                    yb4s = [sbuf(ph, "yb4%d" % i, [128, NH, 512], BF16) for i in range(2)]
                    W4 = [sbuf(ph, "W4%d" % i, [128, 96, 128], BF16) for i in range(2)]
                    sgs = [[sbuf(ph, "sg%d%d" % (t2, i), [128, 512], F32) for i in range(2)] for t2 in range(2)]
                    tts = [[sbuf(ph, "tt%d%d" % (t2, i), [128, 512], F32) for i in range(2)] for t2 in range(2)]
                    mss = [sbuf(ph, "ms%d" % i, [128, 512], BF16) for i in range(2)]
                    for t2 in range(2):
                        tb = 2 * pp + t2
                        P.op(SP, lambda e, tb=tb, t2=t2: e.dma_start(out=h4s[t2][:], in_=hT_d[2 + 4 * tb:6 + 4 * tb].rearrange("t p d -> p t d")),
                             writes=["h4%d" % t2], dma=True)
                        P.op(SP, lambda e, tb=tb, t2=t2: e.dma_start(out=ya4s[t2][:], in_=ya_d[:, :, tb * 512:(tb + 1) * 512].rearrange("h p t -> p h t")),
                             writes=["ya4%d" % t2], dma=True)
                        P.op(SP, lambda e, tb=tb, t2=t2: e.dma_start(out=yb4s[t2][:], in_=yb_d[:, :, tb * 512:(tb + 1) * 512].rearrange("h p t -> p h t")),
                             writes=["yb4%d" % t2], dma=True)
                    for jb in range(32):
                        W, wkey = W4[jb % 2], "W4%d" % (jb % 2)
                        P.op(POOL, lambda e, W=W, jb=jb: e.dma_start(out=W[:].rearrange("p c n -> p (c n)"), in_=w_4a[jb]), writes=[wkey], dma=True)
                        for t2 in range(2):
                            tb = 2 * pp + t2
                            (pga, kga), (pgb, kgb), (ppa, kpa), (ppb, kpb) = banksets[t2]
                            h4, ya4, yb4 = h4s[t2], ya4s[t2], yb4s[t2]
                            sg, tt, ms = sgs[t2], tts[t2], mss[t2]
                            for c in range(32):
                                P.op(PE, lambda e, W=W, c=c, pga=pga, h4=h4: e.matmul(pga, lhsT=W[:, c, :], rhs=h4[:, :, c * 128:(c + 1) * 128], start=(c == 0), stop=(c == 31)),
                                     reads=[wkey, "h4%d" % t2], writes=[kga])
                            for c in range(32):
                                P.op(PE, lambda e, W=W, c=c, pgb=pgb, h4=h4: e.matmul(pgb, lhsT=W[:, 32 + c, :], rhs=h4[:, :, c * 128:(c + 1) * 128], start=(c == 0), stop=(c == 31)),
                                     reads=[wkey, "h4%d" % t2], writes=[kgb])
                            for hh in range(NH):
                                P.op(PE, lambda e, W=W, hh=hh, ppa=ppa, ya4=ya4: e.matmul(ppa, lhsT=W[:, 64 + hh, :], rhs=ya4[:, hh, :], start=(hh == 0), stop=(hh == NH - 1)),
                                     reads=[wkey, "ya4%d" % t2], writes=[kpa])
                            for hh in range(NH):
                                P.op(PE, lambda e, W=W, hh=hh, ppb=ppb, yb4=yb4: e.matmul(ppb, lhsT=W[:, 80 + hh, :], rhs=yb4[:, hh, :], start=(hh == 0), stop=(hh == NH - 1)),
                                     reads=[wkey, "yb4%d" % t2], writes=[kpb])
                            P.op(ACT, lambda e, sg=sg, pga=pga: e.activation(out=sg[0][:], in_=pga, func=AF.Sigmoid), reads=[kga], writes=["sg%d0" % t2])
                            P.op(ACT, lambda e, sg=sg, pgb=pgb: e.activation(out=sg[1][:], in_=pgb, func=AF.Sigmoid), reads=[kgb], writes=["sg%d1" % t2])
                            P.op(DVE, lambda e, tt=tt, sg=sg, ppa=ppa: e.tensor_tensor(out=tt[0][:], in0=ppa, in1=sg[0][:], op=ALU.mult), reads=[kpa, "sg%d0" % t2], writes=["tt%d0" % t2])
                            P.op(DVE, lambda e, tt=tt, sg=sg, ppb=ppb: e.tensor_tensor(out=tt[1][:], in0=ppb, in1=sg[1][:], op=ALU.mult), reads=[kpb, "sg%d1" % t2], writes=["tt%d1" % t2])
                            P.op(DVE, lambda e, tt=tt, ms=ms: e.tensor_tensor(out=ms[:], in0=tt[0][:], in1=tt[1][:], op=ALU.add), reads=["tt%d0" % t2, "tt%d1" % t2], writes=["ms%d" % t2])
                            P.op(SP, lambda e, ms=ms, tb=tb, jb=jb: e.dma_start(out=mT_d[tb][:, jb * 512:(jb + 1) * 512], in_=ms[:]), reads=["ms%d" % t2], writes=["mT_d%d" % tb], dma=True)
                    finish(P)
            with contextlib.ExitStack() as ph4:
                gbc = sbuf(ph4, "gbc", [128, D], F32)
                lgbc = sbuf(ph4, "lgbc", [128, D], F32)
                lbbc = sbuf(ph4, "lbbc", [128, D], F32)
                for tb in range(4):
                    with contextlib.ExitStack() as ph:
                        P = Prog(ss)
                        zt = sbuf(ph, "zt", [128, 4, D], F32)
                        mT = sbuf(ph, "mT", [128, 32, 512], BF16)
                        P.op(SP, lambda e, tb=tb: e.dma_start(out=mT[:].rearrange("p j t -> p (j t)"), in_=mT_d[tb]), writes=["mT"], dma=True)
                        Wo = [sbuf(ph, "Wo%d" % i, [128, 32, 256], BF16) for i in range(3)]
                        t2 = [sbuf(ph, "t2%d" % i, [128, 256], F32) for i in range(2)]
                        stt = sbuf(ph, "stt4", [128, 4, 16, 6], F32)
                        mv = sbuf(ph, "mv4", [128, 4], F32)
                        if tb == 0:
                            P.op(SP, lambda e: e.dma_start(out=gbc[:], in_=gate_d.rearrange("j p -> (j p)").partition_broadcast(128)), writes=["gbc"], dma=True)
                            P.op(SP, lambda e: e.dma_start(out=lgbc[:], in_=lng.rearrange("o d -> (o d)").partition_broadcast(128)), writes=["lgbc"], dma=True)
                            P.op(SP, lambda e: e.dma_start(out=lbbc[:], in_=lnb.rearrange("o d -> (o d)").partition_broadcast(128)), writes=["lbbc"], dma=True)
                        for tl in range(4):
                            r0 = (2 + 4 * tb + tl) * 128
                            P.op(SP, lambda e, tl=tl, r0=r0: e.dma_start(out=zt[:, tl, :], in_=xe[r0:r0 + 128, :]), writes=["zt%d" % tl], dma=True)
                        pso = Rot([(ps4, "ps4"), (ps5, "ps5"), (ps7, "ps7")])
                        for nb in range(16):
                            W, wkey = Wo[nb % 3], "Wo%d" % (nb % 3)
                            P.op(POOL, lambda e, W=W, nb=nb: e.dma_start(out=W[:].rearrange("p c n -> p (c n)"), in_=w_4b[nb]), writes=[wkey], dma=True)
                            for tl in range(4):
                                pt, pk = pso.next()
                                for jb in range(32):
                                    P.op(PE, lambda e, pt=pt, W=W, jb=jb, tl=tl: e.matmul(pt[:, 0:256], lhsT=mT[:, jb, tl * 128:(tl + 1) * 128], rhs=W[:, jb, :],
                                                                                         start=(jb == 0), stop=(jb == 31)), reads=[wkey, "mT"], writes=[pk])
                                T2, tk = t2[(nb * 4 + tl) % 2], "t2%d" % ((nb * 4 + tl) % 2)
                                c0 = nb * 256
                                P.op(DVE, lambda e, pt=pt, T2=T2, c0=c0: e.tensor_tensor(out=T2[:], in0=pt[:, 0:256], in1=gbc[:, c0:c0 + 256], op=ALU.mult),
                                     reads=[pk, "gbc"], writes=[tk])
                                P.op(DVE, lambda e, T2=T2, tl=tl, c0=c0: e.scalar_tensor_tensor(out=zt[:, tl, c0:c0 + 256], in0=zt[:, tl, c0:c0 + 256], scalar=ALPHA,
                                                                                                in1=T2[:], op0=ALU.mult, op1=ALU.add), reads=[tk], writes=["zt%d" % tl])
                                P.op(DVE, lambda e, tl=tl, nb=nb, c0=c0: e.bn_stats(out=stt[:, tl, nb, :], in_=zt[:, tl, c0:c0 + 256]),
                                     reads=["zt%d" % tl], writes=["stt4_%d" % tl])
                        for tl in range(4):
                            zk = "zt%d" % tl
                            P.op(DVE, lambda e, tl=tl: e.bn_aggr(out=mv[:, 0:2], in_=stt[:, tl, :, :]), reads=["stt4_%d" % tl], writes=["mv4"])
                            P.op(ACT, lambda e: e.activation(out=mv[:, 1:2], in_=mv[:, 1:2], func=AF.Sqrt, bias=1e-6, scale=1.0), writes=["mv4"])
                            P.op(DVE, lambda e: e.reciprocal(out=mv[:, 1:2], in_=mv[:, 1:2]), writes=["mv4"])
                            P.op(DVE, lambda e: e.tensor_scalar(out=mv[:, 2:3], in0=mv[:, 0:1], scalar1=mv[:, 1:2], scalar2=-1.0, op0=ALU.mult, op1=ALU.mult),
                                 writes=["mv4"])
                            P.op(ACT, lambda e, tl=tl: e.activation(out=zt[:, tl, :], in_=zt[:, tl, :], func=AF.Identity, scale=mv[:, 1:2], bias=mv[:, 2:3]),
                                 reads=["mv4"], writes=[zk])
                            P.op(DVE, lambda e, tl=tl: e.tensor_tensor(out=zt[:, tl, :], in0=zt[:, tl, :], in1=lgbc[:], op=ALU.mult), reads=["lgbc"], writes=[zk])
                            P.op(DVE, lambda e, tl=tl: e.tensor_tensor(out=zt[:, tl, :], in0=zt[:, tl, :], in1=lbbc[:], op=ALU.add), reads=["lbbc"], writes=[zk])
                            r0 = (4 * tb + tl) * 128
                            P.op(SP, lambda e, tl=tl, r0=r0: e.dma_start(out=out[r0:r0 + 128, :], in_=zt[:, tl, :]), reads=[zk], writes=["out"], dma=True)
                        finish(P)
    return nc, nops[0]


def _bias_tables(rpb):
    qc = np.arange(64)
    cs = np.clip(qc - 8, 0, 48)
    kc = np.arange(64)
    col_in = (kc[None, :] >= cs[:, None]) & (kc[None, :] < cs[:, None] + 16)
    dcol = np.clip(kc[None, :] - qc[:, None], -15, 15) + 15
    tabs = np.full((4, NH, 128, 5, 576), NEG, np.float32)
    for seg in range(4):
        for vi, pr in enumerate((0, 1, 2, 14, 15)):
            for a in range(2):
                lr = 2 * pr + a
                r = seg * 32 + lr
                rs = min(max(r - 4, 0), 120)
                for j in range(9):
                    if not (a <= j <= a + 7):
                        continue
                    kap = 2 * pr + j
                    g = seg * 32 - 4 + kap
                    if seg == 0 and kap < 4:
                        g = 4 + kap
                    if seg == 3 and kap >= 36:
                        g = 120 + (kap - 36)
                    assert rs <= g < rs + 8, (seg, pr, a, j, g, rs)
                    dr = g - r + 7
                    blk = rpb[:, dr][:, dcol]
                    blk = np.where(col_in[None], blk, np.float32(NEG))
                    tabs[seg, :, a * 64:(a + 1) * 64, vi, j * 64:(j + 1) * 64] = blk
    return tabs.reshape(4, NH, 128, 5 * 576)


_CACHE = {}


def _prep_shared(w_ada, b_ada, w_in, hg_lb_fwd, hg_lb_bwd, hg_norm_w, w_pa, w_pb, w_out, ln_g, ln_b):
    w_in0 = w_in[0]
    wv = w_in0.reshape(32, 128, 26624)

    def grp(g):
        s = [0, 2048, 4096, 6144, 8192, 10240, 12288, 14336, 16384, 18432, 22528][g]
        return s

    w_na = np.empty((NH, 128, 32, 512), np.float32)
    w_hg = np.empty((NH, 128, 32, 640), np.float32)
    for h in range(NH):
        for gi in range(4):
            s = grp(gi) + h * 128
            w_na[h, :, :, gi * 128:(gi + 1) * 128] = wv[:, :, s:s + 128].transpose(1, 0, 2)
        for gi in range(5):
            s = grp(4 + gi) + h * 128
            w_hg[h, :, :, gi * 128:(gi + 1) * 128] = wv[:, :, s:s + 128].transpose(1, 0, 2)
    w_4a = np.empty((32, 128, 96, 128), np.float32)
    pav = w_pa[0].reshape(16, 128, D)
    pbv = w_pb[0].reshape(16, 128, D)
    for jb in range(32):
        w_4a[jb, :, 0:32, :] = wv[:, :, 18432 + jb * 128:18432 + (jb + 1) * 128].transpose(1, 0, 2)
        w_4a[jb, :, 32:64, :] = wv[:, :, 22528 + jb * 128:22528 + (jb + 1) * 128].transpose(1, 0, 2)
        w_4a[jb, :, 64:80, :] = pav[:, :, jb * 128:(jb + 1) * 128].transpose(1, 0, 2)
        w_4a[jb, :, 80:96, :] = pbv[:, :, jb * 128:(jb + 1) * 128].transpose(1, 0, 2)
    wov = w_out[0].reshape(32, 128, D)
    w_4b = np.empty((16, 128, 32, 256), np.float32)
    for nb in range(16):
        w_4b[nb] = wov[:, :, nb * 256:(nb + 1) * 256].transpose(1, 0, 2)
    lbl = np.empty((128, NH, 2, 2), np.float32)
    lbl[:, :, 0, :] = hg_lb_fwd.reshape(2, NH, 128).transpose(2, 1, 0)
    lbl[:, :, 1, :] = hg_lb_bwd.reshape(2, NH, 128).transpose(2, 1, 0)
    return dict(
        w_ada=np.ascontiguousarray(w_ada[0]),
        b_adaT=np.ascontiguousarray(b_ada[0].reshape(96, 128).T),
        w_na=w_na.reshape(NH, 128, 32 * 512), w_hg=w_hg.reshape(NH, 128, 32 * 640),
        w_4a=w_4a.reshape(32, 128, 96 * 128), w_4b=w_4b.reshape(16, 128, 32 * 256),
        lbl=lbl.reshape(128, NH * 4), nw=np.ascontiguousarray(hg_norm_w[0].reshape(128, 1)),
        lng=np.ascontiguousarray(ln_g[0].reshape(1, D)), lnb=np.ascontiguousarray(ln_b[0].reshape(1, D)),
    )


def make_in_maps(x, c, ctx, c_ctx, w_ada, b_ada, w_in, na_rpb, hg_lb_fwd, hg_lb_bwd, hg_norm_w, w_pa, w_pb, w_out, ln_g, ln_b):
    f = lambda a: np.asarray(a, dtype=np.float32)
    x, c, ctx, c_ctx = f(x), f(c), f(ctx), f(c_ctx)
    shared = _prep_shared(f(w_ada), f(b_ada), f(w_in), f(hg_lb_fwd), f(hg_lb_bwd), f(hg_norm_w), f(w_pa), f(w_pb), f(w_out), f(ln_g), f(ln_b))
    tabs = _bias_tables(f(na_rpb)[0])
    maps = []
    for core in range(8):
        b, seg = core // 4, core % 4
        t0 = seg * TLOC
        top = x[b, t0 - 256:t0] if seg > 0 else x[b, 256:512]
        bot = x[b, t0 + TLOC:t0 + TLOC + 256] if seg < 3 else x[b, 7680:7936]
        xe = np.concatenate([top, x[b, t0:t0 + TLOC], bot, ctx[b]], axis=0)
        condT = np.concatenate([c[b].reshape(32, 128).T, c_ctx.reshape(32, 128).T], axis=1)
        m0 = 1.0 if seg == 0 else 0.0
        m3 = 1.0 if seg == 3 else 0.0
        cmask = np.tile(np.array([[m0, 1.0 - m0, m3, 1.0 - m3]], np.float32), (128, 1))
        d = dict(shared)
        d.update(xe=np.ascontiguousarray(xe), condT=np.ascontiguousarray(condT), biasT=np.ascontiguousarray(tabs[seg]), cmask=cmask)
        maps.append(d)
    return maps


def kernel(**inputs):
    if "nc" not in _CACHE:
        _CACHE["nc"] = build_program()[0]
    nc = _CACHE["nc"]
    maps = make_in_maps(**inputs)
    res = run_bass_kernel_spmd(nc, maps, core_ids=list(range(8)))
    outp = np.empty((2, 8192, D), np.float32)
    for core in range(8):
        b, seg = core // 4, core % 4
        outp[b, seg * TLOC:(seg + 1) * TLOC] = res.results[core]["out"]
    return outp
```
